# Optimizing a Trainium2 kernel written in Bass

```python
import math
import jax, jax.numpy as jnp
from jax import lax
import numpy as np

D_MODEL = 2048
BATCH = 4
SEQ = 4096
DEPTH = 1

MIX_WIDTH = D_MODEL
ATTN_WIDTH = MIX_WIDTH // 2
SSM_WIDTH = MIX_WIDTH - ATTN_WIDTH
HEAD_DIM = 128
N_HEADS = ATTN_WIDTH // HEAD_DIM
DILATED_PATTERNS = ((128, 1), (512, 4), (2048, 16))
N_BUCKETS = 32
MAX_DISTANCE = 2048
SSM_GROUP = 16
N_SSM_GROUPS = SSM_WIDTH // SSM_GROUP
SSM_STATE = 64
D_FF = 5632
EPS = 1e-6
DT_MIN = 0.001
DT_MAX = 0.1
NEG_INF = -1e30

kernel_name = 'hybrid_dilated_attn_s5_macaron'


def rms_norm(x, g):
    xf = x.astype(jnp.float32)
    y = xf * lax.rsqrt(jnp.mean(xf * xf, axis=-1, keepdims=True) + EPS)
    return (y * g.astype(jnp.float32)).astype(x.dtype)


def swiglu_ffn(x, w_gate, w_up, w_down):
    return (jax.nn.silu(x @ w_gate) * (x @ w_up)) @ w_down


def t5_causal_bucket(dist):
    max_exact = N_BUCKETS // 2
    d_f = jnp.maximum(dist, max_exact).astype(jnp.float32)
    large = max_exact + (jnp.log(d_f / max_exact) / math.log(MAX_DISTANCE / max_exact)
                         * (N_BUCKETS - max_exact)).astype(jnp.int32)
    large = jnp.minimum(large, N_BUCKETS - 1)
    return jnp.where(dist < max_exact, dist, large)


def dilated_window_attention(q, k, v, rel_bias, window, dilation):
    bsz, seq, nh, hd = q.shape
    span = window // dilation
    block = span * dilation
    n_blk = -(-seq // block)
    length = n_blk * block
    pad = length - seq

    def to_blocks(t):
        t = jnp.pad(t, ((0, 0), (0, pad), (0, 0), (0, 0)))
        return t.reshape(bsz, n_blk, span, dilation, nh, hd)

    def with_prev(t):
        prev = jnp.pad(t[:, :-1], ((0, 0), (1, 0), (0, 0), (0, 0), (0, 0), (0, 0)))
        return jnp.concatenate([prev, t], axis=2)

    qb = to_blocks(q)
    kc = with_prev(to_blocks(k))
    vc = with_prev(to_blocks(v))
    scores = jnp.einsum('bnidhe,bnjdhe->bndhij', qb, kc,
                        preferred_element_type=jnp.float32) / math.sqrt(hd)
    qi = jnp.arange(span)[:, None]
    kj = jnp.arange(2 * span)[None, :]
    delta = qi + span - kj
    bucket = t5_causal_bucket(jnp.maximum(delta, 0) * dilation)
    bias = jnp.transpose(rel_bias[bucket].astype(jnp.float32), (2, 0, 1))
    blk_idx = jnp.arange(n_blk)[:, None, None]
    valid = (delta >= 0) & (delta <= span) & ((blk_idx > 0) | (kj >= span))
    logits = scores + bias[None, None, None]
    logits = jnp.where(valid[None, :, None, None], logits, NEG_INF)
    m = jnp.max(logits, axis=-1)
    p = jnp.exp(logits - m[..., None])
    s = jnp.sum(p, axis=-1)
    o = jnp.einsum('bndhij,bnjdhe->bnidhe', p.astype(vc.dtype), vc,
                   preferred_element_type=jnp.float32)
    o = o.reshape(bsz, length, nh, hd)[:, :seq]
    m = jnp.transpose(m, (0, 1, 4, 2, 3)).reshape(bsz, length, nh)[:, :seq]
    s = jnp.transpose(s, (0, 1, 4, 2, 3)).reshape(bsz, length, nh)[:, :seq]
    return o, m, s


def mixture_of_dilations(q, k, v, rel_bias):
    results = [dilated_window_attention(q, k, v, rel_bias, w, d) for (w, d) in DILATED_PATTERNS]
    m_all = jnp.stack([r[1] for r in results], axis=0)
    m_max = jnp.max(m_all, axis=0)
    num = 0.0
    den = 0.0
    for (o, m, s) in results:
        w = jnp.exp(m - m_max)
        num = num + w[..., None] * o
        den = den + w * s
    return num / den[..., None]


def complex_affine_combine(e1, e2):
    ar1, ai1, br1, bi1 = e1
    ar2, ai2, br2, bi2 = e2
    ar = ar2 * ar1 - ai2 * ai1
    ai = ar2 * ai1 + ai2 * ar1
    br = ar2 * br1 - ai2 * bi1 + br2
    bi = ar2 * bi1 + ai2 * br1 + bi2
    return (ar, ai, br, bi)


def s5_ssm(u, lambda_re, lambda_im, log_dt, b_re, b_im, c_re, c_im, d_skip):
    bsz, seq, _ = u.shape
    ug = u.astype(jnp.float32).reshape(bsz, seq, N_SSM_GROUPS, SSM_GROUP)
    dt = jnp.exp(log_dt.astype(jnp.float32))[:, None]
    lr = lambda_re.astype(jnp.float32)
    li = lambda_im.astype(jnp.float32)
    mag = jnp.exp(lr * dt)
    a_re = mag * jnp.cos(li * dt)
    a_im = mag * jnp.sin(li * dt)
    zr = a_re - 1.0
    zi = a_im
    lam_sq = lr * lr + li * li
    coef_re = (zr * lr + zi * li) / lam_sq
    coef_im = (zi * lr - zr * li) / lam_sq
    br = b_re.astype(jnp.float32)
    bim = b_im.astype(jnp.float32)
    bbar_re = coef_re[..., None] * br - coef_im[..., None] * bim
    bbar_im = coef_re[..., None] * bim + coef_im[..., None] * br
    bu_re = jnp.einsum('gpc,bsgc->bsgp', bbar_re, ug)
    bu_im = jnp.einsum('gpc,bsgc->bsgp', bbar_im, ug)
    a_re_b = jnp.broadcast_to(a_re, bu_re.shape)
    a_im_b = jnp.broadcast_to(a_im, bu_im.shape)
    _, _, x_re, x_im = lax.associative_scan(complex_affine_combine,
                                            (a_re_b, a_im_b, bu_re, bu_im), axis=1)
    y = (jnp.einsum('gcp,bsgp->bsgc', c_re.astype(jnp.float32), x_re)
         - jnp.einsum('gcp,bsgp->bsgc', c_im.astype(jnp.float32), x_im)
         + d_skip.astype(jnp.float32).reshape(N_SSM_GROUPS, SSM_GROUP) * ug)
    return y.reshape(bsz, seq, SSM_WIDTH)


def setup_inputs(seed: int = 0) -> dict:
    key = jax.random.key(seed)
    ks = jax.random.split(key, 26)
    f32 = jnp.float32

    def nrm(k, shape, scale):
        return jax.random.normal(k, shape, f32) * scale

    def gain(k, shape):
        return 1.0 + 0.01 * jax.random.normal(k, shape, f32)

    L = DEPTH
    x = jax.random.normal(ks[0], (BATCH, SEQ, D_MODEL), f32)
    ffn1_norm = gain(ks[1], (L, D_MODEL))
    ffn1_w_gate = nrm(ks[2], (L, D_MODEL, D_FF), D_MODEL ** -0.5)
    ffn1_w_up = nrm(ks[3], (L, D_MODEL, D_FF), D_MODEL ** -0.5)
    ffn1_w_down = nrm(ks[4], (L, D_FF, D_MODEL), D_FF ** -0.5)
    mix_norm = gain(ks[5], (L, D_MODEL))
    w_in = nrm(ks[6], (L, D_MODEL, 3 * ATTN_WIDTH + SSM_WIDTH), D_MODEL ** -0.5)
    q_norm = gain(ks[7], (L, HEAD_DIM))
    k_norm = gain(ks[8], (L, HEAD_DIM))
    rel_bias = nrm(ks[9], (N_BUCKETS, N_HEADS), 0.1)
    n_idx = jnp.arange(SSM_STATE, dtype=f32)
    ssm_lambda_re = -0.5 + 0.01 * jax.random.normal(ks[10], (L, N_SSM_GROUPS, SSM_STATE), f32)
    ssm_lambda_im = math.pi * n_idx + 0.01 * jax.random.normal(ks[11], (L, N_SSM_GROUPS, SSM_STATE), f32)
    ssm_log_dt = jax.random.uniform(ks[12], (L, N_SSM_GROUPS), f32,
                                    minval=math.log(DT_MIN), maxval=math.log(DT_MAX))
    ssm_b_re = nrm(ks[13], (L, N_SSM_GROUPS, SSM_STATE, SSM_GROUP), (2 * SSM_GROUP) ** -0.5)
    ssm_b_im = nrm(ks[14], (L, N_SSM_GROUPS, SSM_STATE, SSM_GROUP), (2 * SSM_GROUP) ** -0.5)
    ssm_c_re = nrm(ks[15], (L, N_SSM_GROUPS, SSM_GROUP, SSM_STATE), (2 * SSM_STATE) ** -0.5)
    ssm_c_im = nrm(ks[16], (L, N_SSM_GROUPS, SSM_GROUP, SSM_STATE), (2 * SSM_STATE) ** -0.5)
    ssm_d = nrm(ks[17], (L, SSM_WIDTH), 1.0)
    glu_w = nrm(ks[18], (L, SSM_WIDTH, SSM_WIDTH), SSM_WIDTH ** -0.5)
    glu_b = nrm(ks[19], (L, SSM_WIDTH), 0.01)
    w_out = nrm(ks[20], (L, MIX_WIDTH, D_MODEL), MIX_WIDTH ** -0.5)
    ffn2_norm = gain(ks[21], (L, D_MODEL))
    ffn2_w_gate = nrm(ks[22], (L, D_MODEL, D_FF), D_MODEL ** -0.5)
    ffn2_w_up = nrm(ks[23], (L, D_MODEL, D_FF), D_MODEL ** -0.5)
    ffn2_w_down = nrm(ks[24], (L, D_FF, D_MODEL), D_FF ** -0.5)
    return {'x': x, 'ffn1_norm': ffn1_norm, 'ffn1_w_gate': ffn1_w_gate, 'ffn1_w_up': ffn1_w_up,
            'ffn1_w_down': ffn1_w_down, 'mix_norm': mix_norm, 'w_in': w_in, 'q_norm': q_norm,
            'k_norm': k_norm, 'rel_bias': rel_bias, 'ssm_lambda_re': ssm_lambda_re,
            'ssm_lambda_im': ssm_lambda_im, 'ssm_log_dt': ssm_log_dt, 'ssm_b_re': ssm_b_re,
            'ssm_b_im': ssm_b_im, 'ssm_c_re': ssm_c_re, 'ssm_c_im': ssm_c_im, 'ssm_d': ssm_d,
            'glu_w': glu_w, 'glu_b': glu_b, 'w_out': w_out, 'ffn2_norm': ffn2_norm,
            'ffn2_w_gate': ffn2_w_gate, 'ffn2_w_up': ffn2_w_up, 'ffn2_w_down': ffn2_w_down}


def reference(x, ffn1_norm, ffn1_w_gate, ffn1_w_up, ffn1_w_down, mix_norm, w_in, q_norm,
              k_norm, rel_bias, ssm_lambda_re, ssm_lambda_im, ssm_log_dt, ssm_b_re, ssm_b_im,
              ssm_c_re, ssm_c_im, ssm_d, glu_w, glu_b, w_out, ffn2_norm, ffn2_w_gate,
              ffn2_w_up, ffn2_w_down):
    bsz, seq, _ = x.shape
    h = x
    for l in range(DEPTH):
        h = h + 0.5 * swiglu_ffn(rms_norm(h, ffn1_norm[l]), ffn1_w_gate[l], ffn1_w_up[l], ffn1_w_down[l])
        hn = rms_norm(h, mix_norm[l])
        proj = hn @ w_in[l]
        q = proj[..., :ATTN_WIDTH].reshape(bsz, seq, N_HEADS, HEAD_DIM)
        k = proj[..., ATTN_WIDTH:2 * ATTN_WIDTH].reshape(bsz, seq, N_HEADS, HEAD_DIM)
        v = proj[..., 2 * ATTN_WIDTH:3 * ATTN_WIDTH].reshape(bsz, seq, N_HEADS, HEAD_DIM)
        u = proj[..., 3 * ATTN_WIDTH:]
        q = rms_norm(q, q_norm[l])
        k = rms_norm(k, k_norm[l])
        attn = mixture_of_dilations(q, k, v, rel_bias).reshape(bsz, seq, ATTN_WIDTH).astype(h.dtype)
        y = s5_ssm(u, ssm_lambda_re[l], ssm_lambda_im[l], ssm_log_dt[l], ssm_b_re[l], ssm_b_im[l],
                   ssm_c_re[l], ssm_c_im[l], ssm_d[l])
        y = jax.nn.gelu(y)
        y = y * jax.nn.sigmoid(y @ glu_w[l].astype(jnp.float32) + glu_b[l].astype(jnp.float32))
        mixed = jnp.concatenate([attn, y.astype(h.dtype)], axis=-1) @ w_out[l]
        h = h + mixed
        h = h + 0.5 * swiglu_ffn(rms_norm(h, ffn2_norm[l]), ffn2_w_gate[l], ffn2_w_up[l], ffn2_w_down[l])
    return h
```

```python
from contextlib import ExitStack
import numpy as np
import ml_dtypes
import concourse.bass as bass
import concourse.mybir as mybir
from concourse.bass_utils import run_bass_kernel_spmd

F32 = mybir.dt.float32
BF16 = mybir.dt.bfloat16
ALU = mybir.AluOpType
AF = mybir.ActivationFunctionType

D = 2048
KD = D // 128
FF = 5632
FC = FF // 128
SEQ = 4096
BATCH = 4
EPS = 1e-6
TT = 1024
NTB = TT // 512
NPARTS = 11
FPP = FC // NPARTS
WSLOT = 128
WDW = 512
NST = 3


class Op:
    __slots__ = ("eng", "fn", "deps", "needed", "dma", "tok")

    def __init__(self, eng, fn, dma):
        self.eng = eng
        self.fn = fn
        self.deps = []
        self.needed = False
        self.dma = dma
        self.tok = None


class Prog:
    ENGS = ("pe", "act", "dve", "pool", "sp")
    LIMIT = 20000

    def __init__(self, nc, stack):
        self.nc = nc
        self.stack = stack
        self.eng = {"pe": nc.tensor, "act": nc.scalar, "dve": nc.vector,
                    "pool": nc.gpsimd, "sp": nc.sync}
        self.ops = []
        self.last_w = {}
        self.readers = {}
        self.cnt = {e: 0 for e in self.ENGS}
        self.esems = {e: [] for e in self.ENGS}
        self.streams = {}
        self.waited = {e: {} for e in self.ENGS}
        self.last_op = {}
        self.nsem = 0

    def _newsem(self, name):
        self.nsem += 1
        return self.stack.enter_context(self.nc.semaphore(f"{name}_{self.nsem}"))

    def stream(self, name, nsems):
        self.streams[name] = dict(sems=[self._newsem(name) for _ in range(nsems)], n=0)

    def op(self, eng, fn, reads=(), writes=(), dma=None):
        o = Op(eng, fn, dma)
        deps = {}
        for r in reads:
            w = self.last_w.get(r)
            if w is not None:
                deps[id(w)] = w
        for r in writes:
            w = self.last_w.get(r)
            if w is not None:
                deps[id(w)] = w
            rd = self.readers.get(r)
            if rd:
                for x in rd.values():
                    deps[id(x)] = x
        for p in deps.values():
            if p is o:
                continue
            if p.dma is None and p.eng == "pe" and eng == "pe" and dma is None:
                continue
            p.needed = True
            o.deps.append(p)
        for r in writes:
            self.last_w[r] = o
            self.readers[r] = {}
        key = eng if dma is None else ("dma", dma, len(self.ops))
        for r in reads:
            self.readers.setdefault(r, {})[key if dma is None else id(o)] = o
        self.ops.append(o)
        self.last_op[eng] = o
        return o

    def _wait(self, eng, sem, val):
        k = id(sem)
        w = self.waited[eng]
        if w.get(k, 0) >= val:
            return
        w[k] = val
        self.eng[eng].wait_ge(sem, val)

    def flush(self):
        for o in self.ops:
            for p in o.deps:
                sem, val = p.tok
                self._wait(o.eng, sem, val)
            if o.dma is not None:
                st = self.streams[o.dma]
                n = st["n"]
                st["n"] = n + 1
                R = len(st["sems"])
                sem = st["sems"][n % R]
                prev = 16 * (n // R)
                if prev > 0:
                    self._wait(o.eng, sem, prev)
                ins = o.fn()
                ins.then_inc(sem, 16)
                o.tok = (sem, prev + 16)
            else:
                ins = o.fn()
                if o.needed:
                    c = self.cnt[o.eng]
                    ep, v = divmod(c, self.LIMIT)
                    sems = self.esems[o.eng]
                    if ep >= len(sems):
                        sems.append(self._newsem("e" + o.eng))
                    ins.then_inc(sems[ep], 1)
                    self.cnt[o.eng] = c + 1
                    o.tok = (sems[ep], v + 1)
        self.ops = []

    def barrier(self):
        lasts = []
        for e in self.ENGS:
            o = self.last_op.get(e)
            if o is not None and o.dma is None:
                o.needed = True
                lasts.append(o)
        self.flush()
        for e in self.ENGS:
            for o in lasts:
                if o.eng == e and e == "pe":
                    continue
                self._wait(e, *o.tok)
            for st in self.streams.values():
                R = len(st["sems"])
                for i, sem in enumerate(st["sems"]):
                    cnt = (st["n"] - i + R - 1) // R
                    if cnt > 0:
                        self._wait(e, sem, 16 * cnt)
        self.last_w = {}
        self.readers = {}
        self.last_op = {}


NEG = -30000.0
BUCKET_ROUND = False
N_HEADS = 8
N_GROUPS = 64
DILS = (1, 4, 16)


def _dap(h, off, dims):
    return bass.AP(h, off, [list(d) for d in dims])


def build(cfg):
    npair = cfg.get("npair", 1)
    debug = cfg.get("debug", False)
    stages = cfg.get("stages", "ABC")
    ssm_on = cfg.get("ssm", True)
    ntok = SEQ // npair
    NT = ntok // TT
    HPC = N_HEADS // npair
    GPC = N_GROUPS // npair
    NP2 = GPC // 2
    NUC = GPC * 16 // 128
    MIXR = HPC * 128 + GPC * 16
    nc = bass.Bass("TRN2", target_bir_lowering=False)
    SCR = "ExternalOutput" if debug else "Internal"

    def dt(name, shape, dtype=F32, kind="ExternalInput"):
        return nc.dram_tensor(name, shape, dtype, kind=kind)

    x_d = dt("x", [ntok, D]).ap()
    out_d = dt("out", [ntok, D], kind="ExternalOutput").ap()
    ident_d = dt("ident", [128, 128]).ap()
    ones_d = dt("ones_bf", [128, 128], BF16).ap()
    jrev_d = dt("jrev_bf", [128, 128], BF16).ap()
    oh_d = dt("onehot", [32, 3 * 129]).ap()
    shapes = {"ffn1_norm": [D], "ffn1_w_gate": [D, FF], "ffn1_w_up": [D, FF], "ffn1_w_down": [FF, D],
              "ffn2_norm": [D], "ffn2_w_gate": [D, FF], "ffn2_w_up": [D, FF], "ffn2_w_down": [FF, D],
              "mix_norm": [D], "w_in": [D, 4096], "q_norm": [128], "k_norm": [128], "rel_bias": [32, 8],
              "glu_w": [1024, 1024], "glu_b": [1024], "w_out": [D, D]}
    W = {n: dt(n, shapes[n]).ap() for n in shapes}
    sshapes = {"ssm_lambda_re": [GPC * 64], "ssm_lambda_im": [GPC * 64], "ssm_log_dt": [GPC], "ssm_b_re": [GPC * 1024],
               "ssm_b_im": [GPC * 1024], "ssm_c_re": [GPC * 1024], "ssm_c_im": [GPC * 1024], "ssm_d": [GPC * 16]}
    SS_h = {n: dt(n, sshapes[n]) for n in sshapes}
    SS = {n: h.ap() for n, h in SS_h.items()}
    iotac_d = dt("iota_c", [128, 1]).ap(); iotar_d = dt("iota_r", [128, 128]).ap()
    tri_d = dt("tri_bf", [128, 128], BF16).ap(); ntri_d = dt("ntri_bf", [128, 128], BF16).ap()
    mask2_d = dt("mask2", [128, 2]).ap(); nmask2_d = dt("nmask2", [128, 2]).ap(); mask3_d = dt("mask3", [128, 4]).ap()
    sel_d = dt("sel", [32, 32, 128]).ap()
    prm_h = dt("prm", [2 * GPC * 64], F32, kind=SCR)
    x1s_h = dt("x1s", [NT * 128 * KD * TT], F32, kind=SCR)
    projA_h = dt("projA", [4096 * ntok], BF16, kind=SCR)
    mix_h = dt("mix", [MIXR * SEQ], BF16, kind=SCR)
    zpad_h = dt("zpad", [8 * 3 * 384], F32, kind=SCR)
    projG_h, mixG_h = projA_h, mix_h
    PROJ_R = 4096 * ntok
    MIX_R = MIXR * SEQ

    with ExitStack() as stack:
        P = Prog(nc, stack)
        stack.enter_context(nc.allow_non_contiguous_dma(reason="tiny strided parameter loads"))
        mk_sb = lambda st, pfx='': (lambda name, shape, dtype=F32: st.enter_context(nc.sbuf_tensor(pfx + name, shape, dtype)))
        sb = mk_sb(stack)
        ps = lambda name, shape, dtype=F32: stack.enter_context(nc.psum_tensor(name, shape, dtype))
        sp, act, dve, pool, pe = nc.sync, nc.scalar, nc.vector, nc.gpsimd, nc.tensor

        ident = sb("ident_sb", [128, 128])
        ones = sb("ones_sb", [128, 128], BF16)
        jrev = sb("jrev_sb", [128, 128], BF16)
        g1 = sb("g1", [128, KD])
        g2 = sb("g2", [128, KD])
        g3 = sb("g3", [128, KD])
        gqk = sb("gqk", [128, 2])
        glub = sb("glub", [128, 8])
        pb = [ps(f"pb{i}", [128, 512]) for i in range(8)]
        PT, PG, PU, PD = (0, 1), (2, 3), (4, 5), (6, 7)

        P.stream("xin", 2)
        P.stream("const", 1)
        P.stream("wst", NST)
        P.stream("out", 2)
        P.stream("scr", 4)
        P.stream("ld", 4)

        def cdma(dst, src, res):
            P.op("sp", lambda: sp.dma_start(out=dst, in_=src), writes=[res], dma="const")

        cdma(ident[:], ident_d, "ident")
        cdma(ones[:], ones_d, "ones")
        cdma(jrev[:], jrev_d, "jrev")
        cdma(g1[:], W["ffn1_norm"].rearrange("(k p) -> p k", p=128), "g1")
        cdma(g2[:], W["mix_norm"].rearrange("(k p) -> p k", p=128), "g2")
        cdma(g3[:], W["ffn2_norm"].rearrange("(k p) -> p k", p=128), "g3")
        cdma(gqk[:, 0:1], W["q_norm"].rearrange("(p o) -> p o", o=1), "gqk")
        cdma(gqk[:, 1:2], W["k_norm"].rearrange("(p o) -> p o", o=1), "gqk")
        cdma(glub[:], W["glu_b"].rearrange("(k p) -> p k", p=128), "glub")
        P.op("dve", lambda: dve.tensor_scalar(out=gqk[:, 0:1], in0=gqk[:, 0:1], scalar1=float(128 ** -0.5),
                                              scalar2=None, op0=ALU.mult), reads=["gqk"], writes=["gqk"])

        cnt = {"st": 0, "cp": 0, "xs": 0, "sq": 0, "wg": 0, "wd": 0, "pg": 0, "pd": 0, "pt": 0, "sg": 0,
               "qst": 0, "vst": 0}

        def evac_copy(out_ap, in_ap, reads, writes):
            cnt["cp"] += 1
            if cnt["cp"] % 2:
                P.op("act", lambda: act.copy(out=out_ap, in_=in_ap), reads=reads, writes=writes)
            else:
                P.op("dve", lambda: dve.tensor_copy(out=out_ap, in_=in_ap), reads=reads, writes=writes)

        def row_local_phase(phase):
            with ExitStack() as st:
                sbl = mk_sb(st, phase + "_")
                xT = sbl("xT", [128, KD, TT])
                hnT = sbl("hnT", [128, KD, TT], BF16)
                HT = sbl("HT", [128, FPP, TT], BF16)
                xs = [sbl(f"xs{i}", [128, D]) for i in range(2)]
                wg = [sbl(f"wg{i}", [128, KD, WSLOT], BF16) for i in range(2)]
                wu = [sbl(f"wu{i}", [128, KD, WSLOT], BF16) for i in range(2)]
                wd = [sbl(f"wd{i}", [128, FPP, WDW], BF16) for i in range(2)]
                stage = [sbl(f"stage{i}", [128, KD * WSLOT]) for i in range(NST)]
                sq = [sbl(f"sq{i}", [128, 512], BF16) for i in range(2)]
                sg = [sbl(f"sg{i}", [128, 512]) for i in range(2)]
                rstd = [sbl(f"rstd{i}", [128, 512]) for i in range(NTB)]
                qst = [sbl(f"qst{i}", [128, 512], BF16) for i in range(2)]
                vst = [sbl(f"vst{i}", [128, 4, 128], BF16) for i in range(2)]
                yTb = sbl("yTb", [128, 8, TT], BF16) if phase == "C" else None

                def load_xT(t0):
                    for s in range(TT // 128):
                        slot = cnt["xs"] % 2
                        cnt["xs"] += 1
                        src = x_d[t0 + s * 128:t0 + (s + 1) * 128, :]
                        P.op("sp", lambda slot=slot, src=src: sp.dma_start(out=xs[slot][:], in_=src),
                             writes=[("xs", slot)], dma="xin")
                        for k4 in range(KD // 4):
                            b = PT[cnt["pt"] % 2]
                            cnt["pt"] += 1
                            for j in range(4):
                                k = k4 * 4 + j
                                P.op("pe", lambda b=b, j=j, k=k, slot=slot: pe.transpose(
                                    out=pb[b][:, j * 128:(j + 1) * 128], in_=xs[slot][:, k * 128:(k + 1) * 128],
                                    identity=ident[:]), reads=[("xs", slot), "ident"], writes=[("ps", b)])
                            evac_copy(xT[:, k4 * 4:(k4 + 1) * 4, s * 128:(s + 1) * 128],
                                      pb[b][:, :].rearrange("p (j t) -> p j t", j=4),
                                      [("ps", b)], [("xT", k4 * 4 + j, s // 4) for j in range(4)])

                def store_xT(t0):
                    for s in range(TT // 128):
                        slot = cnt["xs"] % 2
                        cnt["xs"] += 1
                        for k4 in range(KD // 4):
                            b = PT[cnt["pt"] % 2]
                            cnt["pt"] += 1
                            for j in range(4):
                                k = k4 * 4 + j
                                P.op("pe", lambda b=b, j=j, k=k, s=s: pe.transpose(
                                    out=pb[b][:, j * 128:(j + 1) * 128], in_=xT[:, k, s * 128:(s + 1) * 128],
                                    identity=ident[:]), reads=[("xT", k, s // 4), "ident"], writes=[("ps", b)])
                            evac_copy(xs[slot][:, k4 * 512:(k4 + 1) * 512], pb[b][:, :],
                                      [("ps", b)], [("xs", slot)])
                        dst = out_d[t0 + s * 128:t0 + (s + 1) * 128, :]
                        P.op("sp", lambda slot=slot, dst=dst: sp.dma_start(out=dst, in_=xs[slot][:]),
                             reads=[("xs", slot)], dma="out")

                def rsqrt_inplace(t, res):
                    P.op("act", lambda: act.sqrt(out=t, in_=t), reads=[res], writes=[res])
                    P.op("dve", lambda: dve.reciprocal(out=t, in_=t), reads=[res], writes=[res])

                def rmsnorm(g, gname):
                    for tb in range(NTB):
                        b = PD[tb % 2]
                        tsl = slice(tb * 512, (tb + 1) * 512)
                        for k in range(KD):
                            q = cnt["sq"] % 2
                            cnt["sq"] += 1
                            P.op("act", lambda q=q, k=k, tsl=tsl: act.activation(out=sq[q][:], in_=xT[:, k, tsl], func=AF.Square),
                                 reads=[("xT", k, tb)], writes=[("sq", q)])
                            P.op("pe", lambda q=q, k=k, b=b: pe.matmul(pb[b][:, :], lhsT=ones[:], rhs=sq[q][:],
                                                                        start=(k == 0), stop=(k == KD - 1)),
                                 reads=[("sq", q), "ones"], writes=[("ps", b)])
                        P.op("dve", lambda b=b, tb=tb: dve.tensor_scalar(out=rstd[tb][:], in0=pb[b][:, :], scalar1=1.0 / D,
                                                                          scalar2=EPS, op0=ALU.mult, op1=ALU.add),
                             reads=[("ps", b)], writes=[("rstd", tb)])
                        rsqrt_inplace(rstd[tb][:], ("rstd", tb))
                        for k in range(KD):
                            P.op("dve", lambda k=k, tb=tb, tsl=tsl: dve.scalar_tensor_tensor(
                                out=hnT[:, k, tsl], in0=xT[:, k, tsl], scalar=g[:, k:k + 1], in1=rstd[tb][:],
                                op0=ALU.mult, op1=ALU.mult),
                                 reads=[("xT", k, tb), ("rstd", tb), gname], writes=[("hn", k, tb)])

                def wload(dst_ap, src_ap, wres):
                    st_ = cnt["st"] % NST
                    cnt["st"] += 1
                    shp = dst_ap.shape
                    sview = stage[st_][:, 0:shp[1] * shp[2]].rearrange("p (a b) -> p a b", a=shp[1])
                    P.op("sp", lambda: sp.dma_start(out=sview, in_=src_ap), writes=[("stage", st_)], dma="wst")
                    if cnt["st"] % 3 == 0:
                        P.op("act", lambda: act.copy(out=dst_ap, in_=sview), reads=[("stage", st_)], writes=[wres])
                    else:
                        P.op("pool", lambda: pool.tensor_copy(out=dst_ap, in_=sview), reads=[("stage", st_)], writes=[wres])

                def ffn(wgate, wup, wdown):
                    wg_v = wgate.rearrange("(k p) f -> p k f", p=128)
                    wu_v = wup.rearrange("(k p) f -> p k f", p=128)
                    wd_v = wdown.rearrange("(c p) d -> p c d", p=128)
                    for part in range(NPARTS):
                        for j in range(FPP):
                            f0 = (part * FPP + j) * 128
                            slot = cnt["wg"] % 2
                            cnt["wg"] += 1
                            wload(wg[slot][:, :, :], wg_v[:, :, f0:f0 + 128], ("wg", slot))
                            wload(wu[slot][:, :, :], wu_v[:, :, f0:f0 + 128], ("wu", slot))
                            for tb in range(NTB):
                                tsl = slice(tb * 512, (tb + 1) * 512)
                                i = cnt["pg"] % 2
                                cnt["pg"] += 1
                                bg, bu = PG[i], PU[i]
                                for k in range(KD):
                                    P.op("pe", lambda k=k, bg=bg, slot=slot, tsl=tsl: pe.matmul(
                                        pb[bg][:, :], lhsT=wg[slot][:, k, :], rhs=hnT[:, k, tsl],
                                        start=(k == 0), stop=(k == KD - 1)),
                                         reads=[("wg", slot), ("hn", k, tb)], writes=[("ps", bg)])
                                for k in range(KD):
                                    P.op("pe", lambda k=k, bu=bu, slot=slot, tsl=tsl: pe.matmul(
                                        pb[bu][:, :], lhsT=wu[slot][:, k, :], rhs=hnT[:, k, tsl],
                                        start=(k == 0), stop=(k == KD - 1)),
                                         reads=[("wu", slot), ("hn", k, tb)], writes=[("ps", bu)])
                                q = cnt["sg"] % 2
                                cnt["sg"] += 1
                                P.op("act", lambda q=q, bg=bg: act.activation(out=sg[q][:], in_=pb[bg][:, :], func=AF.Silu),
                                     reads=[("ps", bg)], writes=[("sg", q)])
                                P.op("dve", lambda q=q, bu=bu, j=j, tsl=tsl: dve.tensor_tensor(
                                    out=HT[:, j, tsl], in0=sg[q][:], in1=pb[bu][:, :], op=ALU.mult),
                                     reads=[("sg", q), ("ps", bu)], writes=[("H", j, tb)])
                        for dg in range(D // WDW):
                            slot = cnt["wd"] % 2
                            cnt["wd"] += 1
                            wload(wd[slot][:, :, :], wd_v[:, part * FPP:(part + 1) * FPP, dg * WDW:(dg + 1) * WDW], ("wd", slot))
                            for di in range(WDW // 128):
                                dc = dg * (WDW // 128) + di
                                for tb in range(NTB):
                                    tsl = slice(tb * 512, (tb + 1) * 512)
                                    b = PD[cnt["pd"] % 2]
                                    cnt["pd"] += 1
                                    for j in range(FPP):
                                        P.op("pe", lambda j=j, b=b, slot=slot, di=di, tsl=tsl: pe.matmul(
                                            pb[b][:, :], lhsT=wd[slot][:, j, di * 128:(di + 1) * 128], rhs=HT[:, j, tsl],
                                            start=(j == 0), stop=(j == FPP - 1)),
                                             reads=[("wd", slot), ("H", j, tb)], writes=[("ps", b)])
                                    P.op("dve", lambda b=b, dc=dc, tsl=tsl: dve.scalar_tensor_tensor(
                                        out=xT[:, dc, tsl], in0=pb[b][:, :], scalar=0.5, in1=xT[:, dc, tsl],
                                        op0=ALU.mult, op1=ALU.add),
                                         reads=[("ps", b), ("xT", dc, tb)], writes=[("xT", dc, tb)])

                all_xT = [("xT", k, tb) for k in range(KD) for tb in range(NTB)]

                def proj(t0):
                    rmsnorm(g2, "g2")
                    win_v = W["w_in"].rearrange("(k p) f -> p k f", p=128)
                    for oc in range(32):
                        slot = cnt["wg"] % 2
                        cnt["wg"] += 1
                        wload(wg[slot][:, :, :], win_v[:, :, oc * 128:(oc + 1) * 128], ("wg", slot))
                        if 16 <= oc < 24:
                            for s4 in range(TT // 512):
                                b = PD[cnt["pd"] % 2]
                                cnt["pd"] += 1
                                for si in range(4):
                                    s = s4 * 4 + si
                                    for k in range(KD):
                                        P.op("pe", lambda k=k, b=b, si=si, s=s, slot=slot: pe.matmul(
                                            pb[b][:, si * 128:(si + 1) * 128], lhsT=hnT[:, k, s * 128:(s + 1) * 128],
                                            rhs=wg[slot][:, k, :], start=(k == 0), stop=(k == KD - 1)),
                                             reads=[("wg", slot), ("hn", k, s // 4)], writes=[("ps", b)])
                                vq = cnt["vst"] % 2
                                cnt["vst"] += 1
                                evac_copy(vst[vq][:, :, :], pb[b][:, :].rearrange("p (j t) -> p j t", j=4),
                                          [("ps", b)], [("vst", vq)])
                                dst = _dap(projA_h, 3072 * ntok + (t0 + s4 * 512) * 1024 + (oc - 16) * 128,
                                           [[1024, 128], [128 * 1024, 4], [1, 128]])
                                P.op("sp", lambda vq=vq, dst=dst: sp.dma_start(out=dst, in_=vst[vq][:, :, :]),
                                     reads=[("vst", vq)], writes=["projA"], dma="scr")
                            continue
                        for tb in range(NTB):
                            tsl = slice(tb * 512, (tb + 1) * 512)
                            i = cnt["pg"] % 2
                            cnt["pg"] += 1
                            bg, bn = PG[i], PU[i]
                            for k in range(KD):
                                P.op("pe", lambda k=k, bg=bg, slot=slot, tsl=tsl: pe.matmul(
                                    pb[bg][:, :], lhsT=wg[slot][:, k, :], rhs=hnT[:, k, tsl],
                                    start=(k == 0), stop=(k == KD - 1)),
                                     reads=[("wg", slot), ("hn", k, tb)], writes=[("ps", bg)])
                            qq = cnt["qst"] % 2
                            cnt["qst"] += 1
                            if oc < 16:
                                q = cnt["sq"] % 2
                                cnt["sq"] += 1
                                r = cnt["sg"] % 2
                                cnt["sg"] += 1
                                P.op("act", lambda q=q, bg=bg: act.activation(out=sq[q][:], in_=pb[bg][:, :], func=AF.Square),
                                     reads=[("ps", bg)], writes=[("sq", q)])
                                P.op("pe", lambda q=q, bn=bn: pe.matmul(pb[bn][:, :], lhsT=ones[:], rhs=sq[q][:], start=True, stop=True),
                                     reads=[("sq", q), "ones"], writes=[("ps", bn)])
                                P.op("dve", lambda r=r, bn=bn: dve.tensor_scalar(out=sg[r][:], in0=pb[bn][:, :], scalar1=1.0 / 128,
                                                                                  scalar2=EPS, op0=ALU.mult, op1=ALU.add),
                                     reads=[("ps", bn)], writes=[("sg", r)])
                                rsqrt_inplace(sg[r][:], ("sg", r))
                                col = 0 if oc < 8 else 1
                                P.op("dve", lambda r=r, bg=bg, qq=qq, col=col: dve.scalar_tensor_tensor(
                                    out=qst[qq][:], in0=pb[bg][:, :], scalar=gqk[:, col:col + 1], in1=sg[r][:],
                                    op0=ALU.mult, op1=ALU.mult),
                                     reads=[("ps", bg), ("sg", r), "gqk"], writes=[("qst", qq)])
                                row0 = oc * 128
                            else:
                                evac_copy(qst[qq][:], pb[bg][:, :], [("ps", bg)], [("qst", qq)])
                                row0 = 2048 + (oc - 24) * 128
                            dst = _dap(projA_h, row0 * ntok + t0 + tb * 512, [[ntok, 128], [1, 512]])
                            P.op("sp", lambda qq=qq, dst=dst: sp.dma_start(out=dst, in_=qst[qq][:]),
                                 reads=[("qst", qq)], writes=["projA"], dma="scr")

                def mix_out(t0, tg0):
                    for h in range(N_HEADS):
                        r, hl = divmod(h, HPC)
                        src = _dap(mixG_h, r * MIX_R + (hl * 128) * SEQ + tg0, [[SEQ, 128], [1, TT]])
                        P.op("sp", lambda h=h, src=src: sp.dma_start(out=hnT[:, h, :], in_=src),
                             reads=["mix"], writes=[("hn", h, tb) for tb in range(NTB)], dma="ld")
                    for c in range(8):
                        r, cl = divmod(c, NUC)
                        src = _dap(mixG_h, r * MIX_R + (HPC * 128 + cl * 128) * SEQ + tg0, [[SEQ, 128], [1, TT]])
                        P.op("sp", lambda c=c, src=src: sp.dma_start(out=yTb[:, c, :], in_=src),
                             reads=["mix"], writes=[("yT", c)], dma="ld")
                    gw_v = W["glu_w"].rearrange("(k p) f -> p k f", p=128)
                    for c2 in range(8):
                        slot = cnt["wg"] % 2
                        cnt["wg"] += 1
                        wload(wg[slot][:, 0:8, :], gw_v[:, :, c2 * 128:(c2 + 1) * 128], ("wg", slot))
                        for tb in range(NTB):
                            tsl = slice(tb * 512, (tb + 1) * 512)
                            bg = PG[cnt["pg"] % 2]
                            cnt["pg"] += 1
                            for c in range(8):
                                P.op("pe", lambda c=c, bg=bg, slot=slot, tsl=tsl: pe.matmul(
                                    pb[bg][:, :], lhsT=wg[slot][:, c, :], rhs=yTb[:, c, tsl], start=(c == 0), stop=(c == 7)),
                                     reads=[("wg", slot), ("yT", c)], writes=[("ps", bg)])
                            q = cnt["sg"] % 2
                            cnt["sg"] += 1
                            P.op("act", lambda q=q, bg=bg, c2=c2: act.activation(out=sg[q][:], in_=pb[bg][:, :], func=AF.Sigmoid,
                                                                                  bias=glub[:, c2:c2 + 1]),
                                 reads=[("ps", bg), "glub"], writes=[("sg", q)])
                            P.op("dve", lambda q=q, c2=c2, tsl=tsl: dve.tensor_tensor(
                                out=hnT[:, 8 + c2, tsl], in0=sg[q][:], in1=yTb[:, c2, tsl], op=ALU.mult),
                                 reads=[("sg", q), ("yT", c2)], writes=[("hn", 8 + c2, tb)])
                    wo_v = W["w_out"].rearrange("(k p) f -> p k f", p=128)
                    for dc in range(KD):
                        slot = cnt["wg"] % 2
                        cnt["wg"] += 1
                        wload(wg[slot][:, :, :], wo_v[:, :, dc * 128:(dc + 1) * 128], ("wg", slot))
                        for tb in range(NTB):
                            tsl = slice(tb * 512, (tb + 1) * 512)
                            b = PD[cnt["pd"] % 2]
                            cnt["pd"] += 1
                            for k in range(KD):
                                P.op("pe", lambda k=k, b=b, slot=slot, tsl=tsl: pe.matmul(
                                    pb[b][:, :], lhsT=wg[slot][:, k, :], rhs=hnT[:, k, tsl], start=(k == 0), stop=(k == KD - 1)),
                                     reads=[("wg", slot), ("hn", k, tb)], writes=[("ps", b)])
                            P.op("dve", lambda b=b, dc=dc, tsl=tsl: dve.scalar_tensor_tensor(
                                out=xT[:, dc, tsl], in0=pb[b][:, :], scalar=1.0, in1=xT[:, dc, tsl],
                                op0=ALU.mult, op1=ALU.add),
                                 reads=[("ps", b), ("xT", dc, tb)], writes=[("xT", dc, tb)])

                for tt in range(NT):
                    t0 = tt * TT
                    x1v = _dap(x1s_h, tt * 128 * KD * TT, [[KD * TT, 128], [TT, KD], [1, TT]])
                    if phase == "A":
                        load_xT(t0)
                        rmsnorm(g1, "g1")
                        ffn(W["ffn1_w_gate"], W["ffn1_w_up"], W["ffn1_w_down"])
                        P.op("sp", lambda x1v=x1v: sp.dma_start(out=x1v, in_=xT[:, :, :]), reads=all_xT, writes=["x1s"], dma="scr")
                        proj(t0)
                    else:
                        P.op("sp", lambda x1v=x1v: sp.dma_start(out=xT[:, :, :], in_=x1v), reads=["x1s"], writes=all_xT, dma="ld")
                        mix_out(t0, cfg.get("tok_base", 0) + t0)
                        rmsnorm(g3, "g3")
                        ffn(W["ffn2_w_gate"], W["ffn2_w_up"], W["ffn2_w_down"])
                        store_xT(t0)
                P.barrier()

        def attention_phase():
            with ExitStack() as st:
                sbl = mk_sb(st, "B_")
                qTh = sbl("qTh", [128, SEQ], BF16)
                kTh = sbl("kTh", [128, SEQ], BF16)
                vb = [sbl(f"vb{i}", [128, 32, 128], BF16) for i in range(3)]
                acc = sbl("acc", [128, 2, SEQ])
                rden = sbl("rden", [128, SEQ])
                outst = sbl("outst", [128, SEQ], BF16)
                Bt = [[sbl(f"Bt{pi}_{hl}", [128, 256], BF16) for hl in range(HPC)] for pi in range(3)]
                Hf = [sbl(f"Hf{i}", [128, 256]) for i in range(2)]
                pT = [sbl(f"pT{i}", [128, 256], BF16) for i in range(2)]
                rb = sbl("rb", [32, 8])
                oh = sbl("oh", [32, 3 * 129])
                zt = sbl("zt", [8, 3, 384])
                cdma(rb[:], W["rel_bias"], "rb")
                cdma(oh[:], oh_d, "oh")
                P.op("pe", lambda: pe.matmul(pb[0][0:8, 0:387], lhsT=rb[:, :], rhs=oh[:, :], start=True, stop=True),
                     reads=["rb", "oh"], writes=[("ps", 0)])
                P.op("pool", lambda: pool.memset(zt[:], NEG), writes=["zt"])
                P.op("dve", lambda: dve.tensor_copy(out=zt[:, :, 127:256], in_=pb[0][0:8, 0:387].rearrange("p (a b) -> p a b", a=3)),
                     reads=[("ps", 0)], writes=["zt"])
                P.op("sp", lambda: sp.dma_start(out=_dap(zpad_h, 0, [[3 * 384, 8], [384, 3], [1, 384]]), in_=zt[:]),
                     reads=["zt"], writes=["zpad"], dma="scr")
                hbase = cfg.get("head_base", 0)
                for hl in range(HPC):
                    for pi in range(3):
                        i = (hl * 3 + pi) % 2
                        src = _dap(zpad_h, ((hbase + hl) * 3 + pi) * 384, [[1, 128], [1, 256]])
                        P.op("sp", lambda i=i, src=src: sp.dma_start(out=Hf[i][:], in_=src), reads=["zpad"],
                             writes=[("Hf", i)], dma="ld")
                        P.op("act", lambda i=i, hl=hl, pi=pi: act.copy(out=Bt[pi][hl][:], in_=Hf[i][:]),
                             reads=[("Hf", i)], writes=[("Bt", pi, hl)])
                nblk = 0
                for hl in range(HPC):
                    hg = hbase + hl
                    for r in range(npair):
                        P.op("sp", lambda r=r, hg=hg: sp.dma_start(
                            out=qTh[:, r * ntok:(r + 1) * ntok],
                            in_=_dap(projG_h, r * PROJ_R + (hg * 128) * ntok, [[ntok, 128], [1, ntok]])),
                             reads=["projA"], writes=["qTh"], dma="ld")
                        P.op("sp", lambda r=r, hg=hg: sp.dma_start(
                            out=kTh[:, r * ntok:(r + 1) * ntok],
                            in_=_dap(projG_h, r * PROJ_R + (1024 + hg * 128) * ntok, [[ntok, 128], [1, ntok]])),
                             reads=["projA"], writes=["kTh"], dma="ld")
                    for pi, d in enumerate(DILS):
                        nm = 32 // d
                        mpr = nm // npair if nm >= npair else 1
                        for res in range(d):
                            for m0 in range(0, nm, 8):
                                mm = min(8, nm - m0)
                                m = m0
                                while m < m0 + mm:
                                    tok0 = m * 128 * d + res
                                    r = tok0 // ntok
                                    mend = min(m0 + mm, ((r + 1) * ntok) // (128 * d)) if d * 128 <= ntok else m + 1
                                    nmb = mend - m
                                    src = _dap(projG_h, r * PROJ_R + 3072 * ntok + (tok0 - r * ntok) * 1024 + hg * 128,
                                               [[d * 1024, 128], [128 * d * 1024, nmb], [1, 128]])
                                    bi = res * nm + m
                                    P.op("sp", lambda pi=pi, bi=bi, nmb=nmb, src=src: sp.dma_start(
                                        out=vb[pi][:, bi:bi + nmb, :], in_=src),
                                         reads=["projA"], writes=[("vb", pi)], dma="ld")
                                    m = mend
                    for pi, d in enumerate(DILS):
                        nm = 32 // d
                        for res in range(d):
                            for n in range(nm):
                                off = n * 128 * d + res
                                qa = qTh[:, off:off + 127 * d + 1:d]
                                ka = kTh[:, off:off + 127 * d + 1:d]
                                bS = PT[nblk % 2]
                                bO = PG[nblk % 2]
                                ip = nblk % 2
                                nblk += 1
                                Wd_ = 256 if n > 0 else 128
                                P.op("pe", lambda bS=bS, pi=pi, hl=hl, Wd_=Wd_: pe.matmul(
                                    pb[bS][:, 0:Wd_], lhsT=jrev[:], rhs=Bt[pi][hl][:, 0:Wd_], start=True, stop=False),
                                     reads=["jrev", ("Bt", pi, hl)], writes=[("ps", bS)])
                                P.op("pe", lambda bS=bS, ka=ka, qa=qa, n=n: pe.matmul(
                                    pb[bS][:, 0:128], lhsT=ka, rhs=qa, start=False, stop=(n == 0)),
                                     reads=["qTh", "kTh"], writes=[("ps", bS)])
                                if n > 0:
                                    offp = off - 128 * d
                                    kp = kTh[:, offp:offp + 127 * d + 1:d]
                                    P.op("pe", lambda bS=bS, kp=kp, qa=qa: pe.matmul(
                                        pb[bS][:, 128:256], lhsT=kp, rhs=qa, start=False, stop=True),
                                         reads=["qTh", "kTh"], writes=[("ps", bS)])
                                P.op("act", lambda ip=ip, bS=bS, Wd_=Wd_: act.activation(
                                    out=pT[ip][:, 0:Wd_], in_=pb[bS][:, 0:Wd_], func=AF.Exp),
                                     reads=[("ps", bS)], writes=[("pT", ip)])
                                bi = res * nm + n
                                P.op("pe", lambda bO=bO, pi=pi, bi=bi, ip=ip, n=n: pe.matmul(
                                    pb[bO][:, 0:128], lhsT=vb[pi][:, bi, :], rhs=pT[ip][:, 0:128], start=True, stop=(n == 0)),
                                     reads=[("vb", pi), ("pT", ip)], writes=[("ps", bO)])
                                if n > 0:
                                    P.op("pe", lambda bO=bO, pi=pi, bi=bi, ip=ip: pe.matmul(
                                        pb[bO][:, 0:128], lhsT=vb[pi][:, bi - 1, :], rhs=pT[ip][:, 128:256], start=False, stop=True),
                                         reads=[("vb", pi), ("pT", ip)], writes=[("ps", bO)])
                                P.op("pe", lambda bO=bO, ip=ip, n=n: pe.matmul(
                                    pb[bO][:, 128:256], lhsT=ones[:], rhs=pT[ip][:, 0:128], start=True, stop=(n == 0)),
                                     reads=["ones", ("pT", ip)], writes=[("ps", bO)])
                                if n > 0:
                                    P.op("pe", lambda bO=bO, ip=ip: pe.matmul(
                                        pb[bO][:, 128:256], lhsT=ones[:], rhs=pT[ip][:, 128:256], start=False, stop=True),
                                         reads=["ones", ("pT", ip)], writes=[("ps", bO)])
                                av = acc[:, :, off:off + 127 * d + 1:d]
                                pv = pb[bO][:, 0:256].rearrange("p (a b) -> p a b", a=2)
                                if pi == 0:
                                    P.op("dve", lambda av=av, pv=pv: dve.tensor_copy(out=av, in_=pv),
                                         reads=[("ps", bO)], writes=["acc"])
                                else:
                                    P.op("dve", lambda av=av, pv=pv: dve.tensor_tensor(out=av, in0=pv, in1=av, op=ALU.add),
                                         reads=[("ps", bO), "acc"], writes=["acc"])
                    P.op("dve", lambda: dve.reciprocal(out=rden[:], in_=acc[:, 1, :]), reads=["acc"], writes=["rden"])
                    P.op("pool", lambda: pool.tensor_tensor(out=outst[:], in0=acc[:, 0, :], in1=rden[:], op=ALU.mult),
                         reads=["acc", "rden"], writes=["outst"])
                    P.op("sp", lambda hl=hl: sp.dma_start(out=_dap(mix_h, (hl * 128) * SEQ, [[SEQ, 128], [1, SEQ]]), in_=outst[:]),
                         reads=["outst"], writes=["mix"], dma="scr")
                P.barrier()


        def ssm_phase():
            S = GPC * 64
            TWO_PI = float(2 * np.pi)
            with ExitStack() as st:
                sbl = mk_sb(st, "S_")
                Tm_re = sbl("Tm_re", [128, S]); Tm_im = sbl("Tm_im", [128, S])
                Tp_re = sbl("Tp_re", [128, NP2 * 128]); Tp_im = sbl("Tp_im", [128, NP2 * 128])
                t128_re = sbl("t128_re", [128, NP2]); t128_im = sbl("t128_im", [128, NP2])
                Bblk_re = [sbl(f"Bblk_re{i}", [128, 512], BF16) for i in range(NUC)]
                Bblk_im = [sbl(f"Bblk_im{i}", [128, 512], BF16) for i in range(NUC)]
                Cre = sbl("Cre", [128, NP2, 2, 16], BF16); nCre = sbl("nCre", [128, NP2, 2, 16], BF16)
                nCim = sbl("nCim", [128, NP2, 2, 16], BF16)
                d_col = sbl("d_col", [128, NUC])
                iota_c = sbl("iota_c", [128, 1]); iota_r = sbl("iota_r", [128, 128])
                tri = sbl("tri", [128, 128], BF16); ntri = sbl("ntri", [128, 128], BF16)
                mask2 = sbl("mask2", [128, 2]); nmask2 = sbl("nmask2", [128, 2]); mask3 = sbl("mask3", [128, 4])
                sel = sbl("sel", [32, 32, 128])
                inj_re = sbl("inj_re", [128, NP2]); inj_im = sbl("inj_im", [128, NP2])
                injT_re = sbl("injT_re", [32, 128]); injT_im = sbl("injT_im", [32, 128])
                for t_, d_, r_ in ((iota_c, iotac_d, "iota_c"), (iota_r, iotar_d, "iota_r"), (tri, tri_d, "tri"),
                                   (ntri, ntri_d, "ntri"), (mask2, mask2_d, "mask2"), (nmask2, nmask2_d, "nmask2"),
                                   (mask3, mask3_d, "mask3"), (sel, sel_d, "sel")):
                    cdma(t_[:], d_, r_)
                cdma(d_col[:], SS["ssm_d"].rearrange("(c p) -> p c", p=128), "d_col")

                with ExitStack() as st2:
                    sb2 = mk_sb(st2, "S2_")
                    BLK = 1024
                    tmp = {n: sb2("tg_" + n, [128, BLK]) for n in ("t", "fr", "m", "cosv", "sinv", "mag")}
                    tint = sb2("tg_int", [128, BLK], mybir.dt.int32)

                    def dv(fn, reads, writes):
                        P.op("dve", fn, reads=reads, writes=writes)

                    def trig(out_re, out_im, ang, marg, n, np_, sign, rres, wres):
                        for c0 in range(0, n, BLK):
                            w = min(BLK, n - c0)
                            cs = slice(c0, c0 + w)
                            T = {k: v[0:np_, 0:w] for k, v in tmp.items()}
                            ti = tint[0:np_, 0:w]
                            dv(lambda T=T, cs=cs: dve.tensor_scalar(out=T["t"], in0=ang[:, cs], scalar1=1.0 / TWO_PI, scalar2=None, op0=ALU.mult),
                               rres, ["tg_t"])
                            for name, shift in (("cosv", 0.25), ("sinv", 0.0)):
                                dv(lambda T=T, shift=shift: dve.tensor_scalar(out=T["fr"], in0=T["t"], scalar1=shift, scalar2=None, op0=ALU.add),
                                   ["tg_t"], ["tg_fr"])
                                dv(lambda T=T, ti=ti: dve.tensor_copy(out=ti, in_=T["fr"]), ["tg_fr"], ["tg_i"])
                                dv(lambda T=T, ti=ti: dve.tensor_copy(out=T["m"], in_=ti), ["tg_i"], ["tg_m"])
                                dv(lambda T=T: dve.tensor_tensor(out=T["fr"], in0=T["fr"], in1=T["m"], op=ALU.subtract), ["tg_fr", "tg_m"], ["tg_fr"])
                                dv(lambda T=T: dve.tensor_scalar(out=T["m"], in0=T["fr"], scalar1=0.5, scalar2=None, op0=ALU.is_gt), ["tg_fr"], ["tg_m"])
                                dv(lambda T=T: dve.tensor_tensor(out=T["fr"], in0=T["fr"], in1=T["m"], op=ALU.subtract), ["tg_fr", "tg_m"], ["tg_fr"])
                                dv(lambda T=T: dve.tensor_scalar(out=T["m"], in0=T["fr"], scalar1=-0.5, scalar2=None, op0=ALU.is_lt), ["tg_fr"], ["tg_m"])
                                dv(lambda T=T: dve.tensor_tensor(out=T["fr"], in0=T["fr"], in1=T["m"], op=ALU.add), ["tg_fr", "tg_m"], ["tg_fr"])
                                P.op("act", lambda T=T, name=name: act.activation(out=T[name], in_=T["fr"], func=AF.Sin, scale=TWO_PI),
                                     reads=["tg_fr"], writes=["tg_" + name])
                            P.op("act", lambda T=T, cs=cs: act.activation(out=T["mag"], in_=marg[:, cs], func=AF.Exp, scale=float(sign)),
                                 reads=rres, writes=["tg_mag"])
                            dv(lambda T=T, cs=cs: dve.tensor_tensor(out=out_re[:, cs], in0=T["mag"], in1=T["cosv"], op=ALU.mult),
                               ["tg_mag", "tg_cosv"], wres)
                            dv(lambda T=T, cs=cs: dve.scalar_tensor_tensor(out=out_im[:, cs], in0=T["mag"], scalar=float(sign), in1=T["sinv"],
                                                                           op0=ALU.mult, op1=ALU.mult),
                               ["tg_mag", "tg_sinv"], wres)

                    col = lambda n: sb2(n, [128, NP2])
                    lre, lim, ldt, alpha, theta = col("lre"), col("lim"), col("ldt"), col("alpha"), col("theta")
                    a_re, a_im, cf_re, cf_im, w1, w2 = col("a_re"), col("a_im"), col("cf_re"), col("cf_im"), col("w1"), col("w2")
                    al128, th128 = col("al128"), col("th128")
                    cdma(lre[:], SS["ssm_lambda_re"].rearrange("(q p) -> p q", p=128), "lre")
                    cdma(lim[:], SS["ssm_lambda_im"].rearrange("(q p) -> p q", p=128), "lim")
                    ldt_h = SS_h["ssm_log_dt"]
                    cdma(ldt[0:64, :], _dap(ldt_h, 0, [[0, 64], [2, NP2]]), "ldt")
                    cdma(ldt[64:128, :], _dap(ldt_h, 1, [[0, 64], [2, NP2]]), "ldt")
                    P.op("act", lambda: act.activation(out=ldt[:], in_=ldt[:], func=AF.Exp), reads=["ldt"], writes=["ldt"])
                    dv(lambda: dve.tensor_tensor(out=alpha[:], in0=lre[:], in1=ldt[:], op=ALU.mult), ["lre", "ldt"], ["alpha"])
                    dv(lambda: dve.tensor_tensor(out=theta[:], in0=lim[:], in1=ldt[:], op=ALU.mult), ["lim", "ldt"], ["theta"])
                    trig(a_re, a_im, theta, alpha, NP2, 128, 1.0, ["theta", "alpha"], ["a"])
                    dv(lambda: dve.tensor_scalar(out=a_re[:], in0=a_re[:], scalar1=-1.0, scalar2=None, op0=ALU.add), ["a"], ["a"])
                    dv(lambda: dve.tensor_tensor(out=w1[:], in0=lre[:], in1=lre[:], op=ALU.mult), ["lre"], ["w1"])
                    dv(lambda: dve.tensor_tensor(out=w2[:], in0=lim[:], in1=lim[:], op=ALU.mult), ["lim"], ["w2"])
                    dv(lambda: dve.tensor_tensor(out=w1[:], in0=w1[:], in1=w2[:], op=ALU.add), ["w1", "w2"], ["w1"])
                    dv(lambda: dve.reciprocal(out=w1[:], in_=w1[:]), ["w1"], ["w1"])
                    dv(lambda: dve.tensor_tensor(out=cf_re[:], in0=a_re[:], in1=lre[:], op=ALU.mult), ["a", "lre"], ["cf_re"])
                    dv(lambda: dve.tensor_tensor(out=w2[:], in0=a_im[:], in1=lim[:], op=ALU.mult), ["a", "lim"], ["w2"])
                    dv(lambda: dve.tensor_tensor(out=cf_re[:], in0=cf_re[:], in1=w2[:], op=ALU.add), ["cf_re", "w2"], ["cf_re"])
                    dv(lambda: dve.tensor_tensor(out=cf_re[:], in0=cf_re[:], in1=w1[:], op=ALU.mult), ["cf_re", "w1"], ["cf_re"])
                    dv(lambda: dve.tensor_tensor(out=cf_im[:], in0=a_im[:], in1=lre[:], op=ALU.mult), ["a", "lre"], ["cf_im"])
                    dv(lambda: dve.tensor_tensor(out=w2[:], in0=a_re[:], in1=lim[:], op=ALU.mult), ["a", "lim"], ["w2"])
                    dv(lambda: dve.tensor_tensor(out=cf_im[:], in0=cf_im[:], in1=w2[:], op=ALU.subtract), ["cf_im", "w2"], ["cf_im"])
                    dv(lambda: dve.tensor_tensor(out=cf_im[:], in0=cf_im[:], in1=w1[:], op=ALU.mult), ["cf_im", "w1"], ["cf_im"])
                    dv(lambda: dve.tensor_scalar(out=al128[:], in0=alpha[:], scalar1=128.0, scalar2=None, op0=ALU.mult), ["alpha"], ["al128"])
                    dv(lambda: dve.tensor_scalar(out=th128[:], in0=theta[:], scalar1=128.0, scalar2=None, op0=ALU.mult), ["theta"], ["th128"])
                    trig(t128_re, t128_im, th128, al128, NP2, 128, 1.0, ["th128", "al128"], ["t128"])
                    angp = sb2("angp", [128, S]); margp = sb2("margp", [128, S])
                    for q in range(NP2):
                        dv(lambda q=q: dve.tensor_scalar(out=angp[:, q * 128:(q + 1) * 128], in0=iota_r[:], scalar1=theta[:, q:q + 1],
                                                         scalar2=None, op0=ALU.mult), ["iota_r", "theta"], ["angm"])
                        P.op("pool", lambda q=q: pool.tensor_scalar(out=margp[:, q * 128:(q + 1) * 128], in0=iota_r[:], scalar1=alpha[:, q:q + 1],
                                                                    scalar2=None, op0=ALU.mult), reads=["iota_r", "alpha"], writes=["margm"])
                    trig(Tp_re, Tp_im, angp, margp, NP2 * 128, 128, 1.0, ["angm", "margm"], ["Tp"])
                    P.op("sp", lambda: sp.dma_start(out=_dap(prm_h, 0, [[1, 128], [128, NP2]]), in_=theta[:]), reads=["theta"], writes=["prm"], dma="scr")
                    P.op("sp", lambda: sp.dma_start(out=_dap(prm_h, S, [[1, 128], [128, NP2]]), in_=alpha[:]), reads=["alpha"], writes=["prm"], dma="scr")
                    angm, margm = angp, margp
                    P.op("sp", lambda: sp.dma_start(out=angm[:], in_=_dap(prm_h, 0, [[0, 128], [1, S]])), reads=["prm"], writes=["angm"], dma="ld")
                    P.op("sp", lambda: sp.dma_start(out=margm[:], in_=_dap(prm_h, S, [[0, 128], [1, S]])), reads=["prm"], writes=["margm"], dma="ld")
                    dv(lambda: dve.tensor_scalar(out=angm[:], in0=angm[:], scalar1=iota_c[:, 0:1], scalar2=None, op0=ALU.mult), ["angm", "iota_c"], ["angm"])
                    P.op("pool", lambda: pool.tensor_scalar(out=margm[:], in0=margm[:], scalar1=iota_c[:, 0:1], scalar2=None, op0=ALU.mult),
                         reads=["margm", "iota_c"], writes=["margm"])
                    trig(Tm_re, Tm_im, angm, margm, S, 128, -1.0, ["angm", "margm"], ["Tm"])
                    Bn_re = sb2("Bn_re", [128, NP2, 16]); Bn_im = sb2("Bn_im", [128, NP2, 16])
                    tA = sb2("tA", [128, NP2, 16]); tB = sb2("tB", [128, NP2, 16])
                    Bb_re = sb2("Bb_re", [128, NP2, 16]); Bb_im = sb2("Bb_im", [128, NP2, 16])
                    cdma(Bn_re[:], SS["ssm_b_re"].rearrange("(q p c) -> p q c", p=128, c=16), "Bn_re")
                    cdma(Bn_im[:], SS["ssm_b_im"].rearrange("(q p c) -> p q c", p=128, c=16), "Bn_im")
                    for (dst, x1_, c1_, x2_, c2_, op_) in ((Bb_re, Bn_re, cf_re, Bn_im, cf_im, ALU.subtract),
                                                           (Bb_im, Bn_im, cf_re, Bn_re, cf_im, ALU.add)):
                        for c in range(16):
                            dv(lambda c=c, x1_=x1_, c1_=c1_: dve.tensor_tensor(out=tA[:, :, c], in0=x1_[:, :, c], in1=c1_[:], op=ALU.mult),
                               ["Bn_re", "Bn_im", "cf_re", "cf_im"], ["tA"])
                            dv(lambda c=c, x2_=x2_, c2_=c2_: dve.tensor_tensor(out=tB[:, :, c], in0=x2_[:, :, c], in1=c2_[:], op=ALU.mult),
                               ["Bn_re", "Bn_im", "cf_re", "cf_im"], ["tB"])
                        dv(lambda dst=dst, op_=op_: dve.tensor_tensor(out=dst[:], in0=tA[:], in1=tB[:], op=op_), ["tA", "tB"], ["Bb"])
                    src_t = sb2("src_t", [128, 4, 2, 16])
                    for Bb, Bblk in ((Bb_re, Bblk_re), (Bb_im, Bblk_im)):
                        for ch in range(NUC):
                            for g2 in range(2):
                                dv(lambda Bb=Bb, ch=ch, g2=g2: dve.tensor_scalar(out=src_t[:, :, g2, :], in0=Bb[:, 4 * ch:4 * ch + 4, :],
                                                                                 scalar1=mask2[:, g2:g2 + 1], scalar2=None, op0=ALU.mult),
                                   ["Bb", "mask2"], ["src_t"])
                            P.op("pe", lambda: pe.transpose(out=pb[7][:, 0:128], in_=src_t[:].rearrange("p a b c -> p (a b c)"), identity=ident[:]),
                                 reads=["src_t", "ident"], writes=[("ps", 7)])
                            for q4 in range(4):
                                dv(lambda Bblk=Bblk, ch=ch, q4=q4: dve.tensor_scalar(out=Bblk[ch][:, q4 * 128:(q4 + 1) * 128], in0=pb[7][:, 0:128],
                                                                                     scalar1=mask3[:, q4:q4 + 1], scalar2=None, op0=ALU.mult),
                                   [("ps", 7), "mask3"], [("Bblk", ch)])
                    Cd = sb2("Cd", [128, 2, 64])
                    for name, outs in (("ssm_c_re", ((Cre, mask2), (nCre, nmask2))), ("ssm_c_im", ((nCim, nmask2),))):
                        cv = SS[name].rearrange("(c p n) -> c p n", p=128, n=64)
                        for ch in range(NUC):
                            cdma(Cd[:, 0, :], cv[ch], "Cd")
                            cdma(Cd[:, 1, :], cv[ch], "Cd")
                            P.op("pe", lambda: pe.transpose(out=pb[7][:, 0:128], in_=Cd[:].rearrange("p a b -> p (a b)"), identity=ident[:]),
                                 reads=["Cd", "ident"], writes=[("ps", 7)])
                            trv = pb[7][:, 0:128].rearrange("p (a b c) -> p a b c", a=4, b=2)
                            for dst, mk in outs:
                                for g2 in range(2):
                                    dv(lambda dst=dst, mk=mk, ch=ch, g2=g2, trv=trv: dve.tensor_scalar(
                                        out=dst[:, 4 * ch:4 * ch + 4, g2, :], in0=trv[:, :, g2, :], scalar1=mk[:, g2:g2 + 1],
                                        scalar2=None, op0=ALU.mult), [("ps", 7), "mask2", "nmask2"], ["Cw"])
                    P.barrier()

                uTc = [sbl(f"uTc{i}", [128, NUC, 512], BF16) for i in range(2)]
                yst = sbl("yst", [128, NUC, 512])
                dm = [[sbl(f"dm{i}_{j}", [128, 512], BF16) for j in range(4)] for i in range(2)]
                rm = [[sbl(f"rm{i}_{j}", [128, 512], BF16) for j in range(4)] for i in range(2)]
                tn = [sbl(f"tn{i}", [128, 4]) for i in range(4)]
                gt = [sbl(f"gt{i}", [128, 512]) for i in range(3)]
                gout = [sbl(f"gout{i}", [128, 512], BF16) for i in range(2)]
                gbase = cfg.get("group_base", 0)
                nbg = 0
                ngo = 0
                for sc in range(SEQ // 512):
                    ub = sc % 2
                    r, tl = divmod(sc * 512, ntok)
                    src = _dap(projG_h, r * PROJ_R + (2048 + gbase * 16) * ntok + tl, [[ntok, 128], [128 * ntok, NUC], [1, 512]])
                    P.op("sp", lambda ub=ub, src=src: sp.dma_start(out=uTc[ub][:, :, :], in_=src), reads=["projA"], writes=[("uTc", ub)], dma="ld")
                    for c4 in range(4):
                        c = sc * 4 + c4
                        ts = slice(c4 * 128, (c4 + 1) * 128)
                        for ch in range(NUC):
                            i2 = nbg % 2
                            nbg += 1
                            bre, bim = (0, 1) if i2 == 0 else (2, 3)
                            XR, XI, YB = 4, 5, 6
                            P.op("pe", lambda ub=ub, ch=ch, ts=ts, bre=bre: pe.matmul(pb[bre][:, :], lhsT=uTc[ub][:, ch, ts], rhs=Bblk_re[ch][:], start=True, stop=True),
                                 reads=[("uTc", ub), ("Bblk", ch)], writes=[("ps", bre)])
                            P.op("pe", lambda ub=ub, ch=ch, ts=ts, bim=bim: pe.matmul(pb[bim][:, :], lhsT=uTc[ub][:, ch, ts], rhs=Bblk_im[ch][:], start=True, stop=True),
                                 reads=[("uTc", ub), ("Bblk", ch)], writes=[("ps", bim)])
                            tsl = slice(ch * 512, (ch + 1) * 512)
                            A_, B_, C_, D_ = dm[i2]
                            for dst, tab, bsrc, k_ in ((A_, Tm_re, bre, 0), (B_, Tm_im, bim, 1), (C_, Tm_re, bim, 2), (D_, Tm_im, bre, 3)):
                                P.op("dve", lambda dst=dst, tab=tab, bsrc=bsrc, tsl=tsl: dve.tensor_tensor(out=dst[:], in0=pb[bsrc][:, :], in1=tab[:, tsl], op=ALU.mult),
                                     reads=[("ps", bsrc), "Tm"], writes=[("dm", i2, k_)])
                            for q4 in range(4):
                                q = 4 * ch + q4
                                cs = slice(q4 * 128, (q4 + 1) * 128)
                                for (xb, m1, k1, m2, k2, rhs2, injT, inm) in ((XR, A_, 0, B_, 1, ntri, injT_re, "injT_re"), (XI, C_, 2, D_, 3, tri, injT_im, "injT_im")):
                                    P.op("pe", lambda xb=xb, m1=m1, cs=cs: pe.matmul(pb[xb][:, cs], lhsT=m1[:, cs], rhs=tri[:], start=True, stop=False),
                                         reads=[("dm", i2, k1), "tri"], writes=[("ps", xb)])
                                    P.op("pe", lambda xb=xb, m2=m2, cs=cs, rhs2=rhs2, c=c: pe.matmul(pb[xb][:, cs], lhsT=m2[:, cs], rhs=rhs2[:], start=False, stop=(c == 0)),
                                         reads=[("dm", i2, k2), "tri", "ntri"], writes=[("ps", xb)])
                                    if c > 0:
                                        P.op("pe", lambda xb=xb, cs=cs, injT=injT, q=q: pe.matmul(pb[xb][:, cs], lhsT=injT[0:NP2, :], rhs=sel[0:NP2, q, :], start=False, stop=True),
                                             reads=[inm, "sel"], writes=[("ps", xb)])
                            E1, E2, E3, E4 = rm[i2]
                            tps = slice(4 * ch * 128, (4 * ch + 4) * 128)
                            for dst, tab, xsrc, k_ in ((E1, Tp_re, XR, 0), (E2, Tp_im, XI, 1), (E3, Tp_re, XI, 2), (E4, Tp_im, XR, 3)):
                                P.op("dve", lambda dst=dst, tab=tab, xsrc=xsrc, tps=tps: dve.tensor_tensor(out=dst[:], in0=pb[xsrc][:, :], in1=tab[:, tps], op=ALU.mult),
                                     reads=[("ps", xsrc), "Tp"], writes=[("rm", i2, k_)])
                            if c < SEQ // 128 - 1:
                                xr = pb[XR][:, 127:512:128]
                                xi = pb[XI][:, 127:512:128]
                                qs = slice(4 * ch, 4 * ch + 4)
                                P.op("dve", lambda xr=xr, qs=qs: dve.tensor_tensor(out=tn[0][:], in0=xr, in1=t128_re[:, qs], op=ALU.mult), reads=[("ps", XR), "t128"], writes=["tn0"])
                                P.op("dve", lambda xi=xi, qs=qs: dve.tensor_tensor(out=tn[1][:], in0=xi, in1=t128_im[:, qs], op=ALU.mult), reads=[("ps", XI), "t128"], writes=["tn1"])
                                P.op("dve", lambda xi=xi, qs=qs: dve.tensor_tensor(out=tn[2][:], in0=xi, in1=t128_re[:, qs], op=ALU.mult), reads=[("ps", XI), "t128"], writes=["tn2"])
                                P.op("dve", lambda xr=xr, qs=qs: dve.tensor_tensor(out=tn[3][:], in0=xr, in1=t128_im[:, qs], op=ALU.mult), reads=[("ps", XR), "t128"], writes=["tn3"])
                                P.op("dve", lambda qs=qs: dve.tensor_tensor(out=inj_re[:, qs], in0=tn[0][:], in1=tn[1][:], op=ALU.subtract), reads=["tn0", "tn1"], writes=["inj_re"])
                                P.op("dve", lambda qs=qs: dve.tensor_tensor(out=inj_im[:, qs], in0=tn[2][:], in1=tn[3][:], op=ALU.add), reads=["tn2", "tn3"], writes=["inj_im"])
                            for q4 in range(4):
                                q = 4 * ch + q4
                                cs = slice(q4 * 128, (q4 + 1) * 128)
                                yo = pb[YB][32 * q4:32 * q4 + 32, 0:128]
                                for wi, (wt, et, k_) in enumerate(((Cre, E1, 0), (nCre, E2, 1), (nCim, E3, 2), (nCim, E4, 3))):
                                    P.op("pe", lambda yo=yo, wt=wt, et=et, q=q, cs=cs, wi=wi, q4=q4: pe.matmul(
                                        yo, lhsT=wt[:, q, :, :].rearrange("p a b -> p (a b)"), rhs=et[:, cs], start=(wi == 0), stop=(wi == 3),
                                        tile_position=(0, 32 * q4)),
                                         reads=["Cw", ("rm", i2, k_)], writes=[("ps", YB)])
                            P.op("dve", lambda ub=ub, ch=ch, ts=ts: dve.scalar_tensor_tensor(
                                out=yst[:, ch, ts], in0=uTc[ub][:, ch, ts], scalar=d_col[:, ch:ch + 1], in1=pb[YB][:, 0:128], op0=ALU.mult, op1=ALU.add),
                                 reads=[("uTc", ub), "d_col", ("ps", YB)], writes=[("yst", ch)])
                        if c < SEQ // 128 - 1:
                            for src_, dstT, inm, bnk in ((inj_re, injT_re, "injT_re", 7), (inj_im, injT_im, "injT_im", 7)):
                                P.op("pe", lambda src_=src_: pe.transpose(out=pb[7][0:NP2, 0:128], in_=src_[:, :], identity=ident[:]),
                                     reads=["inj_re", "inj_im", "ident"], writes=[("ps", 7)])
                                P.op("act", lambda dstT=dstT: act.copy(out=dstT[0:NP2, :], in_=pb[7][0:NP2, 0:128]), reads=[("ps", 7)], writes=[inm])
                    for ch in range(NUC):
                        yv = yst[:, ch, :]
                        go = ngo % 2
                        ngo += 1
                        P.op("act", lambda yv=yv: act.activation(out=gt[0][:], in_=yv, func=AF.Square), reads=[("yst", ch)], writes=["gt0"])
                        P.op("pool", lambda: pool.tensor_scalar(out=gt[1][:], in0=gt[0][:], scalar1=0.044715, scalar2=1.0, op0=ALU.mult, op1=ALU.add),
                             reads=["gt0"], writes=["gt1"])
                        P.op("pool", lambda yv=yv: pool.tensor_tensor(out=gt[1][:], in0=gt[1][:], in1=yv, op=ALU.mult), reads=["gt1", ("yst", ch)], writes=["gt1"])
                        P.op("act", lambda: act.activation(out=gt[2][:], in_=gt[1][:], func=AF.Sigmoid, scale=1.5957691216057308), reads=["gt1"], writes=["gt2"])
                        P.op("pool", lambda yv=yv, go=go: pool.tensor_tensor(out=gout[go][:], in0=gt[2][:], in1=yv, op=ALU.mult),
                             reads=["gt2", ("yst", ch)], writes=[("gout", go)])
                        dst = _dap(mix_h, (HPC * 128 + ch * 128) * SEQ + sc * 512, [[SEQ, 128], [1, 512]])
                        P.op("sp", lambda go=go, dst=dst: sp.dma_start(out=dst, in_=gout[go][:]), reads=[("gout", go)], writes=["mix"], dma="scr")
                P.barrier()

        if "A" in stages:
            row_local_phase("A")
        if "B" in stages:
            attention_phase()
            if ssm_on:
                ssm_phase()
        if "C" in stages:
            row_local_phase("C")
        P.barrier()
    return nc


_CONSTS = None


def _t5_bucket(dist):
    dist = np.asarray(dist)
    max_exact = 16
    d_f = np.maximum(dist, max_exact).astype(np.float32)
    val = (np.log(d_f / np.float32(max_exact)) / np.float32(np.log(2048 / max_exact)) * np.float32(32 - max_exact))
    large = max_exact + (np.rint(val) if BUCKET_ROUND else val).astype(np.int32)
    large = np.minimum(large, 31)
    return np.where(dist < max_exact, dist, large)


def _consts():
    global _CONSTS
    if _CONSTS is None:
        oh = np.zeros((32, 3 * 129), np.float32)
        for pi, d in enumerate(DILS):
            b = _t5_bucket(np.arange(129) * d)
            oh[b, pi * 129 + np.arange(129)] = 1.0
        _CONSTS = {
            "ident": np.eye(128, dtype=np.float32),
            "ones_bf": np.ones((128, 128), dtype=ml_dtypes.bfloat16),
            "jrev_bf": np.eye(128, dtype=np.float32)[::-1].copy().astype(ml_dtypes.bfloat16),
            "onehot": oh,
            "iota_c": np.arange(128, dtype=np.float32).reshape(128, 1),
            "iota_r": np.tile(np.arange(128, dtype=np.float32), (128, 1)),
            "tri_bf": np.triu(np.ones((128, 128), np.float32)).astype(ml_dtypes.bfloat16),
            "ntri_bf": (-np.triu(np.ones((128, 128), np.float32))).astype(ml_dtypes.bfloat16),
            "mask2": (np.arange(128)[:, None] // 64 == np.arange(2)[None, :]).astype(np.float32),
            "nmask2": -(np.arange(128)[:, None] // 64 == np.arange(2)[None, :]).astype(np.float32),
            "mask3": (np.arange(128)[:, None] // 32 == np.arange(4)[None, :]).astype(np.float32),
            "sel": np.broadcast_to(np.eye(32, dtype=np.float32)[:, :, None], (32, 32, 128)).copy(),
        }
    return _CONSTS


PARAMS = ["ffn1_norm", "ffn1_w_gate", "ffn1_w_up", "ffn1_w_down", "ffn2_norm", "ffn2_w_gate", "ffn2_w_up",
          "ffn2_w_down", "mix_norm", "w_in", "q_norm", "k_norm", "glu_w", "glu_b", "w_out"]


def make_in_maps(inputs, ncores, npair=1):
    x = np.ascontiguousarray(inputs["x"], dtype=np.float32)
    base = dict(_consts())
    for n in PARAMS:
        base[n] = np.ascontiguousarray(inputs[n][0], dtype=np.float32)
    base["rel_bias"] = np.ascontiguousarray(inputs["rel_bias"], dtype=np.float32)
    ntok = SEQ // npair
    gpc = N_GROUPS // npair
    in_maps = []
    for c in range(ncores):
        b, r = divmod(c, npair)
        m = dict(base)
        m["x"] = x[b % BATCH, r * ntok:(r + 1) * ntok]
        gs = slice(r * gpc, (r + 1) * gpc)
        for n in ("ssm_lambda_re", "ssm_lambda_im", "ssm_log_dt", "ssm_b_re", "ssm_b_im", "ssm_c_re", "ssm_c_im"):
            m[n] = np.ascontiguousarray(inputs[n][0][gs], dtype=np.float32).reshape(-1)
        m["ssm_d"] = np.ascontiguousarray(inputs["ssm_d"][0][r * gpc * 16:(r + 1) * gpc * 16], dtype=np.float32)
        in_maps.append(m)
    return in_maps


def kernel(**inputs):
    cfg = dict(npair=1)
    ncores = 4
    nc = build(cfg)
    in_maps = make_in_maps(inputs, ncores, 1)
    res = run_bass_kernel_spmd(nc, in_maps, core_ids=list(range(ncores)))
    out = np.stack([np.asarray(res.results[c]["out"]) for c in range(BATCH)], axis=0)
    return out.astype(np.float32)
```

```python
from contextlib import ExitStack
import numpy as np
import ml_dtypes
import concourse.bass as bass
import concourse.mybir as mybir
from concourse.bass_utils import run_bass_kernel_spmd

F32 = mybir.dt.float32
BF16 = mybir.dt.bfloat16
ALU = mybir.AluOpType
AF = mybir.ActivationFunctionType

D = 2048
KD = D // 128
FF = 5632
FC = FF // 128
SEQ = 4096
BATCH = 4
EPS = 1e-6
TT = 1024
NTB = TT // 512
NPARTS = 11
FPP = FC // NPARTS
WSLOT = 128
WDW = 512
NST = 3


class Op:
    __slots__ = ("eng", "fn", "deps", "needed", "dma", "tok")

    def __init__(self, eng, fn, dma):
        self.eng = eng
        self.fn = fn
        self.deps = []
        self.needed = False
        self.dma = dma
        self.tok = None


class Prog:
    ENGS = ("pe", "act", "dve", "pool", "sp")
    LIMIT = 20000

    def __init__(self, nc, stack):
        self.nc = nc
        self.stack = stack
        self.eng = {"pe": nc.tensor, "act": nc.scalar, "dve": nc.vector,
                    "pool": nc.gpsimd, "sp": nc.sync}
        self.ops = []
        self.last_w = {}
        self.readers = {}
        self.cnt = {e: 0 for e in self.ENGS}
        self.esems = {e: [] for e in self.ENGS}
        self.streams = {}
        self.waited = {e: {} for e in self.ENGS}
        self.last_op = {}
        self.nsem = 0

    def _newsem(self, name):
        self.nsem += 1
        return self.stack.enter_context(self.nc.semaphore(f"{name}_{self.nsem}"))

    def stream(self, name, nsems, inc=16):
        self.streams[name] = dict(sems=[self._newsem(name) for _ in range(nsems)], n=0, inc=inc)

    def op(self, eng, fn, reads=(), writes=(), dma=None):
        o = Op(eng, fn, dma)
        deps = {}
        for r in reads:
            w = self.last_w.get(r)
            if w is not None:
                deps[id(w)] = w
        for r in writes:
            w = self.last_w.get(r)
            if w is not None:
                deps[id(w)] = w
            rd = self.readers.get(r)
            if rd:
                for x in rd.values():
                    deps[id(x)] = x
        for p in deps.values():
            if p is o:
                continue
            if p.dma is None and p.eng == "pe" and eng == "pe" and dma is None:
                continue
            p.needed = True
            o.deps.append(p)
        for r in writes:
            self.last_w[r] = o
            self.readers[r] = {}
        key = eng if dma is None else ("dma", dma, len(self.ops))
        for r in reads:
            self.readers.setdefault(r, {})[key if dma is None else id(o)] = o
        self.ops.append(o)
        self.last_op[eng] = o
        return o

    def _wait(self, eng, sem, val):
        k = id(sem)
        w = self.waited[eng]
        if w.get(k, 0) >= val:
            return
        w[k] = val
        self.eng[eng].wait_ge(sem, val)

    def flush(self):
        for o in self.ops:
            for p in o.deps:
                sem, val = p.tok
                self._wait(o.eng, sem, val)
            if o.dma is not None:
                st = self.streams[o.dma]
                n = st["n"]
                st["n"] = n + 1
                R = len(st["sems"])
                inc = st["inc"]
                sem = st["sems"][n % R]
                prev = inc * (n // R)
                if prev > 0:
                    self._wait(o.eng, sem, prev)
                ins = o.fn()
                if inc == 1:
                    ins.then_inc(sem)
                else:
                    ins.then_inc(sem, inc)
                o.tok = (sem, prev + inc)
            else:
                ins = o.fn()
                if o.needed:
                    c = self.cnt[o.eng]
                    ep, v = divmod(c, self.LIMIT)
                    sems = self.esems[o.eng]
                    if ep >= len(sems):
                        sems.append(self._newsem("e" + o.eng))
                    ins.then_inc(sems[ep], 1)
                    self.cnt[o.eng] = c + 1
                    o.tok = (sems[ep], v + 1)
        self.ops = []

    def barrier(self):
        lasts = []
        for e in self.ENGS:
            o = self.last_op.get(e)
            if o is not None and o.dma is None:
                o.needed = True
                lasts.append(o)
        self.flush()
        for e in self.ENGS:
            for o in lasts:
                if o.eng == e and e == "pe":
                    continue
                self._wait(e, *o.tok)
            for st in self.streams.values():
                R = len(st["sems"])
                for i, sem in enumerate(st["sems"]):
                    cnt = (st["n"] - i + R - 1) // R
                    if cnt > 0:
                        self._wait(e, sem, st["inc"] * cnt)
        self.last_w = {}
        self.readers = {}
        self.last_op = {}


NEG = -30000.0
BUCKET_ROUND = False
N_HEADS = 8
N_GROUPS = 64
DILS = (1, 4, 16)


def _dap(h, off, dims):
    return bass.AP(h, off, [list(d) for d in dims])


def build(cfg):
    npair = cfg.get("npair", 1)
    debug = cfg.get("debug", False)
    stages = cfg.get("stages", "ABC")
    ssm_on = cfg.get("ssm", True)
    ntok = SEQ // npair
    NT = ntok // TT
    HPC = N_HEADS // npair
    GPC = N_GROUPS // npair
    NP2 = GPC // 2
    NUC = GPC * 16 // 128
    MIXR = HPC * 128 + GPC * 16
    nc = bass.Bass("TRN2", target_bir_lowering=False)
    SCR = "ExternalOutput" if debug else "Internal"

    def dt(name, shape, dtype=F32, kind="ExternalInput"):
        return nc.dram_tensor(name, shape, dtype, kind=kind)

    x_d = dt("x", [ntok, D]).ap()
    out_d = dt("out", [ntok, D], kind="ExternalOutput").ap()
    ident_d = dt("ident", [128, 128]).ap()
    ones_d = dt("ones_bf", [128, 128], BF16).ap()
    jrev_d = dt("jrev_bf", [128, 128], BF16).ap()
    oh_d = dt("onehot", [32, 3 * 129]).ap()
    shapes = {"ffn1_norm": [D], "ffn1_w_gate": [D, FF], "ffn1_w_up": [D, FF], "ffn1_w_down": [FF, D],
              "ffn2_norm": [D], "ffn2_w_gate": [D, FF], "ffn2_w_up": [D, FF], "ffn2_w_down": [FF, D],
              "mix_norm": [D], "w_in": [D, 4096], "q_norm": [128], "k_norm": [128], "rel_bias": [32, N_HEADS // cfg.get("npair", 1)],
              "glu_w": [1024, 1024], "glu_b": [1024], "w_out": [D, D]}
    W = {n: dt(n, shapes[n]).ap() for n in shapes}
    sshapes = {"ssm_lambda_re": [GPC * 64], "ssm_lambda_im": [GPC * 64], "ssm_log_dt": [GPC], "ssm_b_re": [GPC * 1024],
               "ssm_b_im": [GPC * 1024], "ssm_c_re": [GPC * 1024], "ssm_c_im": [GPC * 1024], "ssm_d": [GPC * 16]}
    SS_h = {n: dt(n, sshapes[n]) for n in sshapes}
    SS = {n: h.ap() for n, h in SS_h.items()}
    iotac_d = dt("iota_c", [128, 1]).ap(); iotar_d = dt("iota_r", [128, 128]).ap()
    tri_d = dt("tri_bf", [128, 128], BF16).ap(); ntri_d = dt("ntri_bf", [128, 128], BF16).ap()
    mask2_d = dt("mask2", [128, 2]).ap(); nmask2_d = dt("nmask2", [128, 2]).ap(); mask3_d = dt("mask3", [128, 4]).ap()
    sel_d = dt("sel", [32, 32, 128]).ap()
    prm_h = dt("prm", [2 * GPC * 64], F32, kind=SCR)
    x1s_h = dt("x1s", [NT * 128 * KD * TT], F32, kind=SCR)
    own = {"q": dt("qA", [1024, ntok], BF16, kind=SCR), "k": dt("kA", [1024, ntok], BF16, kind=SCR),
           "u": dt("uA", [1024, ntok], BF16, kind=SCR), "v": dt("vA", [ntok, 1024], BF16, kind=SCR)} if npair == 1 else None
    mix_h = dt("mix", [MIXR, SEQ], BF16, kind=SCR) if npair == 1 else None
    zpad_h = dt("zpad", [8 * 3 * 384], F32, kind=SCR)
    bv = lambda h: h.ap().bitcast(BF16)
    if npair > 1:
        f32t = lambda name, rows, cols_bf: dt(name, [rows, cols_bf // 2], F32, kind="Internal")
        ownS = {kd: [f32t(f"{kd}A{j}", 512, ntok) for j in range(2)] for kd in ("q", "k", "u")}
        ownS["v"] = [f32t(f"vA{j}", ntok, 512) for j in range(2)]
        gatS = {kd: [f32t(f"{kd}G{j}", 1024, ntok) for j in range(2)] for kd in ("q", "k", "u")}
        gatS["v"] = [f32t(f"vG{j}", 2 * ntok, 512) for j in range(2)]
        stg = {kd: dt(f"{kd}S", [2 * 1024, ntok], BF16, kind="Internal") for kd in ("q", "k", "u")}
        stg["v"] = dt("vS", [2 * SEQ, 512], BF16, kind="Internal")
        mixA = [[f32t(f"mixA{j}{h}", 512, ntok) for h in range(2)] for j in range(2)]
        mixGt = [[f32t(f"mixG{j}{h}", 1024, ntok) for h in range(2)] for j in range(2)]
        stg_mix = dt("mixS", [2 * 2048, ntok], BF16, kind="Internal")
    RANK = (nc.partition_id() % 2) if npair > 1 else 0
    if npair == 1:
        mine = dict(own)
        mixM_h = mix_h
    else:
        mine = {"q": dt("qM", [HPC * 128, SEQ], BF16, kind="Internal"), "k": dt("kM", [HPC * 128, SEQ], BF16, kind="Internal"),
                "u": dt("uM", [GPC * 16, SEQ], BF16, kind="Internal"), "v": dt("vM", [SEQ, HPC * 128], BF16, kind="Internal")}
        mixM_h = dt("mixM", [2 * MIXR, ntok], BF16, kind="Internal")

    def rows_dyn(ap, static, size, mult):
        if npair == 1:
            return ap[static:static + size, :]
        return ap[static:static + mult + size, :][bass.ds(RANK * mult, size), :]

    def cols_dyn(ap, static, size, mult):
        if npair == 1:
            return ap[:, static:static + size]
        return ap[:, static:static + mult + size][:, bass.ds(RANK * mult, size)]

    with ExitStack() as stack:
        P = Prog(nc, stack)
        stack.enter_context(nc.allow_non_contiguous_dma(reason="tiny strided parameter loads"))
        mk_sb = lambda st, pfx='': (lambda name, shape, dtype=F32: st.enter_context(nc.sbuf_tensor(pfx + name, shape, dtype)))
        sb = mk_sb(stack)
        ps = lambda name, shape, dtype=F32: stack.enter_context(nc.psum_tensor(name, shape, dtype))
        sp, act, dve, pool, pe = nc.sync, nc.scalar, nc.vector, nc.gpsimd, nc.tensor

        ident = sb("ident_sb", [128, 128])
        ones = sb("ones_sb", [128, 128], BF16)
        jrev = sb("jrev_sb", [128, 128], BF16)
        g1 = sb("g1", [128, KD])
        g2 = sb("g2", [128, KD])
        g3 = sb("g3", [128, KD])
        gqk = sb("gqk", [128, 2])
        glub = sb("glub", [128, 8])
        pb = [ps(f"pb{i}", [128, 512]) for i in range(8)]
        PT, PG, PU, PD = (0, 1), (2, 3), (4, 5), (6, 7)

        P.stream("xin", 2)
        P.stream("const", 1)
        P.stream("wst", NST)
        P.stream("out", 2)
        P.stream("scr", 4)
        P.stream("ld", 4)
        P.stream("cc", 1, inc=1)
        PAIRS = [[0, 1], [2, 3], [4, 5], [6, 7]]

        def allgather(name, src_h, dst_h, rres, wres):
            P.op("pool", lambda: pool.collective_compute("AllGather", ALU.bypass, replica_groups=PAIRS,
                                                         ins=[src_h.ap().opt()], outs=[dst_h.ap().opt()]),
                 reads=[rres], writes=[wres], dma="cc")

        if npair == 1:
            _op = P.op

            def _op_alias(eng, fn, reads=(), writes=(), dma=None):
                al = lambda rs: [{"projG": "projA", "projM": "projA", "mixG": "mix", "mixM": "mix"}.get(x, x) if isinstance(x, str) else x for x in rs]
                return _op(eng, fn, reads=al(reads), writes=al(writes), dma=dma)
            P.op = _op_alias

        def cdma(dst, src, res):
            P.op("sp", lambda: sp.dma_start(out=dst, in_=src), writes=[res], dma="const")

        cdma(ident[:], ident_d, "ident")
        cdma(ones[:], ones_d, "ones")
        cdma(jrev[:], jrev_d, "jrev")
        cdma(g1[:], W["ffn1_norm"].rearrange("(k p) -> p k", p=128), "g1")
        cdma(g2[:], W["mix_norm"].rearrange("(k p) -> p k", p=128), "g2")
        cdma(g3[:], W["ffn2_norm"].rearrange("(k p) -> p k", p=128), "g3")
        cdma(gqk[:, 0:1], W["q_norm"].rearrange("(p o) -> p o", o=1), "gqk")
        cdma(gqk[:, 1:2], W["k_norm"].rearrange("(p o) -> p o", o=1), "gqk")
        cdma(glub[:], W["glu_b"].rearrange("(k p) -> p k", p=128), "glub")
        P.op("dve", lambda: dve.tensor_scalar(out=gqk[:, 0:1], in0=gqk[:, 0:1], scalar1=float(128 ** -0.5),
                                              scalar2=None, op0=ALU.mult), reads=["gqk"], writes=["gqk"])

        cnt = {"st": 0, "cp": 0, "xs": 0, "sq": 0, "wg": 0, "wd": 0, "pg": 0, "pd": 0, "pt": 0, "sg": 0,
               "qst": 0, "vst": 0}

        def evac_copy(out_ap, in_ap, reads, writes):
            cnt["cp"] += 1
            if cnt["cp"] % 2:
                P.op("act", lambda: act.copy(out=out_ap, in_=in_ap), reads=reads, writes=writes)
            else:
                P.op("dve", lambda: dve.tensor_copy(out=out_ap, in_=in_ap), reads=reads, writes=writes)

        def row_local_phase(phase):
            with ExitStack() as st:
                sbl = mk_sb(st, phase + "_")
                xT = sbl("xT", [128, KD, TT])
                hnT = sbl("hnT", [128, KD, TT], BF16)
                HT = sbl("HT", [128, FPP, TT], BF16)
                xs = [sbl(f"xs{i}", [128, D]) for i in range(2)]
                wg = [sbl(f"wg{i}", [128, KD, WSLOT], BF16) for i in range(2)]
                wu = [sbl(f"wu{i}", [128, KD, WSLOT], BF16) for i in range(2)]
                wd = [sbl(f"wd{i}", [128, FPP, WDW], BF16) for i in range(2)]
                stage = [sbl(f"stage{i}", [128, KD * WSLOT]) for i in range(NST)]
                sq = [sbl(f"sq{i}", [128, 512], BF16) for i in range(2)]
                sg = [sbl(f"sg{i}", [128, 512]) for i in range(2)]
                rstd = [sbl(f"rstd{i}", [128, 512]) for i in range(NTB)]
                qst = [sbl(f"qst{i}", [128, 512], BF16) for i in range(2)]
                vst = [sbl(f"vst{i}", [128, 4, 128], BF16) for i in range(2)]
                yTb = sbl("yTb", [128, 8, TT], BF16) if phase == "C" else None

                def load_xT(t0):
                    for s in range(TT // 128):
                        slot = cnt["xs"] % 2
                        cnt["xs"] += 1
                        src = x_d[t0 + s * 128:t0 + (s + 1) * 128, :]
                        P.op("sp", lambda slot=slot, src=src: sp.dma_start(out=xs[slot][:], in_=src),
                             writes=[("xs", slot)], dma="xin")
                        for k4 in range(KD // 4):
                            b = PT[cnt["pt"] % 2]
                            cnt["pt"] += 1
                            for j in range(4):
                                k = k4 * 4 + j
                                P.op("pe", lambda b=b, j=j, k=k, slot=slot: pe.transpose(
                                    out=pb[b][:, j * 128:(j + 1) * 128], in_=xs[slot][:, k * 128:(k + 1) * 128],
                                    identity=ident[:]), reads=[("xs", slot), "ident"], writes=[("ps", b)])
                            evac_copy(xT[:, k4 * 4:(k4 + 1) * 4, s * 128:(s + 1) * 128],
                                      pb[b][:, :].rearrange("p (j t) -> p j t", j=4),
                                      [("ps", b)], [("xT", k4 * 4 + j, s // 4) for j in range(4)])

                def store_xT(t0):
                    for s in range(TT // 128):
                        slot = cnt["xs"] % 2
                        cnt["xs"] += 1
                        for k4 in range(KD // 4):
                            b = PT[cnt["pt"] % 2]
                            cnt["pt"] += 1
                            for j in range(4):
                                k = k4 * 4 + j
                                P.op("pe", lambda b=b, j=j, k=k, s=s: pe.transpose(
                                    out=pb[b][:, j * 128:(j + 1) * 128], in_=xT[:, k, s * 128:(s + 1) * 128],
                                    identity=ident[:]), reads=[("xT", k, s // 4), "ident"], writes=[("ps", b)])
                            evac_copy(xs[slot][:, k4 * 512:(k4 + 1) * 512], pb[b][:, :],
                                      [("ps", b)], [("xs", slot)])
                        dst = out_d[t0 + s * 128:t0 + (s + 1) * 128, :]
                        P.op("sp", lambda slot=slot, dst=dst: sp.dma_start(out=dst, in_=xs[slot][:]),
                             reads=[("xs", slot)], dma="out")

                def rsqrt_inplace(t, res):
                    P.op("act", lambda: act.sqrt(out=t, in_=t), reads=[res], writes=[res])
                    P.op("dve", lambda: dve.reciprocal(out=t, in_=t), reads=[res], writes=[res])

                def rmsnorm(g, gname):
                    for tb in range(NTB):
                        b = PD[tb % 2]
                        tsl = slice(tb * 512, (tb + 1) * 512)
                        for k in range(KD):
                            q = cnt["sq"] % 2
                            cnt["sq"] += 1
                            P.op("act", lambda q=q, k=k, tsl=tsl: act.activation(out=sq[q][:], in_=xT[:, k, tsl], func=AF.Square),
                                 reads=[("xT", k, tb)], writes=[("sq", q)])
                            P.op("pe", lambda q=q, k=k, b=b: pe.matmul(pb[b][:, :], lhsT=ones[:], rhs=sq[q][:],
                                                                        start=(k == 0), stop=(k == KD - 1)),
                                 reads=[("sq", q), "ones"], writes=[("ps", b)])
                        P.op("dve", lambda b=b, tb=tb: dve.tensor_scalar(out=rstd[tb][:], in0=pb[b][:, :], scalar1=1.0 / D,
                                                                          scalar2=EPS, op0=ALU.mult, op1=ALU.add),
                             reads=[("ps", b)], writes=[("rstd", tb)])
                        rsqrt_inplace(rstd[tb][:], ("rstd", tb))
                        for k in range(KD):
                            P.op("dve", lambda k=k, tb=tb, tsl=tsl: dve.scalar_tensor_tensor(
                                out=hnT[:, k, tsl], in0=xT[:, k, tsl], scalar=g[:, k:k + 1], in1=rstd[tb][:],
                                op0=ALU.mult, op1=ALU.mult),
                                 reads=[("xT", k, tb), ("rstd", tb), gname], writes=[("hn", k, tb)])

                def wload(dst_ap, src_ap, wres):
                    st_ = cnt["st"] % NST
                    cnt["st"] += 1
                    shp = dst_ap.shape
                    sview = stage[st_][:, 0:shp[1] * shp[2]].rearrange("p (a b) -> p a b", a=shp[1])
                    P.op("sp", lambda: sp.dma_start(out=sview, in_=src_ap), writes=[("stage", st_)], dma="wst")
                    if cnt["st"] % 3 == 0:
                        P.op("act", lambda: act.copy(out=dst_ap, in_=sview), reads=[("stage", st_)], writes=[wres])
                    else:
                        P.op("pool", lambda: pool.tensor_copy(out=dst_ap, in_=sview), reads=[("stage", st_)], writes=[wres])

                def ffn(wgate, wup, wdown):
                    wg_v = wgate.rearrange("(k p) f -> p k f", p=128)
                    wu_v = wup.rearrange("(k p) f -> p k f", p=128)
                    wd_v = wdown.rearrange("(c p) d -> p c d", p=128)
                    for part in range(NPARTS):
                        for j in range(FPP):
                            f0 = (part * FPP + j) * 128
                            slot = cnt["wg"] % 2
                            cnt["wg"] += 1
                            wload(wg[slot][:, :, :], wg_v[:, :, f0:f0 + 128], ("wg", slot))
                            wload(wu[slot][:, :, :], wu_v[:, :, f0:f0 + 128], ("wu", slot))
                            for tb in range(NTB):
                                tsl = slice(tb * 512, (tb + 1) * 512)
                                i = cnt["pg"] % 2
                                cnt["pg"] += 1
                                bg, bu = PG[i], PU[i]
                                for k in range(KD):
                                    P.op("pe", lambda k=k, bg=bg, slot=slot, tsl=tsl: pe.matmul(
                                        pb[bg][:, :], lhsT=wg[slot][:, k, :], rhs=hnT[:, k, tsl],
                                        start=(k == 0), stop=(k == KD - 1)),
                                         reads=[("wg", slot), ("hn", k, tb)], writes=[("ps", bg)])
                                for k in range(KD):
                                    P.op("pe", lambda k=k, bu=bu, slot=slot, tsl=tsl: pe.matmul(
                                        pb[bu][:, :], lhsT=wu[slot][:, k, :], rhs=hnT[:, k, tsl],
                                        start=(k == 0), stop=(k == KD - 1)),
                                         reads=[("wu", slot), ("hn", k, tb)], writes=[("ps", bu)])
                                q = cnt["sg"] % 2
                                cnt["sg"] += 1
                                P.op("act", lambda q=q, bg=bg: act.activation(out=sg[q][:], in_=pb[bg][:, :], func=AF.Silu),
                                     reads=[("ps", bg)], writes=[("sg", q)])
                                P.op("dve", lambda q=q, bu=bu, j=j, tsl=tsl: dve.tensor_tensor(
                                    out=HT[:, j, tsl], in0=sg[q][:], in1=pb[bu][:, :], op=ALU.mult),
                                     reads=[("sg", q), ("ps", bu)], writes=[("H", j, tb)])
                        for dg in range(D // WDW):
                            slot = cnt["wd"] % 2
                            cnt["wd"] += 1
                            wload(wd[slot][:, :, :], wd_v[:, part * FPP:(part + 1) * FPP, dg * WDW:(dg + 1) * WDW], ("wd", slot))
                            for di in range(WDW // 128):
                                dc = dg * (WDW // 128) + di
                                for tb in range(NTB):
                                    tsl = slice(tb * 512, (tb + 1) * 512)
                                    b = PD[cnt["pd"] % 2]
                                    cnt["pd"] += 1
                                    for j in range(FPP):
                                        P.op("pe", lambda j=j, b=b, slot=slot, di=di, tsl=tsl: pe.matmul(
                                            pb[b][:, :], lhsT=wd[slot][:, j, di * 128:(di + 1) * 128], rhs=HT[:, j, tsl],
                                            start=(j == 0), stop=(j == FPP - 1)),
                                             reads=[("wd", slot), ("H", j, tb)], writes=[("ps", b)])
                                    P.op("dve", lambda b=b, dc=dc, tsl=tsl: dve.scalar_tensor_tensor(
                                        out=xT[:, dc, tsl], in0=pb[b][:, :], scalar=0.5, in1=xT[:, dc, tsl],
                                        op0=ALU.mult, op1=ALU.add),
                                         reads=[("ps", b), ("xT", dc, tb)], writes=[("xT", dc, tb)])

                all_xT = [("xT", k, tb) for k in range(KD) for tb in range(NTB)]

                def proj(t0):
                    rmsnorm(g2, "g2")
                    win_v = W["w_in"].rearrange("(k p) f -> p k f", p=128)
                    for oc in range(32):
                        slot = cnt["wg"] % 2
                        cnt["wg"] += 1
                        wload(wg[slot][:, :, :], win_v[:, :, oc * 128:(oc + 1) * 128], ("wg", slot))
                        if 16 <= oc < 24:
                            for s4 in range(TT // 512):
                                b = PD[cnt["pd"] % 2]
                                cnt["pd"] += 1
                                for si in range(4):
                                    s = s4 * 4 + si
                                    for k in range(KD):
                                        P.op("pe", lambda k=k, b=b, si=si, s=s, slot=slot: pe.matmul(
                                            pb[b][:, si * 128:(si + 1) * 128], lhsT=hnT[:, k, s * 128:(s + 1) * 128],
                                            rhs=wg[slot][:, k, :], start=(k == 0), stop=(k == KD - 1)),
                                             reads=[("wg", slot), ("hn", k, s // 4)], writes=[("ps", b)])
                                vq = cnt["vst"] % 2
                                cnt["vst"] += 1
                                evac_copy(vst[vq][:, :, :], pb[b][:, :].rearrange("p (j t) -> p j t", j=4),
                                          [("ps", b)], [("vst", vq)])
                                if npair == 1:
                                    dst = _dap(own["v"], (t0 + s4 * 512) * 1024 + (oc - 16) * 128,
                                               [[1024, 128], [128 * 1024, 4], [1, 128]])
                                else:
                                    hj, hl_ = divmod(oc - 16, HPC)
                                    dst = bv(ownS["v"][hj])[t0 + s4 * 512:t0 + (s4 + 1) * 512, hl_ * 128:(hl_ + 1) * 128] \
                                        .rearrange("(s p) e -> p s e", p=128)
                                P.op("sp", lambda vq=vq, dst=dst: sp.dma_start(out=dst, in_=vst[vq][:, :, :]),
                                     reads=[("vst", vq)], writes=["projA"], dma="scr")
                            continue
                        for tb in range(NTB):
                            tsl = slice(tb * 512, (tb + 1) * 512)
                            i = cnt["pg"] % 2
                            cnt["pg"] += 1
                            bg, bn = PG[i], PU[i]
                            for k in range(KD):
                                P.op("pe", lambda k=k, bg=bg, slot=slot, tsl=tsl: pe.matmul(
                                    pb[bg][:, :], lhsT=wg[slot][:, k, :], rhs=hnT[:, k, tsl],
                                    start=(k == 0), stop=(k == KD - 1)),
                                     reads=[("wg", slot), ("hn", k, tb)], writes=[("ps", bg)])
                            qq = cnt["qst"] % 2
                            cnt["qst"] += 1
                            if oc < 16:
                                q = cnt["sq"] % 2
                                cnt["sq"] += 1
                                r = cnt["sg"] % 2
                                cnt["sg"] += 1
                                P.op("act", lambda q=q, bg=bg: act.activation(out=sq[q][:], in_=pb[bg][:, :], func=AF.Square),
                                     reads=[("ps", bg)], writes=[("sq", q)])
                                P.op("pe", lambda q=q, bn=bn: pe.matmul(pb[bn][:, :], lhsT=ones[:], rhs=sq[q][:], start=True, stop=True),
                                     reads=[("sq", q), "ones"], writes=[("ps", bn)])
                                P.op("dve", lambda r=r, bn=bn: dve.tensor_scalar(out=sg[r][:], in0=pb[bn][:, :], scalar1=1.0 / 128,
                                                                                  scalar2=EPS, op0=ALU.mult, op1=ALU.add),
                                     reads=[("ps", bn)], writes=[("sg", r)])
                                rsqrt_inplace(sg[r][:], ("sg", r))
                                col = 0 if oc < 8 else 1
                                P.op("dve", lambda r=r, bg=bg, qq=qq, col=col: dve.scalar_tensor_tensor(
                                    out=qst[qq][:], in0=pb[bg][:, :], scalar=gqk[:, col:col + 1], in1=sg[r][:],
                                    op0=ALU.mult, op1=ALU.mult),
                                     reads=[("ps", bg), ("sg", r), "gqk"], writes=[("qst", qq)])
                                row0 = (oc % 8) * 128
                                dh = (own["q"] if oc < 8 else own["k"]) if npair == 1 else None
                            else:
                                evac_copy(qst[qq][:], pb[bg][:, :], [("ps", bg)], [("qst", qq)])
                                row0 = (oc - 24) * 128
                                dh = own["u"] if npair == 1 else None
                            if npair == 1:
                                dst = dh.ap()[row0:row0 + 128, t0 + tb * 512:t0 + (tb + 1) * 512]
                            else:
                                kd_ = "q" if oc < 8 else ("k" if oc < 16 else "u")
                                hj, r0_ = divmod(row0, 512)
                                dst = bv(ownS[kd_][hj])[r0_:r0_ + 128, t0 + tb * 512:t0 + (tb + 1) * 512]
                            P.op("sp", lambda qq=qq, dst=dst: sp.dma_start(out=dst, in_=qst[qq][:]),
                                 reads=[("qst", qq)], writes=["projA"], dma="scr")

                def mix_out(t0, tg0):
                    for h in range(N_HEADS):
                        r, hl = divmod(h, HPC)
                        row0 = r * MIXR + hl * 128
                        P.op("sp", lambda h=h, row0=row0: sp.dma_start(
                            out=hnT[:, h, :], in_=mixM_h.ap()[row0:row0 + 128, tg0:tg0 + TT]),
                             reads=["mixM"], writes=[("hn", h, tb) for tb in range(NTB)], dma="ld")
                    for c in range(8):
                        r, cl = divmod(c, NUC)
                        row0 = r * MIXR + HPC * 128 + cl * 128
                        P.op("sp", lambda c=c, row0=row0: sp.dma_start(
                            out=yTb[:, c, :], in_=mixM_h.ap()[row0:row0 + 128, tg0:tg0 + TT]),
                             reads=["mixM"], writes=[("yT", c)], dma="ld")
                    gw_v = W["glu_w"].rearrange("(k p) f -> p k f", p=128)
                    for c2 in range(8):
                        slot = cnt["wg"] % 2
                        cnt["wg"] += 1
                        wload(wg[slot][:, 0:8, :], gw_v[:, :, c2 * 128:(c2 + 1) * 128], ("wg", slot))
                        for tb in range(NTB):
                            tsl = slice(tb * 512, (tb + 1) * 512)
                            bg = PG[cnt["pg"] % 2]
                            cnt["pg"] += 1
                            for c in range(8):
                                P.op("pe", lambda c=c, bg=bg, slot=slot, tsl=tsl: pe.matmul(
                                    pb[bg][:, :], lhsT=wg[slot][:, c, :], rhs=yTb[:, c, tsl], start=(c == 0), stop=(c == 7)),
                                     reads=[("wg", slot), ("yT", c)], writes=[("ps", bg)])
                            q = cnt["sg"] % 2
                            cnt["sg"] += 1
                            P.op("act", lambda q=q, bg=bg, c2=c2: act.activation(out=sg[q][:], in_=pb[bg][:, :], func=AF.Sigmoid,
                                                                                  bias=glub[:, c2:c2 + 1]),
                                 reads=[("ps", bg), "glub"], writes=[("sg", q)])
                            P.op("dve", lambda q=q, c2=c2, tsl=tsl: dve.tensor_tensor(
                                out=hnT[:, 8 + c2, tsl], in0=sg[q][:], in1=yTb[:, c2, tsl], op=ALU.mult),
                                 reads=[("sg", q), ("yT", c2)], writes=[("hn", 8 + c2, tb)])
                    wo_v = W["w_out"].rearrange("(k p) f -> p k f", p=128)
                    for dc in range(KD):
                        slot = cnt["wg"] % 2
                        cnt["wg"] += 1
                        wload(wg[slot][:, :, :], wo_v[:, :, dc * 128:(dc + 1) * 128], ("wg", slot))
                        for tb in range(NTB):
                            tsl = slice(tb * 512, (tb + 1) * 512)
                            b = PD[cnt["pd"] % 2]
                            cnt["pd"] += 1
                            for k in range(KD):
                                P.op("pe", lambda k=k, b=b, slot=slot, tsl=tsl: pe.matmul(
                                    pb[b][:, :], lhsT=wg[slot][:, k, :], rhs=hnT[:, k, tsl], start=(k == 0), stop=(k == KD - 1)),
                                     reads=[("wg", slot), ("hn", k, tb)], writes=[("ps", b)])
                            P.op("dve", lambda b=b, dc=dc, tsl=tsl: dve.scalar_tensor_tensor(
                                out=xT[:, dc, tsl], in0=pb[b][:, :], scalar=1.0, in1=xT[:, dc, tsl],
                                op0=ALU.mult, op1=ALU.add),
                                 reads=[("ps", b), ("xT", dc, tb)], writes=[("xT", dc, tb)])

                for tt in range(NT):
                    t0 = tt * TT
                    x1v = _dap(x1s_h, tt * 128 * KD * TT, [[KD * TT, 128], [TT, KD], [1, TT]])
                    if phase == "A":
                        load_xT(t0)
                        rmsnorm(g1, "g1")
                        ffn(W["ffn1_w_gate"], W["ffn1_w_up"], W["ffn1_w_down"])
                        P.op("sp", lambda x1v=x1v: sp.dma_start(out=x1v, in_=xT[:, :, :]), reads=all_xT, writes=["x1s"], dma="scr")
                        proj(t0)
                    else:
                        P.op("sp", lambda x1v=x1v: sp.dma_start(out=xT[:, :, :], in_=x1v), reads=["x1s"], writes=all_xT, dma="ld")
                        mix_out(t0, t0)
                        rmsnorm(g3, "g3")
                        ffn(W["ffn2_w_gate"], W["ffn2_w_up"], W["ffn2_w_down"])
                        store_xT(t0)
                P.barrier()

        def attention_phase():
            with ExitStack() as st:
                sbl = mk_sb(st, "B_")
                qTh = sbl("qTh", [128, SEQ], BF16)
                kTh = sbl("kTh", [128, SEQ], BF16)
                vb = [sbl(f"vb{i}", [128, 32, 128], BF16) for i in range(3)]
                acc = sbl("acc", [128, 2, SEQ])
                rden = sbl("rden", [128, SEQ])
                outst = sbl("outst", [128, SEQ], BF16)
                Bt = [[sbl(f"Bt{pi}_{hl}", [128, 256], BF16) for hl in range(HPC)] for pi in range(3)]
                Hf = [sbl(f"Hf{i}", [128, 256]) for i in range(2)]
                pT = [sbl(f"pT{i}", [128, 256], BF16) for i in range(2)]
                rb = sbl("rb", [32, 8])
                oh = sbl("oh", [32, 3 * 129])
                zt = sbl("zt", [8, 3, 384])
                cdma(rb[:, 0:HPC], W["rel_bias"], "rb")
                cdma(oh[:], oh_d, "oh")
                P.op("pe", lambda: pe.matmul(pb[0][0:HPC, 0:387], lhsT=rb[:, 0:HPC], rhs=oh[:, :], start=True, stop=True),
                     reads=["rb", "oh"], writes=[("ps", 0)])
                P.op("pool", lambda: pool.memset(zt[:], NEG), writes=["zt"])
                P.op("dve", lambda: dve.tensor_copy(out=zt[0:HPC, :, 127:256], in_=pb[0][0:HPC, 0:387].rearrange("p (a b) -> p a b", a=3)),
                     reads=[("ps", 0)], writes=["zt"])
                P.op("sp", lambda: sp.dma_start(out=_dap(zpad_h, 0, [[3 * 384, 8], [384, 3], [1, 384]]), in_=zt[:]),
                     reads=["zt"], writes=["zpad"], dma="scr")
                hbase = 0
                for hl in range(HPC):
                    for pi in range(3):
                        i = (hl * 3 + pi) % 2
                        src = _dap(zpad_h, ((hbase + hl) * 3 + pi) * 384, [[1, 128], [1, 256]])
                        P.op("sp", lambda i=i, src=src: sp.dma_start(out=Hf[i][:], in_=src), reads=["zpad"],
                             writes=[("Hf", i)], dma="ld")
                        P.op("act", lambda i=i, hl=hl, pi=pi: act.copy(out=Bt[pi][hl][:], in_=Hf[i][:]),
                             reads=[("Hf", i)], writes=[("Bt", pi, hl)])
                nblk = 0
                for hl in range(HPC):
                    hg = hbase + hl
                    P.op("sp", lambda hl=hl: sp.dma_start(out=qTh[:, :], in_=mine["q"].ap()[hl * 128:(hl + 1) * 128, :]),
                         reads=["projM"], writes=["qTh"], dma="ld")
                    P.op("sp", lambda hl=hl: sp.dma_start(out=kTh[:, :], in_=mine["k"].ap()[hl * 128:(hl + 1) * 128, :]),
                         reads=["projM"], writes=["kTh"], dma="ld")
                    for pi, d in enumerate(DILS):
                        nm = 32 // d
                        for res in range(d):
                            for m0 in range(0, nm, 8):
                                mm = min(8, nm - m0)
                                t_0 = m0 * 128 * d + res
                                bi = res * nm + m0
                                P.op("sp", lambda pi=pi, bi=bi, mm=mm, t_0=t_0, d=d, hl=hl: sp.dma_start(
                                    out=vb[pi][:, bi:bi + mm, :],
                                    in_=mine["v"].ap()[t_0:t_0 + (mm * 128 - 1) * d + 1:d, hl * 128:(hl + 1) * 128]
                                    .rearrange("(m j) e -> j m e", j=128)),
                                     reads=["projM"], writes=[("vb", pi)], dma="ld")
                    for pi, d in enumerate(DILS):
                        nm = 32 // d
                        for res in range(d):
                            for n in range(nm):
                                off = n * 128 * d + res
                                qa = qTh[:, off:off + 127 * d + 1:d]
                                ka = kTh[:, off:off + 127 * d + 1:d]
                                bS = PT[nblk % 2]
                                bO = PG[nblk % 2]
                                ip = nblk % 2
                                nblk += 1
                                Wd_ = 256 if n > 0 else 128
                                P.op("pe", lambda bS=bS, pi=pi, hl=hl, Wd_=Wd_: pe.matmul(
                                    pb[bS][:, 0:Wd_], lhsT=jrev[:], rhs=Bt[pi][hl][:, 0:Wd_], start=True, stop=False),
                                     reads=["jrev", ("Bt", pi, hl)], writes=[("ps", bS)])
                                P.op("pe", lambda bS=bS, ka=ka, qa=qa, n=n: pe.matmul(
                                    pb[bS][:, 0:128], lhsT=ka, rhs=qa, start=False, stop=(n == 0)),
                                     reads=["qTh", "kTh"], writes=[("ps", bS)])
                                if n > 0:
                                    offp = off - 128 * d
                                    kp = kTh[:, offp:offp + 127 * d + 1:d]
                                    P.op("pe", lambda bS=bS, kp=kp, qa=qa: pe.matmul(
                                        pb[bS][:, 128:256], lhsT=kp, rhs=qa, start=False, stop=True),
                                         reads=["qTh", "kTh"], writes=[("ps", bS)])
                                P.op("act", lambda ip=ip, bS=bS, Wd_=Wd_: act.activation(
                                    out=pT[ip][:, 0:Wd_], in_=pb[bS][:, 0:Wd_], func=AF.Exp),
                                     reads=[("ps", bS)], writes=[("pT", ip)])
                                bi = res * nm + n
                                P.op("pe", lambda bO=bO, pi=pi, bi=bi, ip=ip, n=n: pe.matmul(
                                    pb[bO][:, 0:128], lhsT=vb[pi][:, bi, :], rhs=pT[ip][:, 0:128], start=True, stop=(n == 0)),
                                     reads=[("vb", pi), ("pT", ip)], writes=[("ps", bO)])
                                if n > 0:
                                    P.op("pe", lambda bO=bO, pi=pi, bi=bi, ip=ip: pe.matmul(
                                        pb[bO][:, 0:128], lhsT=vb[pi][:, bi - 1, :], rhs=pT[ip][:, 128:256], start=False, stop=True),
                                         reads=[("vb", pi), ("pT", ip)], writes=[("ps", bO)])
                                P.op("pe", lambda bO=bO, ip=ip, n=n: pe.matmul(
                                    pb[bO][:, 128:256], lhsT=ones[:], rhs=pT[ip][:, 0:128], start=True, stop=(n == 0)),
                                     reads=["ones", ("pT", ip)], writes=[("ps", bO)])
                                if n > 0:
                                    P.op("pe", lambda bO=bO, ip=ip: pe.matmul(
                                        pb[bO][:, 128:256], lhsT=ones[:], rhs=pT[ip][:, 128:256], start=False, stop=True),
                                         reads=["ones", ("pT", ip)], writes=[("ps", bO)])
                                av = acc[:, :, off:off + 127 * d + 1:d]
                                pv = pb[bO][:, 0:256].rearrange("p (a b) -> p a b", a=2)
                                if pi == 0:
                                    P.op("dve", lambda av=av, pv=pv: dve.tensor_copy(out=av, in_=pv),
                                         reads=[("ps", bO)], writes=["acc"])
                                else:
                                    P.op("dve", lambda av=av, pv=pv: dve.tensor_tensor(out=av, in0=pv, in1=av, op=ALU.add),
                                         reads=[("ps", bO), "acc"], writes=["acc"])
                    P.op("dve", lambda: dve.reciprocal(out=rden[:], in_=acc[:, 1, :]), reads=["acc"], writes=["rden"])
                    P.op("pool", lambda: pool.tensor_tensor(out=outst[:], in0=acc[:, 0, :], in1=rden[:], op=ALU.mult),
                         reads=["acc", "rden"], writes=["outst"])
                    if npair == 1:
                        P.op("sp", lambda hl=hl: sp.dma_start(out=mix_h.ap()[hl * 128:(hl + 1) * 128, :], in_=outst[:]),
                             reads=["outst"], writes=["mix"], dma="scr")
                    else:
                        for j in range(2):
                            P.op("sp", lambda hl=hl, j=j: sp.dma_start(out=bv(mixA[j][0])[hl * 128:(hl + 1) * 128, :],
                                                                       in_=outst[:, j * ntok:(j + 1) * ntok]),
                                 reads=["outst"], writes=["mix"], dma="scr")
                P.barrier()


        def ssm_phase():
            S = GPC * 64
            TWO_PI = float(2 * np.pi)
            with ExitStack() as st:
                sbl = mk_sb(st, "S_")
                Tm_re = sbl("Tm_re", [128, S]); Tm_im = sbl("Tm_im", [128, S])
                Tp_re = sbl("Tp_re", [128, NP2 * 128]); Tp_im = sbl("Tp_im", [128, NP2 * 128])
                t128_re = sbl("t128_re", [128, NP2]); t128_im = sbl("t128_im", [128, NP2])
                Bblk_re = [sbl(f"Bblk_re{i}", [128, 512], BF16) for i in range(NUC)]
                Bblk_im = [sbl(f"Bblk_im{i}", [128, 512], BF16) for i in range(NUC)]
                Cre = sbl("Cre", [128, NP2, 2, 16], BF16); nCre = sbl("nCre", [128, NP2, 2, 16], BF16)
                nCim = sbl("nCim", [128, NP2, 2, 16], BF16)
                d_col = sbl("d_col", [128, NUC])
                iota_c = sbl("iota_c", [128, 1]); iota_r = sbl("iota_r", [128, 128])
                tri = sbl("tri", [128, 128], BF16); ntri = sbl("ntri", [128, 128], BF16)
                mask2 = sbl("mask2", [128, 2]); nmask2 = sbl("nmask2", [128, 2]); mask3 = sbl("mask3", [128, 4])
                sel = sbl("sel", [32, 32, 128])
                inj_re = sbl("inj_re", [128, NP2]); inj_im = sbl("inj_im", [128, NP2])
                injT_re = sbl("injT_re", [32, 128]); injT_im = sbl("injT_im", [32, 128])
                for t_, d_, r_ in ((iota_c, iotac_d, "iota_c"), (iota_r, iotar_d, "iota_r"), (tri, tri_d, "tri"),
                                   (ntri, ntri_d, "ntri"), (mask2, mask2_d, "mask2"), (nmask2, nmask2_d, "nmask2"),
                                   (mask3, mask3_d, "mask3"), (sel, sel_d, "sel")):
                    cdma(t_[:], d_, r_)
                cdma(d_col[:], SS["ssm_d"].rearrange("(c p) -> p c", p=128), "d_col")

                with ExitStack() as st2:
                    sb2 = mk_sb(st2, "S2_")
                    BLK = 1024
                    tmp = {n: sb2("tg_" + n, [128, BLK]) for n in ("t", "fr", "m", "cosv", "sinv", "mag")}
                    tint = sb2("tg_int", [128, BLK], mybir.dt.int32)

                    def dv(fn, reads, writes):
                        P.op("dve", fn, reads=reads, writes=writes)

                    def trig(out_re, out_im, ang, marg, n, np_, sign, rres, wres):
                        for c0 in range(0, n, BLK):
                            w = min(BLK, n - c0)
                            cs = slice(c0, c0 + w)
                            T = {k: v[0:np_, 0:w] for k, v in tmp.items()}
                            ti = tint[0:np_, 0:w]
                            dv(lambda T=T, cs=cs: dve.tensor_scalar(out=T["t"], in0=ang[:, cs], scalar1=1.0 / TWO_PI, scalar2=None, op0=ALU.mult),
                               rres, ["tg_t"])
                            for name, shift in (("cosv", 0.25), ("sinv", 0.0)):
                                dv(lambda T=T, shift=shift: dve.tensor_scalar(out=T["fr"], in0=T["t"], scalar1=shift, scalar2=None, op0=ALU.add),
                                   ["tg_t"], ["tg_fr"])
                                dv(lambda T=T, ti=ti: dve.tensor_copy(out=ti, in_=T["fr"]), ["tg_fr"], ["tg_i"])
                                dv(lambda T=T, ti=ti: dve.tensor_copy(out=T["m"], in_=ti), ["tg_i"], ["tg_m"])
                                dv(lambda T=T: dve.tensor_tensor(out=T["fr"], in0=T["fr"], in1=T["m"], op=ALU.subtract), ["tg_fr", "tg_m"], ["tg_fr"])
                                dv(lambda T=T: dve.tensor_scalar(out=T["m"], in0=T["fr"], scalar1=0.5, scalar2=None, op0=ALU.is_gt), ["tg_fr"], ["tg_m"])
                                dv(lambda T=T: dve.tensor_tensor(out=T["fr"], in0=T["fr"], in1=T["m"], op=ALU.subtract), ["tg_fr", "tg_m"], ["tg_fr"])
                                dv(lambda T=T: dve.tensor_scalar(out=T["m"], in0=T["fr"], scalar1=-0.5, scalar2=None, op0=ALU.is_lt), ["tg_fr"], ["tg_m"])
                                dv(lambda T=T: dve.tensor_tensor(out=T["fr"], in0=T["fr"], in1=T["m"], op=ALU.add), ["tg_fr", "tg_m"], ["tg_fr"])
                                P.op("act", lambda T=T, name=name: act.activation(out=T[name], in_=T["fr"], func=AF.Sin, scale=TWO_PI),
                                     reads=["tg_fr"], writes=["tg_" + name])
                            P.op("act", lambda T=T, cs=cs: act.activation(out=T["mag"], in_=marg[:, cs], func=AF.Exp, scale=float(sign)),
                                 reads=rres, writes=["tg_mag"])
                            dv(lambda T=T, cs=cs: dve.tensor_tensor(out=out_re[:, cs], in0=T["mag"], in1=T["cosv"], op=ALU.mult),
                               ["tg_mag", "tg_cosv"], wres)
                            dv(lambda T=T, cs=cs: dve.scalar_tensor_tensor(out=out_im[:, cs], in0=T["mag"], scalar=float(sign), in1=T["sinv"],
                                                                           op0=ALU.mult, op1=ALU.mult),
                               ["tg_mag", "tg_sinv"], wres)

                    col = lambda n: sb2(n, [128, NP2])
                    lre, lim, ldt, alpha, theta = col("lre"), col("lim"), col("ldt"), col("alpha"), col("theta")
                    a_re, a_im, cf_re, cf_im, w1, w2 = col("a_re"), col("a_im"), col("cf_re"), col("cf_im"), col("w1"), col("w2")
                    al128, th128 = col("al128"), col("th128")
                    cdma(lre[:], SS["ssm_lambda_re"].rearrange("(q p) -> p q", p=128), "lre")
                    cdma(lim[:], SS["ssm_lambda_im"].rearrange("(q p) -> p q", p=128), "lim")
                    ldt_h = SS_h["ssm_log_dt"]
                    cdma(ldt[0:64, :], _dap(ldt_h, 0, [[0, 64], [2, NP2]]), "ldt")
                    cdma(ldt[64:128, :], _dap(ldt_h, 1, [[0, 64], [2, NP2]]), "ldt")
                    P.op("act", lambda: act.activation(out=ldt[:], in_=ldt[:], func=AF.Exp), reads=["ldt"], writes=["ldt"])
                    dv(lambda: dve.tensor_tensor(out=alpha[:], in0=lre[:], in1=ldt[:], op=ALU.mult), ["lre", "ldt"], ["alpha"])
                    dv(lambda: dve.tensor_tensor(out=theta[:], in0=lim[:], in1=ldt[:], op=ALU.mult), ["lim", "ldt"], ["theta"])
                    trig(a_re, a_im, theta, alpha, NP2, 128, 1.0, ["theta", "alpha"], ["a"])
                    dv(lambda: dve.tensor_scalar(out=a_re[:], in0=a_re[:], scalar1=-1.0, scalar2=None, op0=ALU.add), ["a"], ["a"])
                    dv(lambda: dve.tensor_tensor(out=w1[:], in0=lre[:], in1=lre[:], op=ALU.mult), ["lre"], ["w1"])
                    dv(lambda: dve.tensor_tensor(out=w2[:], in0=lim[:], in1=lim[:], op=ALU.mult), ["lim"], ["w2"])
                    dv(lambda: dve.tensor_tensor(out=w1[:], in0=w1[:], in1=w2[:], op=ALU.add), ["w1", "w2"], ["w1"])
                    dv(lambda: dve.reciprocal(out=w1[:], in_=w1[:]), ["w1"], ["w1"])
                    dv(lambda: dve.tensor_tensor(out=cf_re[:], in0=a_re[:], in1=lre[:], op=ALU.mult), ["a", "lre"], ["cf_re"])
                    dv(lambda: dve.tensor_tensor(out=w2[:], in0=a_im[:], in1=lim[:], op=ALU.mult), ["a", "lim"], ["w2"])
                    dv(lambda: dve.tensor_tensor(out=cf_re[:], in0=cf_re[:], in1=w2[:], op=ALU.add), ["cf_re", "w2"], ["cf_re"])
                    dv(lambda: dve.tensor_tensor(out=cf_re[:], in0=cf_re[:], in1=w1[:], op=ALU.mult), ["cf_re", "w1"], ["cf_re"])
                    dv(lambda: dve.tensor_tensor(out=cf_im[:], in0=a_im[:], in1=lre[:], op=ALU.mult), ["a", "lre"], ["cf_im"])
                    dv(lambda: dve.tensor_tensor(out=w2[:], in0=a_re[:], in1=lim[:], op=ALU.mult), ["a", "lim"], ["w2"])
                    dv(lambda: dve.tensor_tensor(out=cf_im[:], in0=cf_im[:], in1=w2[:], op=ALU.subtract), ["cf_im", "w2"], ["cf_im"])
                    dv(lambda: dve.tensor_tensor(out=cf_im[:], in0=cf_im[:], in1=w1[:], op=ALU.mult), ["cf_im", "w1"], ["cf_im"])
                    dv(lambda: dve.tensor_scalar(out=al128[:], in0=alpha[:], scalar1=128.0, scalar2=None, op0=ALU.mult), ["alpha"], ["al128"])
                    dv(lambda: dve.tensor_scalar(out=th128[:], in0=theta[:], scalar1=128.0, scalar2=None, op0=ALU.mult), ["theta"], ["th128"])
                    trig(t128_re, t128_im, th128, al128, NP2, 128, 1.0, ["th128", "al128"], ["t128"])
                    angp = sb2("angp", [128, S]); margp = sb2("margp", [128, S])
                    for q in range(NP2):
                        dv(lambda q=q: dve.tensor_scalar(out=angp[:, q * 128:(q + 1) * 128], in0=iota_r[:], scalar1=theta[:, q:q + 1],
                                                         scalar2=None, op0=ALU.mult), ["iota_r", "theta"], ["angm"])
                        P.op("pool", lambda q=q: pool.tensor_scalar(out=margp[:, q * 128:(q + 1) * 128], in0=iota_r[:], scalar1=alpha[:, q:q + 1],
                                                                    scalar2=None, op0=ALU.mult), reads=["iota_r", "alpha"], writes=["margm"])
                    trig(Tp_re, Tp_im, angp, margp, NP2 * 128, 128, 1.0, ["angm", "margm"], ["Tp"])
                    P.op("sp", lambda: sp.dma_start(out=_dap(prm_h, 0, [[1, 128], [128, NP2]]), in_=theta[:]), reads=["theta"], writes=["prm"], dma="scr")
                    P.op("sp", lambda: sp.dma_start(out=_dap(prm_h, S, [[1, 128], [128, NP2]]), in_=alpha[:]), reads=["alpha"], writes=["prm"], dma="scr")
                    angm, margm = angp, margp
                    P.op("sp", lambda: sp.dma_start(out=angm[:], in_=_dap(prm_h, 0, [[0, 128], [1, S]])), reads=["prm"], writes=["angm"], dma="ld")
                    P.op("sp", lambda: sp.dma_start(out=margm[:], in_=_dap(prm_h, S, [[0, 128], [1, S]])), reads=["prm"], writes=["margm"], dma="ld")
                    dv(lambda: dve.tensor_scalar(out=angm[:], in0=angm[:], scalar1=iota_c[:, 0:1], scalar2=None, op0=ALU.mult), ["angm", "iota_c"], ["angm"])
                    P.op("pool", lambda: pool.tensor_scalar(out=margm[:], in0=margm[:], scalar1=iota_c[:, 0:1], scalar2=None, op0=ALU.mult),
                         reads=["margm", "iota_c"], writes=["margm"])
                    trig(Tm_re, Tm_im, angm, margm, S, 128, -1.0, ["angm", "margm"], ["Tm"])
                    Bn_re = sb2("Bn_re", [128, NP2, 16]); Bn_im = sb2("Bn_im", [128, NP2, 16])
                    tA = sb2("tA", [128, NP2, 16]); tB = sb2("tB", [128, NP2, 16])
                    Bb_re = sb2("Bb_re", [128, NP2, 16]); Bb_im = sb2("Bb_im", [128, NP2, 16])
                    cdma(Bn_re[:], SS["ssm_b_re"].rearrange("(q p c) -> p q c", p=128, c=16), "Bn_re")
                    cdma(Bn_im[:], SS["ssm_b_im"].rearrange("(q p c) -> p q c", p=128, c=16), "Bn_im")
                    for (dst, x1_, c1_, x2_, c2_, op_) in ((Bb_re, Bn_re, cf_re, Bn_im, cf_im, ALU.subtract),
                                                           (Bb_im, Bn_im, cf_re, Bn_re, cf_im, ALU.add)):
                        for c in range(16):
                            dv(lambda c=c, x1_=x1_, c1_=c1_: dve.tensor_tensor(out=tA[:, :, c], in0=x1_[:, :, c], in1=c1_[:], op=ALU.mult),
                               ["Bn_re", "Bn_im", "cf_re", "cf_im"], ["tA"])
                            dv(lambda c=c, x2_=x2_, c2_=c2_: dve.tensor_tensor(out=tB[:, :, c], in0=x2_[:, :, c], in1=c2_[:], op=ALU.mult),
                               ["Bn_re", "Bn_im", "cf_re", "cf_im"], ["tB"])
                        dv(lambda dst=dst, op_=op_: dve.tensor_tensor(out=dst[:], in0=tA[:], in1=tB[:], op=op_), ["tA", "tB"], ["Bb"])
                    src_t = sb2("src_t", [128, 4, 2, 16])
                    for Bb, Bblk in ((Bb_re, Bblk_re), (Bb_im, Bblk_im)):
                        for ch in range(NUC):
                            for g2 in range(2):
                                dv(lambda Bb=Bb, ch=ch, g2=g2: dve.tensor_scalar(out=src_t[:, :, g2, :], in0=Bb[:, 4 * ch:4 * ch + 4, :],
                                                                                 scalar1=mask2[:, g2:g2 + 1], scalar2=None, op0=ALU.mult),
                                   ["Bb", "mask2"], ["src_t"])
                            P.op("pe", lambda: pe.transpose(out=pb[7][:, 0:128], in_=src_t[:].rearrange("p a b c -> p (a b c)"), identity=ident[:]),
                                 reads=["src_t", "ident"], writes=[("ps", 7)])
                            for q4 in range(4):
                                dv(lambda Bblk=Bblk, ch=ch, q4=q4: dve.tensor_scalar(out=Bblk[ch][:, q4 * 128:(q4 + 1) * 128], in0=pb[7][:, 0:128],
                                                                                     scalar1=mask3[:, q4:q4 + 1], scalar2=None, op0=ALU.mult),
                                   [("ps", 7), "mask3"], [("Bblk", ch)])
                    Cd = sb2("Cd", [128, 2, 64])
                    for name, outs in (("ssm_c_re", ((Cre, mask2), (nCre, nmask2))), ("ssm_c_im", ((nCim, nmask2),))):
                        cv = SS[name].rearrange("(c p n) -> c p n", p=128, n=64)
                        for ch in range(NUC):
                            cdma(Cd[:, 0, :], cv[ch], "Cd")
                            cdma(Cd[:, 1, :], cv[ch], "Cd")
                            P.op("pe", lambda: pe.transpose(out=pb[7][:, 0:128], in_=Cd[:].rearrange("p a b -> p (a b)"), identity=ident[:]),
                                 reads=["Cd", "ident"], writes=[("ps", 7)])
                            trv = pb[7][:, 0:128].rearrange("p (a b c) -> p a b c", a=4, b=2)
                            for dst, mk in outs:
                                for g2 in range(2):
                                    dv(lambda dst=dst, mk=mk, ch=ch, g2=g2, trv=trv: dve.tensor_scalar(
                                        out=dst[:, 4 * ch:4 * ch + 4, g2, :], in0=trv[:, :, g2, :], scalar1=mk[:, g2:g2 + 1],
                                        scalar2=None, op0=ALU.mult), [("ps", 7), "mask2", "nmask2"], ["Cw"])
                    P.barrier()

                uTc = [sbl(f"uTc{i}", [128, NUC, 512], BF16) for i in range(2)]
                yst = sbl("yst", [128, NUC, 512])
                dm = [[sbl(f"dm{i}_{j}", [128, 512], BF16) for j in range(4)] for i in range(2)]
                rm = [[sbl(f"rm{i}_{j}", [128, 512], BF16) for j in range(4)] for i in range(2)]
                tn = [sbl(f"tn{i}", [128, 4]) for i in range(4)]
                gt = [sbl(f"gt{i}", [128, 512]) for i in range(3)]
                gout = [sbl(f"gout{i}", [128, 512], BF16) for i in range(2)]
                gbase = cfg.get("group_base", 0)
                nbg = 0
                ngo = 0
                for sc in range(SEQ // 512):
                    ub = sc % 2
                    P.op("sp", lambda ub=ub, sc=sc: sp.dma_start(
                        out=uTc[ub][:, :, :],
                        in_=mine["u"].ap()[:, sc * 512:(sc + 1) * 512].rearrange("(c p) t -> p c t", p=128)),
                         reads=["projM"], writes=[("uTc", ub)], dma="ld")
                    for c4 in range(4):
                        c = sc * 4 + c4
                        ts = slice(c4 * 128, (c4 + 1) * 128)
                        for ch in range(NUC):
                            i2 = nbg % 2
                            nbg += 1
                            bre, bim = (0, 1) if i2 == 0 else (2, 3)
                            XR, XI, YB = 4, 5, 6
                            P.op("pe", lambda ub=ub, ch=ch, ts=ts, bre=bre: pe.matmul(pb[bre][:, :], lhsT=uTc[ub][:, ch, ts], rhs=Bblk_re[ch][:], start=True, stop=True),
                                 reads=[("uTc", ub), ("Bblk", ch)], writes=[("ps", bre)])
                            P.op("pe", lambda ub=ub, ch=ch, ts=ts, bim=bim: pe.matmul(pb[bim][:, :], lhsT=uTc[ub][:, ch, ts], rhs=Bblk_im[ch][:], start=True, stop=True),
                                 reads=[("uTc", ub), ("Bblk", ch)], writes=[("ps", bim)])
                            tsl = slice(ch * 512, (ch + 1) * 512)
                            A_, B_, C_, D_ = dm[i2]
                            for dst, tab, bsrc, k_ in ((A_, Tm_re, bre, 0), (B_, Tm_im, bim, 1), (C_, Tm_re, bim, 2), (D_, Tm_im, bre, 3)):
                                P.op("dve", lambda dst=dst, tab=tab, bsrc=bsrc, tsl=tsl: dve.tensor_tensor(out=dst[:], in0=pb[bsrc][:, :], in1=tab[:, tsl], op=ALU.mult),
                                     reads=[("ps", bsrc), "Tm"], writes=[("dm", i2, k_)])
                            for q4 in range(4):
                                q = 4 * ch + q4
                                cs = slice(q4 * 128, (q4 + 1) * 128)
                                for (xb, m1, k1, m2, k2, rhs2, injT, inm) in ((XR, A_, 0, B_, 1, ntri, injT_re, "injT_re"), (XI, C_, 2, D_, 3, tri, injT_im, "injT_im")):
                                    P.op("pe", lambda xb=xb, m1=m1, cs=cs: pe.matmul(pb[xb][:, cs], lhsT=m1[:, cs], rhs=tri[:], start=True, stop=False),
                                         reads=[("dm", i2, k1), "tri"], writes=[("ps", xb)])
                                    P.op("pe", lambda xb=xb, m2=m2, cs=cs, rhs2=rhs2, c=c: pe.matmul(pb[xb][:, cs], lhsT=m2[:, cs], rhs=rhs2[:], start=False, stop=(c == 0)),
                                         reads=[("dm", i2, k2), "tri", "ntri"], writes=[("ps", xb)])
                                    if c > 0:
                                        P.op("pe", lambda xb=xb, cs=cs, injT=injT, q=q: pe.matmul(pb[xb][:, cs], lhsT=injT[0:NP2, :], rhs=sel[0:NP2, q, :], start=False, stop=True),
                                             reads=[inm, "sel"], writes=[("ps", xb)])
                            E1, E2, E3, E4 = rm[i2]
                            tps = slice(4 * ch * 128, (4 * ch + 4) * 128)
                            for dst, tab, xsrc, k_ in ((E1, Tp_re, XR, 0), (E2, Tp_im, XI, 1), (E3, Tp_re, XI, 2), (E4, Tp_im, XR, 3)):
                                P.op("dve", lambda dst=dst, tab=tab, xsrc=xsrc, tps=tps: dve.tensor_tensor(out=dst[:], in0=pb[xsrc][:, :], in1=tab[:, tps], op=ALU.mult),
                                     reads=[("ps", xsrc), "Tp"], writes=[("rm", i2, k_)])
                            if c < SEQ // 128 - 1:
                                xr = pb[XR][:, 127:512:128]
                                xi = pb[XI][:, 127:512:128]
                                qs = slice(4 * ch, 4 * ch + 4)
                                P.op("dve", lambda xr=xr, qs=qs: dve.tensor_tensor(out=tn[0][:], in0=xr, in1=t128_re[:, qs], op=ALU.mult), reads=[("ps", XR), "t128"], writes=["tn0"])
                                P.op("dve", lambda xi=xi, qs=qs: dve.tensor_tensor(out=tn[1][:], in0=xi, in1=t128_im[:, qs], op=ALU.mult), reads=[("ps", XI), "t128"], writes=["tn1"])
                                P.op("dve", lambda xi=xi, qs=qs: dve.tensor_tensor(out=tn[2][:], in0=xi, in1=t128_re[:, qs], op=ALU.mult), reads=[("ps", XI), "t128"], writes=["tn2"])
                                P.op("dve", lambda xr=xr, qs=qs: dve.tensor_tensor(out=tn[3][:], in0=xr, in1=t128_im[:, qs], op=ALU.mult), reads=[("ps", XR), "t128"], writes=["tn3"])
                                P.op("dve", lambda qs=qs: dve.tensor_tensor(out=inj_re[:, qs], in0=tn[0][:], in1=tn[1][:], op=ALU.subtract), reads=["tn0", "tn1"], writes=["inj_re"])
                                P.op("dve", lambda qs=qs: dve.tensor_tensor(out=inj_im[:, qs], in0=tn[2][:], in1=tn[3][:], op=ALU.add), reads=["tn2", "tn3"], writes=["inj_im"])
                            for q4 in range(4):
                                q = 4 * ch + q4
                                cs = slice(q4 * 128, (q4 + 1) * 128)
                                yo = pb[YB][32 * q4:32 * q4 + 32, 0:128]
                                for wi, (wt, et, k_) in enumerate(((Cre, E1, 0), (nCre, E2, 1), (nCim, E3, 2), (nCim, E4, 3))):
                                    P.op("pe", lambda yo=yo, wt=wt, et=et, q=q, cs=cs, wi=wi, q4=q4: pe.matmul(
                                        yo, lhsT=wt[:, q, :, :].rearrange("p a b -> p (a b)"), rhs=et[:, cs], start=(wi == 0), stop=(wi == 3),
                                        tile_position=(0, 32 * q4)),
                                         reads=["Cw", ("rm", i2, k_)], writes=[("ps", YB)])
                            P.op("dve", lambda ub=ub, ch=ch, ts=ts: dve.scalar_tensor_tensor(
                                out=yst[:, ch, ts], in0=uTc[ub][:, ch, ts], scalar=d_col[:, ch:ch + 1], in1=pb[YB][:, 0:128], op0=ALU.mult, op1=ALU.add),
                                 reads=[("uTc", ub), "d_col", ("ps", YB)], writes=[("yst", ch)])
                        if c < SEQ // 128 - 1:
                            for src_, dstT, inm, bnk in ((inj_re, injT_re, "injT_re", 7), (inj_im, injT_im, "injT_im", 7)):
                                P.op("pe", lambda src_=src_: pe.transpose(out=pb[7][0:NP2, 0:128], in_=src_[:, :], identity=ident[:]),
                                     reads=["inj_re", "inj_im", "ident"], writes=[("ps", 7)])
                                P.op("act", lambda dstT=dstT: act.copy(out=dstT[0:NP2, :], in_=pb[7][0:NP2, 0:128]), reads=[("ps", 7)], writes=[inm])
                    for ch in range(NUC):
                        yv = yst[:, ch, :]
                        go = ngo % 2
                        ngo += 1
                        P.op("act", lambda yv=yv: act.activation(out=gt[0][:], in_=yv, func=AF.Square), reads=[("yst", ch)], writes=["gt0"])
                        P.op("pool", lambda: pool.tensor_scalar(out=gt[1][:], in0=gt[0][:], scalar1=0.044715, scalar2=1.0, op0=ALU.mult, op1=ALU.add),
                             reads=["gt0"], writes=["gt1"])
                        P.op("pool", lambda yv=yv: pool.tensor_tensor(out=gt[1][:], in0=gt[1][:], in1=yv, op=ALU.mult), reads=["gt1", ("yst", ch)], writes=["gt1"])
                        P.op("act", lambda: act.activation(out=gt[2][:], in_=gt[1][:], func=AF.Sigmoid, scale=1.5957691216057308), reads=["gt1"], writes=["gt2"])
                        P.op("pool", lambda yv=yv, go=go: pool.tensor_tensor(out=gout[go][:], in0=gt[2][:], in1=yv, op=ALU.mult),
                             reads=["gt2", ("yst", ch)], writes=[("gout", go)])
                        if npair == 1:
                            dst = mix_h.ap()[HPC * 128 + ch * 128:HPC * 128 + (ch + 1) * 128, sc * 512:(sc + 1) * 512]
                        else:
                            j_, tl_ = divmod(sc * 512, ntok)
                            dst = bv(mixA[j_][1])[ch * 128:(ch + 1) * 128, tl_:tl_ + 512]
                        P.op("sp", lambda go=go, dst=dst: sp.dma_start(out=dst, in_=gout[go][:]), reads=[("gout", go)], writes=["mix"], dma="scr")
                P.barrier()

        if "A" in stages:
            row_local_phase("A")
        if npair > 1:
            for kd in ("q", "k", "u", "v"):
                for j in range(2):
                    allgather(kd, ownS[kd][j], gatS[kd][j], "projA", "projG")
            for kd in ("q", "k", "u", "v"):
                nr = SEQ if kd == "v" else 1024
                for j in range(2):
                    P.op("sp", lambda kd=kd, j=j, nr=nr: sp.dma_start(out=stg[kd].ap()[j * nr:(j + 1) * nr, :], in_=bv(gatS[kd][j])),
                         reads=["projG"], writes=["projS"], dma="ld")
            for kd in ("q", "k", "u"):
                for rb in range(2):
                    P.op("sp", lambda kd=kd, rb=rb: sp.dma_start(
                        out=mine[kd].ap()[:, rb * ntok:(rb + 1) * ntok], in_=rows_dyn(stg[kd].ap(), rb * 512, 512, 1024)),
                         reads=["projS"], writes=["projM"], dma="ld")
            for hf in range(2):
                P.op("sp", lambda hf=hf: sp.dma_start(
                    out=mine["v"].ap()[hf * 2048:(hf + 1) * 2048, :], in_=rows_dyn(stg["v"].ap(), hf * 2048, 2048, SEQ)),
                     reads=["projS"], writes=["projM"], dma="ld")
            P.barrier()
        if "B" in stages:
            attention_phase()
            if ssm_on:
                ssm_phase()
        if npair > 1:
            for j in range(2):
                for h in range(2):
                    allgather("mix", mixA[j][h], mixGt[j][h], "mix", "mixG")
            for j in range(2):
                for h in range(2):
                    P.op("sp", lambda j=j, h=h: sp.dma_start(
                        out=stg_mix.ap()[j * 2048 + h * 1024:j * 2048 + (h + 1) * 1024, :], in_=bv(mixGt[j][h])),
                         reads=["mixG"], writes=["mixS"], dma="ld")
            for h in range(2):
                for rb in range(2):
                    P.op("sp", lambda h=h, rb=rb: sp.dma_start(
                        out=mixM_h.ap()[rb * 1024 + h * 512:rb * 1024 + (h + 1) * 512, :],
                        in_=rows_dyn(stg_mix.ap(), h * 1024 + rb * 512, 512, 2048)),
                         reads=["mixS"], writes=["mixM"], dma="ld")
            P.barrier()
        if "C" in stages:
            row_local_phase("C")
        P.barrier()
    return nc


_CONSTS = None


def _t5_bucket(dist):
    dist = np.asarray(dist)
    max_exact = 16
    d_f = np.maximum(dist, max_exact).astype(np.float32)
    val = (np.log(d_f / np.float32(max_exact)) / np.float32(np.log(2048 / max_exact)) * np.float32(32 - max_exact))
    large = max_exact + (np.rint(val) if BUCKET_ROUND else val).astype(np.int32)
    large = np.minimum(large, 31)
    return np.where(dist < max_exact, dist, large)


def _consts():
    global _CONSTS
    if _CONSTS is None:
        oh = np.zeros((32, 3 * 129), np.float32)
        for pi, d in enumerate(DILS):
            b = _t5_bucket(np.arange(129) * d)
            oh[b, pi * 129 + np.arange(129)] = 1.0
        _CONSTS = {
            "ident": np.eye(128, dtype=np.float32),
            "ones_bf": np.ones((128, 128), dtype=ml_dtypes.bfloat16),
            "jrev_bf": np.eye(128, dtype=np.float32)[::-1].copy().astype(ml_dtypes.bfloat16),
            "onehot": oh,
            "iota_c": np.arange(128, dtype=np.float32).reshape(128, 1),
            "iota_r": np.tile(np.arange(128, dtype=np.float32), (128, 1)),
            "tri_bf": np.triu(np.ones((128, 128), np.float32)).astype(ml_dtypes.bfloat16),
            "ntri_bf": (-np.triu(np.ones((128, 128), np.float32))).astype(ml_dtypes.bfloat16),
            "mask2": (np.arange(128)[:, None] // 64 == np.arange(2)[None, :]).astype(np.float32),
            "nmask2": -(np.arange(128)[:, None] // 64 == np.arange(2)[None, :]).astype(np.float32),
            "mask3": (np.arange(128)[:, None] // 32 == np.arange(4)[None, :]).astype(np.float32),
            "sel": np.broadcast_to(np.eye(32, dtype=np.float32)[:, :, None], (32, 32, 128)).copy(),
        }
    return _CONSTS


PARAMS = ["ffn1_norm", "ffn1_w_gate", "ffn1_w_up", "ffn1_w_down", "ffn2_norm", "ffn2_w_gate", "ffn2_w_up",
          "ffn2_w_down", "mix_norm", "w_in", "q_norm", "k_norm", "glu_w", "glu_b", "w_out"]


def make_in_maps(inputs, ncores, npair=1):
    x = np.ascontiguousarray(inputs["x"], dtype=np.float32)
    base = dict(_consts())
    for n in PARAMS:
        base[n] = np.ascontiguousarray(inputs[n][0], dtype=np.float32)
    ntok = SEQ // npair
    gpc = N_GROUPS // npair
    in_maps = []
    for c in range(ncores):
        b, r = divmod(c, npair)
        m = dict(base)
        m["x"] = x[b % BATCH, r * ntok:(r + 1) * ntok]
        gs = slice(r * gpc, (r + 1) * gpc)
        for n in ("ssm_lambda_re", "ssm_lambda_im", "ssm_log_dt", "ssm_b_re", "ssm_b_im", "ssm_c_re", "ssm_c_im"):
            m[n] = np.ascontiguousarray(inputs[n][0][gs], dtype=np.float32).reshape(-1)
        m["ssm_d"] = np.ascontiguousarray(inputs["ssm_d"][0][r * gpc * 16:(r + 1) * gpc * 16], dtype=np.float32)
        hpc = N_HEADS // npair
        m["rel_bias"] = np.ascontiguousarray(np.asarray(inputs["rel_bias"], dtype=np.float32)[:, r * hpc:(r + 1) * hpc])
        in_maps.append(m)
    return in_maps


NPAIR = 2


def kernel(**inputs):
    npair = NPAIR
    ncores = BATCH * npair
    nc = build(dict(npair=npair))
    in_maps = make_in_maps(inputs, ncores, npair)
    res = run_bass_kernel_spmd(nc, in_maps, core_ids=list(range(ncores)))
    ntok = SEQ // npair
    out = np.empty((BATCH, SEQ, D), np.float32)
    for c in range(ncores):
        b, r = divmod(c, npair)
        out[b, r * ntok:(r + 1) * ntok] = np.asarray(res.results[c]["out"])
    return out
```

```python
from contextlib import ExitStack
import numpy as np
import ml_dtypes
import concourse.bass as bass
import concourse.mybir as mybir
from concourse.bass_utils import run_bass_kernel_spmd

F32 = mybir.dt.float32
BF16 = mybir.dt.bfloat16
ALU = mybir.AluOpType
AF = mybir.ActivationFunctionType

D = 2048
KD = D // 128
FF = 5632
FC = FF // 128
SEQ = 4096
BATCH = 4
EPS = 1e-6
TT = 1024
NTB = TT // 512
NPARTS = 11
FPP = FC // NPARTS
WSLOT = 128
WDW = 512
NST = 3


class Op:
    __slots__ = ("eng", "fn", "deps", "needed", "dma", "tok")

    def __init__(self, eng, fn, dma):
        self.eng = eng
        self.fn = fn
        self.deps = []
        self.needed = False
        self.dma = dma
        self.tok = None


class Prog:
    ENGS = ("pe", "act", "dve", "pool", "sp")
    LIMIT = 20000

    def __init__(self, nc, stack):
        self.nc = nc
        self.stack = stack
        self.eng = {"pe": nc.tensor, "act": nc.scalar, "dve": nc.vector,
                    "pool": nc.gpsimd, "sp": nc.sync}
        self.ops = []
        self.last_w = {}
        self.readers = {}
        self.cnt = {e: 0 for e in self.ENGS}
        self.esems = {e: [] for e in self.ENGS}
        self.streams = {}
        self.waited = {e: {} for e in self.ENGS}
        self.last_op = {}
        self.nsem = 0

    def _newsem(self, name):
        self.nsem += 1
        return self.stack.enter_context(self.nc.semaphore(f"{name}_{self.nsem}"))

    def stream(self, name, nsems, inc=16):
        self.streams[name] = dict(sems=[self._newsem(name) for _ in range(nsems)], n=0, inc=inc)

    def op(self, eng, fn, reads=(), writes=(), dma=None):
        o = Op(eng, fn, dma)
        deps = {}
        for r in reads:
            w = self.last_w.get(r)
            if w is not None:
                deps[id(w)] = w
        for r in writes:
            w = self.last_w.get(r)
            if w is not None:
                deps[id(w)] = w
            rd = self.readers.get(r)
            if rd:
                for x in rd.values():
                    deps[id(x)] = x
        for p in deps.values():
            if p is o:
                continue
            if p.dma is None and p.eng == "pe" and eng == "pe" and dma is None:
                continue
            p.needed = True
            o.deps.append(p)
        for r in writes:
            self.last_w[r] = o
            self.readers[r] = {}
        key = eng if dma is None else ("dma", dma, len(self.ops))
        for r in reads:
            self.readers.setdefault(r, {})[key if dma is None else id(o)] = o
        self.ops.append(o)
        self.last_op[eng] = o
        return o

    def _wait(self, eng, sem, val):
        k = id(sem)
        w = self.waited[eng]
        if w.get(k, 0) >= val:
            return
        w[k] = val
        self.eng[eng].wait_ge(sem, val)

    def flush(self):
        for o in self.ops:
            for p in o.deps:
                sem, val = p.tok
                self._wait(o.eng, sem, val)
            if o.dma is not None:
                st = self.streams[o.dma]
                n = st["n"]
                st["n"] = n + 1
                R = len(st["sems"])
                inc = st["inc"]
                sem = st["sems"][n % R]
                prev = inc * (n // R)
                if prev > 0:
                    self._wait(o.eng, sem, prev)
                ins = o.fn()
                if inc == 1:
                    ins.then_inc(sem)
                else:
                    ins.then_inc(sem, inc)
                o.tok = (sem, prev + inc)
            else:
                ins = o.fn()
                if o.needed:
                    c = self.cnt[o.eng]
                    ep, v = divmod(c, self.LIMIT)
                    sems = self.esems[o.eng]
                    if ep >= len(sems):
                        sems.append(self._newsem("e" + o.eng))
                    ins.then_inc(sems[ep], 1)
                    self.cnt[o.eng] = c + 1
                    o.tok = (sems[ep], v + 1)
        self.ops = []

    def barrier(self):
        lasts = []
        for e in self.ENGS:
            o = self.last_op.get(e)
            if o is not None and o.dma is None:
                o.needed = True
                lasts.append(o)
        self.flush()
        for e in self.ENGS:
            for o in lasts:
                if o.eng == e and e == "pe":
                    continue
                self._wait(e, *o.tok)
            for st in self.streams.values():
                R = len(st["sems"])
                for i, sem in enumerate(st["sems"]):
                    cnt = (st["n"] - i + R - 1) // R
                    if cnt > 0:
                        self._wait(e, sem, st["inc"] * cnt)
        self.last_w = {}
        self.readers = {}
        self.last_op = {}


NEG = -30000.0
BUCKET_ROUND = False
N_HEADS = 8
N_GROUPS = 64
DILS = (1, 4, 16)


def _dap(h, off, dims):
    return bass.AP(h, off, [list(d) for d in dims])


def build(cfg):
    npair = cfg.get("npair", 1)
    debug = cfg.get("debug", False)
    stages = cfg.get("stages", "ABC")
    ssm_on = cfg.get("ssm", True)
    ntok = SEQ // npair
    NT = ntok // TT
    HPC = N_HEADS // npair
    GPC = N_GROUPS // npair
    NP2 = GPC // 2
    NUC = GPC * 16 // 128
    MIXR = HPC * 128 + GPC * 16
    nc = bass.Bass("TRN2", target_bir_lowering=False)
    SCR = "ExternalOutput" if debug else "Internal"

    def dt(name, shape, dtype=F32, kind="ExternalInput"):
        return nc.dram_tensor(name, shape, dtype, kind=kind)

    x_d = dt("x", [ntok, D]).ap()
    out_d = dt("out", [ntok, D], kind="ExternalOutput").ap()
    ident_d = dt("ident", [128, 128]).ap()
    ones_d = dt("ones_bf", [128, 128], BF16).ap()
    jrev_d = dt("jrev_bf", [128, 128], BF16).ap()
    oh_d = dt("onehot", [32, 3 * 129]).ap()
    shapes = {"ffn1_norm": [D], "ffn1_w_gate": [D, FF], "ffn1_w_up": [D, FF], "ffn1_w_down": [FF, D],
              "ffn2_norm": [D], "ffn2_w_gate": [D, FF], "ffn2_w_up": [D, FF], "ffn2_w_down": [FF, D],
              "mix_norm": [D], "w_in": [D, 4096], "q_norm": [128], "k_norm": [128], "rel_bias": [32, N_HEADS // cfg.get("npair", 1)],
              "glu_w": [1024, 1024], "glu_b": [1024], "w_out": [D, D]}
    W = {n: dt(n, shapes[n]).ap() for n in shapes}
    sshapes = {"ssm_lambda_re": [GPC * 64], "ssm_lambda_im": [GPC * 64], "ssm_log_dt": [GPC], "ssm_b_re": [GPC * 1024],
               "ssm_b_im": [GPC * 1024], "ssm_c_re": [GPC * 1024], "ssm_c_im": [GPC * 1024], "ssm_d": [GPC * 16]}
    SS_h = {n: dt(n, sshapes[n]) for n in sshapes}
    SS = {n: h.ap() for n, h in SS_h.items()}
    iotac_d = dt("iota_c", [128, 1]).ap(); iotar_d = dt("iota_r", [128, 128]).ap()
    tri_d = dt("tri_bf", [128, 128], BF16).ap(); ntri_d = dt("ntri_bf", [128, 128], BF16).ap()
    mask2_d = dt("mask2", [128, 2]).ap(); nmask2_d = dt("nmask2", [128, 2]).ap(); mask3_d = dt("mask3", [128, 4]).ap()
    sel_d = dt("sel", [32, 32, 128]).ap()
    prm_h = dt("prm", [2 * GPC * 64], F32, kind=SCR)
    x1s_h = dt("x1s", [NT * 128 * KD * TT], F32, kind=SCR)
    own = {"q": dt("qA", [1024, ntok], BF16, kind=SCR), "k": dt("kA", [1024, ntok], BF16, kind=SCR),
           "u": dt("uA", [1024, ntok], BF16, kind=SCR), "v": dt("vA", [ntok, 1024], BF16, kind=SCR)} if npair == 1 else None
    mix_h = dt("mix", [MIXR, SEQ], BF16, kind=SCR) if npair == 1 else None
    zpad_h = dt("zpad", [8 * 3 * 384], F32, kind=SCR)
    bv = lambda h: h.ap().bitcast(BF16)
    if npair > 1:
        f32t = lambda name, rows, cols_bf: dt(name, [rows, cols_bf // 2], F32, kind="Internal")
        ownS = {kd: [f32t(f"{kd}A{j}", 512, ntok) for j in range(2)] for kd in ("q", "k", "u")}
        ownS["v"] = [f32t(f"vA{j}", ntok, 512) for j in range(2)]
        gatS = {kd: [f32t(f"{kd}G{j}", 1024, ntok) for j in range(2)] for kd in ("q", "k", "u")}
        gatS["v"] = [f32t(f"vG{j}", 2 * ntok, 512) for j in range(2)]
        stg = {kd: dt(f"{kd}S", [2 * 1024, ntok], BF16, kind="Internal") for kd in ("q", "k", "u")}
        stg["v"] = dt("vS", [2 * SEQ, 512], BF16, kind="Internal")
        mixA = [[f32t(f"mixA{j}{h}", 512, ntok) for h in range(2)] for j in range(2)]
        mixGt = [[f32t(f"mixG{j}{h}", 1024, ntok) for h in range(2)] for j in range(2)]
        stg_mix = dt("mixS", [2 * 2048, ntok], BF16, kind="Internal")
    RANK = (nc.partition_id() % 2) if npair > 1 else 0
    if npair == 1:
        mine = dict(own)
        mixM_h = mix_h
    else:
        mine = {"q": dt("qM", [HPC * 128, SEQ], BF16, kind="Internal"), "k": dt("kM", [HPC * 128, SEQ], BF16, kind="Internal"),
                "u": dt("uM", [GPC * 16, SEQ], BF16, kind="Internal"), "v": dt("vM", [SEQ, HPC * 128], BF16, kind="Internal")}
        mixM_h = dt("mixM", [2 * MIXR, ntok], BF16, kind="Internal")

    def rows_dyn(ap, static, size, mult):
        if npair == 1:
            return ap[static:static + size, :]
        return ap[static:static + mult + size, :][bass.ds(RANK * mult, size), :]

    def cols_dyn(ap, static, size, mult):
        if npair == 1:
            return ap[:, static:static + size]
        return ap[:, static:static + mult + size][:, bass.ds(RANK * mult, size)]

    with ExitStack() as stack:
        P = Prog(nc, stack)
        stack.enter_context(nc.allow_non_contiguous_dma(reason="tiny strided parameter loads"))
        mk_sb = lambda st, pfx='': (lambda name, shape, dtype=F32: st.enter_context(nc.sbuf_tensor(pfx + name, shape, dtype)))
        sb = mk_sb(stack)
        ps = lambda name, shape, dtype=F32: stack.enter_context(nc.psum_tensor(name, shape, dtype))
        sp, act, dve, pool, pe = nc.sync, nc.scalar, nc.vector, nc.gpsimd, nc.tensor

        ident = sb("ident_sb", [128, 128])
        ones = sb("ones_sb", [128, 128], BF16)
        jrev = sb("jrev_sb", [128, 128], BF16)
        g1 = sb("g1", [128, KD])
        g2 = sb("g2", [128, KD])
        g3 = sb("g3", [128, KD])
        gqk = sb("gqk", [128, 2])
        glub = sb("glub", [128, 8])
        pb = [ps(f"pb{i}", [128, 512]) for i in range(8)]
        PT, PG, PU, PD = (0, 1), (2, 3), (4, 5), (6, 7)

        P.stream("xin", 2)
        P.stream("const", 1)
        P.stream("wst", NST)
        P.stream("out", 2)
        P.stream("scr", 4)
        P.stream("ld", 4)
        P.stream("cc", 1, inc=1)
        PAIRS = [[0, 1], [2, 3], [4, 5], [6, 7]]

        def allgather(name, src_h, dst_h, rres, wres):
            P.op("pool", lambda: pool.collective_compute("AllGather", ALU.bypass, replica_groups=PAIRS,
                                                         ins=[src_h.ap().opt()], outs=[dst_h.ap().opt()]),
                 reads=[rres], writes=[wres], dma="cc")

        if npair == 1:
            _op = P.op

            def _op_alias(eng, fn, reads=(), writes=(), dma=None):
                al = lambda rs: [{"projG": "projA", "projM": "projA", "mixG": "mix", "mixM": "mix"}.get(x, x) if isinstance(x, str) else x for x in rs]
                return _op(eng, fn, reads=al(reads), writes=al(writes), dma=dma)
            P.op = _op_alias

        def cdma(dst, src, res):
            P.op("sp", lambda: sp.dma_start(out=dst, in_=src), writes=[res], dma="const")

        cdma(ident[:], ident_d, "ident")
        cdma(ones[:], ones_d, "ones")
        cdma(jrev[:], jrev_d, "jrev")
        cdma(g1[:], W["ffn1_norm"].rearrange("(k p) -> p k", p=128), "g1")
        cdma(g2[:], W["mix_norm"].rearrange("(k p) -> p k", p=128), "g2")
        cdma(g3[:], W["ffn2_norm"].rearrange("(k p) -> p k", p=128), "g3")
        cdma(gqk[:, 0:1], W["q_norm"].rearrange("(p o) -> p o", o=1), "gqk")
        cdma(gqk[:, 1:2], W["k_norm"].rearrange("(p o) -> p o", o=1), "gqk")
        cdma(glub[:], W["glu_b"].rearrange("(k p) -> p k", p=128), "glub")
        P.op("dve", lambda: dve.tensor_scalar(out=gqk[:, 0:1], in0=gqk[:, 0:1], scalar1=float(128 ** -0.5),
                                              scalar2=None, op0=ALU.mult), reads=["gqk"], writes=["gqk"])

        cnt = {"st": 0, "cp": 0, "xs": 0, "sq": 0, "wg": 0, "wd": 0, "pg": 0, "pd": 0, "pt": 0, "sg": 0,
               "qst": 0, "vst": 0}

        def evac_copy(out_ap, in_ap, reads, writes):
            cnt["cp"] += 1
            if cnt["cp"] % 2:
                P.op("act", lambda: act.copy(out=out_ap, in_=in_ap), reads=reads, writes=writes)
            else:
                P.op("dve", lambda: dve.tensor_copy(out=out_ap, in_=in_ap), reads=reads, writes=writes)

        def row_local_phase(phase):
            with ExitStack() as st:
                sbl = mk_sb(st, phase + "_")
                xT = sbl("xT", [128, KD, TT])
                hnT = sbl("hnT", [128, KD, TT], BF16)
                HT = sbl("HT", [128, FPP, TT], BF16)
                xs = [sbl(f"xs{i}", [128, D]) for i in range(2)]
                wg = [sbl(f"wg{i}", [128, KD, WSLOT], BF16) for i in range(2)]
                wu = [sbl(f"wu{i}", [128, KD, WSLOT], BF16) for i in range(2)]
                wd = [sbl(f"wd{i}", [128, FPP, WDW], BF16) for i in range(2)]
                stage = [sbl(f"stage{i}", [128, KD * WSLOT]) for i in range(NST)]
                sq = [sbl(f"sq{i}", [128, 512], BF16) for i in range(2)]
                sg = [sbl(f"sg{i}", [128, 512]) for i in range(2)]
                rstd = [sbl(f"rstd{i}", [128, 512]) for i in range(NTB)]
                qst = [sbl(f"qst{i}", [128, 512], BF16) for i in range(2)]
                vst = [sbl(f"vst{i}", [128, 4, 128], BF16) for i in range(2)]
                yTb = sbl("yTb", [128, 8, TT], BF16) if phase == "C" else None

                def load_xT(t0):
                    for s in range(TT // 128):
                        slot = cnt["xs"] % 2
                        cnt["xs"] += 1
                        src = x_d[t0 + s * 128:t0 + (s + 1) * 128, :]
                        P.op("sp", lambda slot=slot, src=src: sp.dma_start(out=xs[slot][:], in_=src),
                             writes=[("xs", slot)], dma="xin")
                        for k4 in range(KD // 4):
                            b = PT[cnt["pt"] % 2]
                            cnt["pt"] += 1
                            for j in range(4):
                                k = k4 * 4 + j
                                P.op("pe", lambda b=b, j=j, k=k, slot=slot: pe.transpose(
                                    out=pb[b][:, j * 128:(j + 1) * 128], in_=xs[slot][:, k * 128:(k + 1) * 128],
                                    identity=ident[:]), reads=[("xs", slot), "ident"], writes=[("ps", b)])
                            evac_copy(xT[:, k4 * 4:(k4 + 1) * 4, s * 128:(s + 1) * 128],
                                      pb[b][:, :].rearrange("p (j t) -> p j t", j=4),
                                      [("ps", b)], [("xT", k4 * 4 + j, s // 4) for j in range(4)])

                def store_xT(t0):
                    for s in range(TT // 128):
                        slot = cnt["xs"] % 2
                        cnt["xs"] += 1
                        for k4 in range(KD // 4):
                            b = PT[cnt["pt"] % 2]
                            cnt["pt"] += 1
                            for j in range(4):
                                k = k4 * 4 + j
                                P.op("pe", lambda b=b, j=j, k=k, s=s: pe.transpose(
                                    out=pb[b][:, j * 128:(j + 1) * 128], in_=xT[:, k, s * 128:(s + 1) * 128],
                                    identity=ident[:]), reads=[("xT", k, s // 4), "ident"], writes=[("ps", b)])
                            evac_copy(xs[slot][:, k4 * 512:(k4 + 1) * 512], pb[b][:, :],
                                      [("ps", b)], [("xs", slot)])
                        dst = out_d[t0 + s * 128:t0 + (s + 1) * 128, :]
                        P.op("sp", lambda slot=slot, dst=dst: sp.dma_start(out=dst, in_=xs[slot][:]),
                             reads=[("xs", slot)], dma="out")

                def rsqrt_inplace(t, res):
                    P.op("act", lambda: act.sqrt(out=t, in_=t), reads=[res], writes=[res])
                    P.op("dve", lambda: dve.reciprocal(out=t, in_=t), reads=[res], writes=[res])

                def rmsnorm(g, gname):
                    for tb in range(NTB):
                        b = PD[tb % 2]
                        tsl = slice(tb * 512, (tb + 1) * 512)
                        for k in range(KD):
                            q = cnt["sq"] % 2
                            cnt["sq"] += 1
                            P.op("act", lambda q=q, k=k, tsl=tsl: act.activation(out=sq[q][:], in_=xT[:, k, tsl], func=AF.Square),
                                 reads=[("xT", k, tb)], writes=[("sq", q)])
                            P.op("pe", lambda q=q, k=k, b=b: pe.matmul(pb[b][:, :], lhsT=ones[:], rhs=sq[q][:],
                                                                        start=(k == 0), stop=(k == KD - 1)),
                                 reads=[("sq", q), "ones"], writes=[("ps", b)])
                        P.op("dve", lambda b=b, tb=tb: dve.tensor_scalar(out=rstd[tb][:], in0=pb[b][:, :], scalar1=1.0 / D,
                                                                          scalar2=EPS, op0=ALU.mult, op1=ALU.add),
                             reads=[("ps", b)], writes=[("rstd", tb)])
                        rsqrt_inplace(rstd[tb][:], ("rstd", tb))
                        for k in range(KD):
                            P.op("dve", lambda k=k, tb=tb, tsl=tsl: dve.scalar_tensor_tensor(
                                out=hnT[:, k, tsl], in0=xT[:, k, tsl], scalar=g[:, k:k + 1], in1=rstd[tb][:],
                                op0=ALU.mult, op1=ALU.mult),
                                 reads=[("xT", k, tb), ("rstd", tb), gname], writes=[("hn", k, tb)])

                def wload(dst_ap, src_ap, wres):
                    st_ = cnt["st"] % NST
                    cnt["st"] += 1
                    shp = dst_ap.shape
                    sview = stage[st_][:, 0:shp[1] * shp[2]].rearrange("p (a b) -> p a b", a=shp[1])
                    P.op("sp", lambda: sp.dma_start(out=sview, in_=src_ap), writes=[("stage", st_)], dma="wst")
                    if cnt["st"] % 3 == 0:
                        P.op("act", lambda: act.copy(out=dst_ap, in_=sview), reads=[("stage", st_)], writes=[wres])
                    else:
                        P.op("pool", lambda: pool.tensor_copy(out=dst_ap, in_=sview), reads=[("stage", st_)], writes=[wres])

                def ffn(wgate, wup, wdown):
                    wg_v = wgate.rearrange("(k p) f -> p k f", p=128)
                    wu_v = wup.rearrange("(k p) f -> p k f", p=128)
                    wd_v = wdown.rearrange("(c p) d -> p c d", p=128)
                    for part in range(NPARTS):
                        for j in range(FPP):
                            f0 = (part * FPP + j) * 128
                            slot = cnt["wg"] % 2
                            cnt["wg"] += 1
                            wload(wg[slot][:, :, :], wg_v[:, :, f0:f0 + 128], ("wg", slot))
                            wload(wu[slot][:, :, :], wu_v[:, :, f0:f0 + 128], ("wu", slot))
                            for tb in range(NTB):
                                tsl = slice(tb * 512, (tb + 1) * 512)
                                i = cnt["pg"] % 2
                                cnt["pg"] += 1
                                bg, bu = PG[i], PU[i]
                                for k in range(KD):
                                    P.op("pe", lambda k=k, bg=bg, slot=slot, tsl=tsl: pe.matmul(
                                        pb[bg][:, :], lhsT=wg[slot][:, k, :], rhs=hnT[:, k, tsl],
                                        start=(k == 0), stop=(k == KD - 1)),
                                         reads=[("wg", slot), ("hn", k, tb)], writes=[("ps", bg)])
                                for k in range(KD):
                                    P.op("pe", lambda k=k, bu=bu, slot=slot, tsl=tsl: pe.matmul(
                                        pb[bu][:, :], lhsT=wu[slot][:, k, :], rhs=hnT[:, k, tsl],
                                        start=(k == 0), stop=(k == KD - 1)),
                                         reads=[("wu", slot), ("hn", k, tb)], writes=[("ps", bu)])
                                q = cnt["sg"] % 2
                                cnt["sg"] += 1
                                P.op("act", lambda q=q, bg=bg: act.activation(out=sg[q][:], in_=pb[bg][:, :], func=AF.Silu),
                                     reads=[("ps", bg)], writes=[("sg", q)])
                                P.op("dve", lambda q=q, bu=bu, j=j, tsl=tsl: dve.tensor_tensor(
                                    out=HT[:, j, tsl], in0=sg[q][:], in1=pb[bu][:, :], op=ALU.mult),
                                     reads=[("sg", q), ("ps", bu)], writes=[("H", j, tb)])
                        for dg in range(D // WDW):
                            slot = cnt["wd"] % 2
                            cnt["wd"] += 1
                            wload(wd[slot][:, :, :], wd_v[:, part * FPP:(part + 1) * FPP, dg * WDW:(dg + 1) * WDW], ("wd", slot))
                            for di in range(WDW // 128):
                                dc = dg * (WDW // 128) + di
                                for tb in range(NTB):
                                    tsl = slice(tb * 512, (tb + 1) * 512)
                                    b = PD[cnt["pd"] % 2]
                                    cnt["pd"] += 1
                                    for j in range(FPP):
                                        P.op("pe", lambda j=j, b=b, slot=slot, di=di, tsl=tsl: pe.matmul(
                                            pb[b][:, :], lhsT=wd[slot][:, j, di * 128:(di + 1) * 128], rhs=HT[:, j, tsl],
                                            start=(j == 0), stop=(j == FPP - 1)),
                                             reads=[("wd", slot), ("H", j, tb)], writes=[("ps", b)])
                                    P.op("dve", lambda b=b, dc=dc, tsl=tsl: dve.scalar_tensor_tensor(
                                        out=xT[:, dc, tsl], in0=pb[b][:, :], scalar=0.5, in1=xT[:, dc, tsl],
                                        op0=ALU.mult, op1=ALU.add),
                                         reads=[("ps", b), ("xT", dc, tb)], writes=[("xT", dc, tb)])

                all_xT = [("xT", k, tb) for k in range(KD) for tb in range(NTB)]

                def proj(t0):
                    rmsnorm(g2, "g2")
                    win_v = W["w_in"].rearrange("(k p) f -> p k f", p=128)
                    for oc in range(32):
                        slot = cnt["wg"] % 2
                        cnt["wg"] += 1
                        wload(wg[slot][:, :, :], win_v[:, :, oc * 128:(oc + 1) * 128], ("wg", slot))
                        if 16 <= oc < 24:
                            for s4 in range(TT // 512):
                                b = PD[cnt["pd"] % 2]
                                cnt["pd"] += 1
                                for si in range(4):
                                    s = s4 * 4 + si
                                    for k in range(KD):
                                        P.op("pe", lambda k=k, b=b, si=si, s=s, slot=slot: pe.matmul(
                                            pb[b][:, si * 128:(si + 1) * 128], lhsT=hnT[:, k, s * 128:(s + 1) * 128],
                                            rhs=wg[slot][:, k, :], start=(k == 0), stop=(k == KD - 1)),
                                             reads=[("wg", slot), ("hn", k, s // 4)], writes=[("ps", b)])
                                vq = cnt["vst"] % 2
                                cnt["vst"] += 1
                                evac_copy(vst[vq][:, :, :], pb[b][:, :].rearrange("p (j t) -> p j t", j=4),
                                          [("ps", b)], [("vst", vq)])
                                if npair == 1:
                                    dst = _dap(own["v"], (t0 + s4 * 512) * 1024 + (oc - 16) * 128,
                                               [[1024, 128], [128 * 1024, 4], [1, 128]])
                                else:
                                    hj, hl_ = divmod(oc - 16, HPC)
                                    dst = bv(ownS["v"][hj])[t0 + s4 * 512:t0 + (s4 + 1) * 512, hl_ * 128:(hl_ + 1) * 128] \
                                        .rearrange("(s p) e -> p s e", p=128)
                                P.op("sp", lambda vq=vq, dst=dst: sp.dma_start(out=dst, in_=vst[vq][:, :, :]),
                                     reads=[("vst", vq)], writes=["projA"], dma="scr")
                            continue
                        for tb in range(NTB):
                            tsl = slice(tb * 512, (tb + 1) * 512)
                            i = cnt["pg"] % 2
                            cnt["pg"] += 1
                            bg, bn = PG[i], PU[i]
                            for k in range(KD):
                                P.op("pe", lambda k=k, bg=bg, slot=slot, tsl=tsl: pe.matmul(
                                    pb[bg][:, :], lhsT=wg[slot][:, k, :], rhs=hnT[:, k, tsl],
                                    start=(k == 0), stop=(k == KD - 1)),
                                     reads=[("wg", slot), ("hn", k, tb)], writes=[("ps", bg)])
                            qq = cnt["qst"] % 2
                            cnt["qst"] += 1
                            if oc < 16:
                                q = cnt["sq"] % 2
                                cnt["sq"] += 1
                                r = cnt["sg"] % 2
                                cnt["sg"] += 1
                                P.op("act", lambda q=q, bg=bg: act.activation(out=sq[q][:], in_=pb[bg][:, :], func=AF.Square),
                                     reads=[("ps", bg)], writes=[("sq", q)])
                                P.op("pe", lambda q=q, bn=bn: pe.matmul(pb[bn][:, :], lhsT=ones[:], rhs=sq[q][:], start=True, stop=True),
                                     reads=[("sq", q), "ones"], writes=[("ps", bn)])
                                P.op("dve", lambda r=r, bn=bn: dve.tensor_scalar(out=sg[r][:], in0=pb[bn][:, :], scalar1=1.0 / 128,
                                                                                  scalar2=EPS, op0=ALU.mult, op1=ALU.add),
                                     reads=[("ps", bn)], writes=[("sg", r)])
                                rsqrt_inplace(sg[r][:], ("sg", r))
                                col = 0 if oc < 8 else 1
                                P.op("dve", lambda r=r, bg=bg, qq=qq, col=col: dve.scalar_tensor_tensor(
                                    out=qst[qq][:], in0=pb[bg][:, :], scalar=gqk[:, col:col + 1], in1=sg[r][:],
                                    op0=ALU.mult, op1=ALU.mult),
                                     reads=[("ps", bg), ("sg", r), "gqk"], writes=[("qst", qq)])
                                row0 = (oc % 8) * 128
                                dh = (own["q"] if oc < 8 else own["k"]) if npair == 1 else None
                            else:
                                evac_copy(qst[qq][:], pb[bg][:, :], [("ps", bg)], [("qst", qq)])
                                row0 = (oc - 24) * 128
                                dh = own["u"] if npair == 1 else None
                            if npair == 1:
                                dst = dh.ap()[row0:row0 + 128, t0 + tb * 512:t0 + (tb + 1) * 512]
                            else:
                                kd_ = "q" if oc < 8 else ("k" if oc < 16 else "u")
                                hj, r0_ = divmod(row0, 512)
                                dst = bv(ownS[kd_][hj])[r0_:r0_ + 128, t0 + tb * 512:t0 + (tb + 1) * 512]
                            P.op("sp", lambda qq=qq, dst=dst: sp.dma_start(out=dst, in_=qst[qq][:]),
                                 reads=[("qst", qq)], writes=["projA"], dma="scr")

                def mix_out(t0, tg0):
                    for h in range(N_HEADS):
                        r, hl = divmod(h, HPC)
                        row0 = r * MIXR + hl * 128
                        P.op("sp", lambda h=h, row0=row0: sp.dma_start(
                            out=hnT[:, h, :], in_=mixM_h.ap()[row0:row0 + 128, tg0:tg0 + TT]),
                             reads=["mixM"], writes=[("hn", h, tb) for tb in range(NTB)], dma="ld")
                    for c in range(8):
                        r, cl = divmod(c, NUC)
                        row0 = r * MIXR + HPC * 128 + cl * 128
                        P.op("sp", lambda c=c, row0=row0: sp.dma_start(
                            out=yTb[:, c, :], in_=mixM_h.ap()[row0:row0 + 128, tg0:tg0 + TT]),
                             reads=["mixM"], writes=[("yT", c)], dma="ld")
                    gw_v = W["glu_w"].rearrange("(k p) f -> p k f", p=128)
                    for c2 in range(8):
                        slot = cnt["wg"] % 2
                        cnt["wg"] += 1
                        wload(wg[slot][:, 0:8, :], gw_v[:, :, c2 * 128:(c2 + 1) * 128], ("wg", slot))
                        for tb in range(NTB):
                            tsl = slice(tb * 512, (tb + 1) * 512)
                            bg = PG[cnt["pg"] % 2]
                            cnt["pg"] += 1
                            for c in range(8):
                                P.op("pe", lambda c=c, bg=bg, slot=slot, tsl=tsl: pe.matmul(
                                    pb[bg][:, :], lhsT=wg[slot][:, c, :], rhs=yTb[:, c, tsl], start=(c == 0), stop=(c == 7)),
                                     reads=[("wg", slot), ("yT", c)], writes=[("ps", bg)])
                            q = cnt["sg"] % 2
                            cnt["sg"] += 1
                            P.op("act", lambda q=q, bg=bg, c2=c2: act.activation(out=sg[q][:], in_=pb[bg][:, :], func=AF.Sigmoid,
                                                                                  bias=glub[:, c2:c2 + 1]),
                                 reads=[("ps", bg), "glub"], writes=[("sg", q)])
                            P.op("dve", lambda q=q, c2=c2, tsl=tsl: dve.tensor_tensor(
                                out=hnT[:, 8 + c2, tsl], in0=sg[q][:], in1=yTb[:, c2, tsl], op=ALU.mult),
                                 reads=[("sg", q), ("yT", c2)], writes=[("hn", 8 + c2, tb)])
                    wo_v = W["w_out"].rearrange("(k p) f -> p k f", p=128)
                    for dc in range(KD):
                        slot = cnt["wg"] % 2
                        cnt["wg"] += 1
                        wload(wg[slot][:, :, :], wo_v[:, :, dc * 128:(dc + 1) * 128], ("wg", slot))
                        for tb in range(NTB):
                            tsl = slice(tb * 512, (tb + 1) * 512)
                            b = PD[cnt["pd"] % 2]
                            cnt["pd"] += 1
                            for k in range(KD):
                                P.op("pe", lambda k=k, b=b, slot=slot, tsl=tsl: pe.matmul(
                                    pb[b][:, :], lhsT=wg[slot][:, k, :], rhs=hnT[:, k, tsl], start=(k == 0), stop=(k == KD - 1)),
                                     reads=[("wg", slot), ("hn", k, tb)], writes=[("ps", b)])
                            P.op("dve", lambda b=b, dc=dc, tsl=tsl: dve.scalar_tensor_tensor(
                                out=xT[:, dc, tsl], in0=pb[b][:, :], scalar=1.0, in1=xT[:, dc, tsl],
                                op0=ALU.mult, op1=ALU.add),
                                 reads=[("ps", b), ("xT", dc, tb)], writes=[("xT", dc, tb)])

                for tt in range(NT):
                    t0 = tt * TT
                    x1v = _dap(x1s_h, tt * 128 * KD * TT, [[KD * TT, 128], [TT, KD], [1, TT]])
                    if phase == "A":
                        load_xT(t0)
                        rmsnorm(g1, "g1")
                        ffn(W["ffn1_w_gate"], W["ffn1_w_up"], W["ffn1_w_down"])
                        P.op("sp", lambda x1v=x1v: sp.dma_start(out=x1v, in_=xT[:, :, :]), reads=all_xT, writes=["x1s"], dma="scr")
                        proj(t0)
                    else:
                        P.op("sp", lambda x1v=x1v: sp.dma_start(out=xT[:, :, :], in_=x1v), reads=["x1s"], writes=all_xT, dma="ld")
                        mix_out(t0, t0)
                        rmsnorm(g3, "g3")
                        ffn(W["ffn2_w_gate"], W["ffn2_w_up"], W["ffn2_w_down"])
                        store_xT(t0)
                P.barrier()

        def attention_phase():
            with ExitStack() as st:
                sbl = mk_sb(st, "B_")
                qTh = sbl("qTh", [128, SEQ], BF16)
                kTh = sbl("kTh", [128, SEQ], BF16)
                vb = [sbl(f"vb{i}", [128, 32, 128], BF16) for i in range(3)]
                acc = sbl("acc", [128, 2, SEQ])
                rden = sbl("rden", [128, SEQ])
                outst = sbl("outst", [128, SEQ], BF16)
                Bt = [[sbl(f"Bt{pi}_{hl}", [128, 256], BF16) for hl in range(HPC)] for pi in range(3)]
                Hf = [sbl(f"Hf{i}", [128, 256]) for i in range(2)]
                pT = [sbl(f"pT{i}", [128, 256], BF16) for i in range(2)]
                rb = sbl("rb", [32, 8])
                oh = sbl("oh", [32, 3 * 129])
                zt = sbl("zt", [8, 3, 384])
                cdma(rb[:, 0:HPC], W["rel_bias"], "rb")
                cdma(oh[:], oh_d, "oh")
                P.op("pe", lambda: pe.matmul(pb[0][0:HPC, 0:387], lhsT=rb[:, 0:HPC], rhs=oh[:, :], start=True, stop=True),
                     reads=["rb", "oh"], writes=[("ps", 0)])
                P.op("pool", lambda: pool.memset(zt[:], NEG), writes=["zt"])
                P.op("dve", lambda: dve.tensor_copy(out=zt[0:HPC, :, 127:256], in_=pb[0][0:HPC, 0:387].rearrange("p (a b) -> p a b", a=3)),
                     reads=[("ps", 0)], writes=["zt"])
                P.op("sp", lambda: sp.dma_start(out=_dap(zpad_h, 0, [[3 * 384, 8], [384, 3], [1, 384]]), in_=zt[:]),
                     reads=["zt"], writes=["zpad"], dma="scr")
                hbase = 0
                for hl in range(HPC):
                    for pi in range(3):
                        i = (hl * 3 + pi) % 2
                        src = _dap(zpad_h, ((hbase + hl) * 3 + pi) * 384, [[1, 128], [1, 256]])
                        P.op("sp", lambda i=i, src=src: sp.dma_start(out=Hf[i][:], in_=src), reads=["zpad"],
                             writes=[("Hf", i)], dma="ld")
                        P.op("act", lambda i=i, hl=hl, pi=pi: act.copy(out=Bt[pi][hl][:], in_=Hf[i][:]),
                             reads=[("Hf", i)], writes=[("Bt", pi, hl)])
                nblk = 0
                for hl in range(HPC):
                    hg = hbase + hl
                    P.op("sp", lambda hl=hl: sp.dma_start(out=qTh[:, :], in_=mine["q"].ap()[hl * 128:(hl + 1) * 128, :]),
                         reads=["projM"], writes=["qTh"], dma="ld")
                    P.op("sp", lambda hl=hl: sp.dma_start(out=kTh[:, :], in_=mine["k"].ap()[hl * 128:(hl + 1) * 128, :]),
                         reads=["projM"], writes=["kTh"], dma="ld")
                    for pi, d in enumerate(DILS):
                        nm = 32 // d
                        for res in range(d):
                            for m0 in range(0, nm, 8):
                                mm = min(8, nm - m0)
                                t_0 = m0 * 128 * d + res
                                bi = res * nm + m0
                                P.op("sp", lambda pi=pi, bi=bi, mm=mm, t_0=t_0, d=d, hl=hl: sp.dma_start(
                                    out=vb[pi][:, bi:bi + mm, :],
                                    in_=mine["v"].ap()[t_0:t_0 + (mm * 128 - 1) * d + 1:d, hl * 128:(hl + 1) * 128]
                                    .rearrange("(m j) e -> j m e", j=128)),
                                     reads=["projM"], writes=[("vb", pi)], dma="ld")
                    blocks = []
                    for pi, d in enumerate(DILS):
                        nm = 32 // d
                        for res in range(d):
                            for n in range(nm):
                                blocks.append((pi, d, nm, res, n, nblk))
                                nblk += 1

                    def stA(blk, hl=hl):
                        pi, d, nm, res, n, ib = blk
                        off = n * 128 * d + res
                        qa = qTh[:, off:off + 127 * d + 1:d]
                        ka = kTh[:, off:off + 127 * d + 1:d]
                        bS = PT[ib % 2]
                        ip = ib % 2
                        Wd_ = 256 if n > 0 else 128
                        P.op("pe", lambda: pe.matmul(pb[bS][:, 0:Wd_], lhsT=jrev[:], rhs=Bt[pi][hl][:, 0:Wd_], start=True, stop=False),
                             reads=["jrev", ("Bt", pi, hl)], writes=[("ps", bS)])
                        P.op("pe", lambda: pe.matmul(pb[bS][:, 0:128], lhsT=ka, rhs=qa, start=False, stop=(n == 0)),
                             reads=["qTh", "kTh"], writes=[("ps", bS)])
                        if n > 0:
                            offp = off - 128 * d
                            kp = kTh[:, offp:offp + 127 * d + 1:d]
                            P.op("pe", lambda: pe.matmul(pb[bS][:, 128:256], lhsT=kp, rhs=qa, start=False, stop=True),
                                 reads=["qTh", "kTh"], writes=[("ps", bS)])
                        P.op("act", lambda: act.activation(out=pT[ip][:, 0:Wd_], in_=pb[bS][:, 0:Wd_], func=AF.Exp),
                             reads=[("ps", bS)], writes=[("pT", ip)])

                    def stB(blk):
                        pi, d, nm, res, n, ib = blk
                        off = n * 128 * d + res
                        bO = PG[ib % 2]
                        ip = ib % 2
                        bi = res * nm + n
                        P.op("pe", lambda: pe.matmul(pb[bO][:, 0:128], lhsT=vb[pi][:, bi, :], rhs=pT[ip][:, 0:128], start=True, stop=(n == 0)),
                             reads=[("vb", pi), ("pT", ip)], writes=[("ps", bO)])
                        if n > 0:
                            P.op("pe", lambda: pe.matmul(pb[bO][:, 0:128], lhsT=vb[pi][:, bi - 1, :], rhs=pT[ip][:, 128:256], start=False, stop=True),
                                 reads=[("vb", pi), ("pT", ip)], writes=[("ps", bO)])
                        P.op("pe", lambda: pe.matmul(pb[bO][:, 128:256], lhsT=ones[:], rhs=pT[ip][:, 0:128], start=True, stop=(n == 0)),
                             reads=["ones", ("pT", ip)], writes=[("ps", bO)])
                        if n > 0:
                            P.op("pe", lambda: pe.matmul(pb[bO][:, 128:256], lhsT=ones[:], rhs=pT[ip][:, 128:256], start=False, stop=True),
                                 reads=["ones", ("pT", ip)], writes=[("ps", bO)])
                        av = acc[:, :, off:off + 127 * d + 1:d]
                        pv = pb[bO][:, 0:256].rearrange("p (a b) -> p a b", a=2)
                        if pi == 0:
                            P.op("dve", lambda: dve.tensor_copy(out=av, in_=pv), reads=[("ps", bO)], writes=[("acc", 0, n)])
                        else:
                            if pi == 1:
                                prev = [("acc", 0, 4 * n + k_) for k_ in range(4)]
                            else:
                                prev = [("acc", 1, r_, 4 * n + k_) for r_ in range(4) for k_ in range(4)]
                            P.op("dve", lambda: dve.tensor_tensor(out=av, in0=pv, in1=av, op=ALU.add),
                                 reads=[("ps", bO)] + prev, writes=[("acc", pi, res, n)])

                    for i_ in range(len(blocks) + 1):
                        if i_ < len(blocks):
                            stA(blocks[i_])
                        if i_ >= 1:
                            stB(blocks[i_ - 1])
                    accall = [("acc", 2, r_, n_) for r_ in range(16) for n_ in range(2)]
                    P.op("dve", lambda: dve.reciprocal(out=rden[:], in_=acc[:, 1, :]), reads=accall, writes=["rden"])
                    P.op("pool", lambda: pool.tensor_tensor(out=outst[:], in0=acc[:, 0, :], in1=rden[:], op=ALU.mult),
                         reads=accall + ["rden"], writes=["outst"] + [("acc", 0, n_) for n_ in range(32)]
                         + [("acc", 1, r_, n_) for r_ in range(4) for n_ in range(8)] + accall)
                    if npair == 1:
                        P.op("sp", lambda hl=hl: sp.dma_start(out=mix_h.ap()[hl * 128:(hl + 1) * 128, :], in_=outst[:]),
                             reads=["outst"], writes=["mix"], dma="scr")
                    else:
                        for j in range(2):
                            P.op("sp", lambda hl=hl, j=j: sp.dma_start(out=bv(mixA[j][0])[hl * 128:(hl + 1) * 128, :],
                                                                       in_=outst[:, j * ntok:(j + 1) * ntok]),
                                 reads=["outst"], writes=["mix"], dma="scr")
                P.barrier()


        def ssm_phase():
            S = GPC * 64
            TWO_PI = float(2 * np.pi)
            with ExitStack() as st:
                sbl = mk_sb(st, "S_")
                Tm_re = sbl("Tm_re", [128, S]); Tm_im = sbl("Tm_im", [128, S])
                Tp_re = sbl("Tp_re", [128, NP2 * 128]); Tp_im = sbl("Tp_im", [128, NP2 * 128])
                t128_re = sbl("t128_re", [128, NP2]); t128_im = sbl("t128_im", [128, NP2])
                Bblk_re = [sbl(f"Bblk_re{i}", [128, 512], BF16) for i in range(NUC)]
                Bblk_im = [sbl(f"Bblk_im{i}", [128, 512], BF16) for i in range(NUC)]
                Cre = sbl("Cre", [128, NP2, 2, 16], BF16); nCre = sbl("nCre", [128, NP2, 2, 16], BF16)
                nCim = sbl("nCim", [128, NP2, 2, 16], BF16)
                d_col = sbl("d_col", [128, NUC])
                iota_c = sbl("iota_c", [128, 1]); iota_r = sbl("iota_r", [128, 128])
                tri = sbl("tri", [128, 128], BF16); ntri = sbl("ntri", [128, 128], BF16)
                mask2 = sbl("mask2", [128, 2]); nmask2 = sbl("nmask2", [128, 2]); mask3 = sbl("mask3", [128, 4])
                sel = sbl("sel", [32, 32, 128])
                inj_re = sbl("inj_re", [128, NP2]); inj_im = sbl("inj_im", [128, NP2])
                injT_re = sbl("injT_re", [32, 128]); injT_im = sbl("injT_im", [32, 128])
                for t_, d_, r_ in ((iota_c, iotac_d, "iota_c"), (iota_r, iotar_d, "iota_r"), (tri, tri_d, "tri"),
                                   (ntri, ntri_d, "ntri"), (mask2, mask2_d, "mask2"), (nmask2, nmask2_d, "nmask2"),
                                   (mask3, mask3_d, "mask3"), (sel, sel_d, "sel")):
                    cdma(t_[:], d_, r_)
                cdma(d_col[:], SS["ssm_d"].rearrange("(c p) -> p c", p=128), "d_col")

                with ExitStack() as st2:
                    sb2 = mk_sb(st2, "S2_")
                    BLK = 1024
                    tmp = {n: sb2("tg_" + n, [128, BLK]) for n in ("t", "fr", "m", "cosv", "sinv", "mag")}
                    tint = sb2("tg_int", [128, BLK], mybir.dt.int32)

                    def dv(fn, reads, writes):
                        P.op("dve", fn, reads=reads, writes=writes)

                    def trig(out_re, out_im, ang, marg, n, np_, sign, rres, wres):
                        for c0 in range(0, n, BLK):
                            w = min(BLK, n - c0)
                            cs = slice(c0, c0 + w)
                            T = {k: v[0:np_, 0:w] for k, v in tmp.items()}
                            ti = tint[0:np_, 0:w]
                            dv(lambda T=T, cs=cs: dve.tensor_scalar(out=T["t"], in0=ang[:, cs], scalar1=1.0 / TWO_PI, scalar2=None, op0=ALU.mult),
                               rres, ["tg_t"])
                            for name, shift in (("cosv", 0.25), ("sinv", 0.0)):
                                dv(lambda T=T, shift=shift: dve.tensor_scalar(out=T["fr"], in0=T["t"], scalar1=shift, scalar2=None, op0=ALU.add),
                                   ["tg_t"], ["tg_fr"])
                                dv(lambda T=T, ti=ti: dve.tensor_copy(out=ti, in_=T["fr"]), ["tg_fr"], ["tg_i"])
                                dv(lambda T=T, ti=ti: dve.tensor_copy(out=T["m"], in_=ti), ["tg_i"], ["tg_m"])
                                dv(lambda T=T: dve.tensor_tensor(out=T["fr"], in0=T["fr"], in1=T["m"], op=ALU.subtract), ["tg_fr", "tg_m"], ["tg_fr"])
                                dv(lambda T=T: dve.tensor_scalar(out=T["m"], in0=T["fr"], scalar1=0.5, scalar2=None, op0=ALU.is_gt), ["tg_fr"], ["tg_m"])
                                dv(lambda T=T: dve.tensor_tensor(out=T["fr"], in0=T["fr"], in1=T["m"], op=ALU.subtract), ["tg_fr", "tg_m"], ["tg_fr"])
                                dv(lambda T=T: dve.tensor_scalar(out=T["m"], in0=T["fr"], scalar1=-0.5, scalar2=None, op0=ALU.is_lt), ["tg_fr"], ["tg_m"])
                                dv(lambda T=T: dve.tensor_tensor(out=T["fr"], in0=T["fr"], in1=T["m"], op=ALU.add), ["tg_fr", "tg_m"], ["tg_fr"])
                                P.op("act", lambda T=T, name=name: act.activation(out=T[name], in_=T["fr"], func=AF.Sin, scale=TWO_PI),
                                     reads=["tg_fr"], writes=["tg_" + name])
                            P.op("act", lambda T=T, cs=cs: act.activation(out=T["mag"], in_=marg[:, cs], func=AF.Exp, scale=float(sign)),
                                 reads=rres, writes=["tg_mag"])
                            dv(lambda T=T, cs=cs: dve.tensor_tensor(out=out_re[:, cs], in0=T["mag"], in1=T["cosv"], op=ALU.mult),
                               ["tg_mag", "tg_cosv"], wres)
                            dv(lambda T=T, cs=cs: dve.scalar_tensor_tensor(out=out_im[:, cs], in0=T["mag"], scalar=float(sign), in1=T["sinv"],
                                                                           op0=ALU.mult, op1=ALU.mult),
                               ["tg_mag", "tg_sinv"], wres)

                    col = lambda n: sb2(n, [128, NP2])
                    lre, lim, ldt, alpha, theta = col("lre"), col("lim"), col("ldt"), col("alpha"), col("theta")
                    a_re, a_im, cf_re, cf_im, w1, w2 = col("a_re"), col("a_im"), col("cf_re"), col("cf_im"), col("w1"), col("w2")
                    al128, th128 = col("al128"), col("th128")
                    cdma(lre[:], SS["ssm_lambda_re"].rearrange("(q p) -> p q", p=128), "lre")
                    cdma(lim[:], SS["ssm_lambda_im"].rearrange("(q p) -> p q", p=128), "lim")
                    ldt_h = SS_h["ssm_log_dt"]
                    cdma(ldt[0:64, :], _dap(ldt_h, 0, [[0, 64], [2, NP2]]), "ldt")
                    cdma(ldt[64:128, :], _dap(ldt_h, 1, [[0, 64], [2, NP2]]), "ldt")
                    P.op("act", lambda: act.activation(out=ldt[:], in_=ldt[:], func=AF.Exp), reads=["ldt"], writes=["ldt"])
                    dv(lambda: dve.tensor_tensor(out=alpha[:], in0=lre[:], in1=ldt[:], op=ALU.mult), ["lre", "ldt"], ["alpha"])
                    dv(lambda: dve.tensor_tensor(out=theta[:], in0=lim[:], in1=ldt[:], op=ALU.mult), ["lim", "ldt"], ["theta"])
                    trig(a_re, a_im, theta, alpha, NP2, 128, 1.0, ["theta", "alpha"], ["a"])
                    dv(lambda: dve.tensor_scalar(out=a_re[:], in0=a_re[:], scalar1=-1.0, scalar2=None, op0=ALU.add), ["a"], ["a"])
                    dv(lambda: dve.tensor_tensor(out=w1[:], in0=lre[:], in1=lre[:], op=ALU.mult), ["lre"], ["w1"])
                    dv(lambda: dve.tensor_tensor(out=w2[:], in0=lim[:], in1=lim[:], op=ALU.mult), ["lim"], ["w2"])
                    dv(lambda: dve.tensor_tensor(out=w1[:], in0=w1[:], in1=w2[:], op=ALU.add), ["w1", "w2"], ["w1"])
                    dv(lambda: dve.reciprocal(out=w1[:], in_=w1[:]), ["w1"], ["w1"])
                    dv(lambda: dve.tensor_tensor(out=cf_re[:], in0=a_re[:], in1=lre[:], op=ALU.mult), ["a", "lre"], ["cf_re"])
                    dv(lambda: dve.tensor_tensor(out=w2[:], in0=a_im[:], in1=lim[:], op=ALU.mult), ["a", "lim"], ["w2"])
                    dv(lambda: dve.tensor_tensor(out=cf_re[:], in0=cf_re[:], in1=w2[:], op=ALU.add), ["cf_re", "w2"], ["cf_re"])
                    dv(lambda: dve.tensor_tensor(out=cf_re[:], in0=cf_re[:], in1=w1[:], op=ALU.mult), ["cf_re", "w1"], ["cf_re"])
                    dv(lambda: dve.tensor_tensor(out=cf_im[:], in0=a_im[:], in1=lre[:], op=ALU.mult), ["a", "lre"], ["cf_im"])
                    dv(lambda: dve.tensor_tensor(out=w2[:], in0=a_re[:], in1=lim[:], op=ALU.mult), ["a", "lim"], ["w2"])
                    dv(lambda: dve.tensor_tensor(out=cf_im[:], in0=cf_im[:], in1=w2[:], op=ALU.subtract), ["cf_im", "w2"], ["cf_im"])
                    dv(lambda: dve.tensor_tensor(out=cf_im[:], in0=cf_im[:], in1=w1[:], op=ALU.mult), ["cf_im", "w1"], ["cf_im"])
                    dv(lambda: dve.tensor_scalar(out=al128[:], in0=alpha[:], scalar1=128.0, scalar2=None, op0=ALU.mult), ["alpha"], ["al128"])
                    dv(lambda: dve.tensor_scalar(out=th128[:], in0=theta[:], scalar1=128.0, scalar2=None, op0=ALU.mult), ["theta"], ["th128"])
                    trig(t128_re, t128_im, th128, al128, NP2, 128, 1.0, ["th128", "al128"], ["t128"])
                    angp = sb2("angp", [128, S]); margp = sb2("margp", [128, S])
                    for q in range(NP2):
                        dv(lambda q=q: dve.tensor_scalar(out=angp[:, q * 128:(q + 1) * 128], in0=iota_r[:], scalar1=theta[:, q:q + 1],
                                                         scalar2=None, op0=ALU.mult), ["iota_r", "theta"], ["angm"])
                        P.op("pool", lambda q=q: pool.tensor_scalar(out=margp[:, q * 128:(q + 1) * 128], in0=iota_r[:], scalar1=alpha[:, q:q + 1],
                                                                    scalar2=None, op0=ALU.mult), reads=["iota_r", "alpha"], writes=["margm"])
                    trig(Tp_re, Tp_im, angp, margp, NP2 * 128, 128, 1.0, ["angm", "margm"], ["Tp"])
                    P.op("sp", lambda: sp.dma_start(out=_dap(prm_h, 0, [[1, 128], [128, NP2]]), in_=theta[:]), reads=["theta"], writes=["prm"], dma="scr")
                    P.op("sp", lambda: sp.dma_start(out=_dap(prm_h, S, [[1, 128], [128, NP2]]), in_=alpha[:]), reads=["alpha"], writes=["prm"], dma="scr")
                    angm, margm = angp, margp
                    P.op("sp", lambda: sp.dma_start(out=angm[:], in_=_dap(prm_h, 0, [[0, 128], [1, S]])), reads=["prm"], writes=["angm"], dma="ld")
                    P.op("sp", lambda: sp.dma_start(out=margm[:], in_=_dap(prm_h, S, [[0, 128], [1, S]])), reads=["prm"], writes=["margm"], dma="ld")
                    dv(lambda: dve.tensor_scalar(out=angm[:], in0=angm[:], scalar1=iota_c[:, 0:1], scalar2=None, op0=ALU.mult), ["angm", "iota_c"], ["angm"])
                    P.op("pool", lambda: pool.tensor_scalar(out=margm[:], in0=margm[:], scalar1=iota_c[:, 0:1], scalar2=None, op0=ALU.mult),
                         reads=["margm", "iota_c"], writes=["margm"])
                    trig(Tm_re, Tm_im, angm, margm, S, 128, -1.0, ["angm", "margm"], ["Tm"])
                    Bn_re = sb2("Bn_re", [128, NP2, 16]); Bn_im = sb2("Bn_im", [128, NP2, 16])
                    tA = sb2("tA", [128, NP2, 16]); tB = sb2("tB", [128, NP2, 16])
                    Bb_re = sb2("Bb_re", [128, NP2, 16]); Bb_im = sb2("Bb_im", [128, NP2, 16])
                    cdma(Bn_re[:], SS["ssm_b_re"].rearrange("(q p c) -> p q c", p=128, c=16), "Bn_re")
                    cdma(Bn_im[:], SS["ssm_b_im"].rearrange("(q p c) -> p q c", p=128, c=16), "Bn_im")
                    for (dst, x1_, c1_, x2_, c2_, op_) in ((Bb_re, Bn_re, cf_re, Bn_im, cf_im, ALU.subtract),
                                                           (Bb_im, Bn_im, cf_re, Bn_re, cf_im, ALU.add)):
                        for c in range(16):
                            dv(lambda c=c, x1_=x1_, c1_=c1_: dve.tensor_tensor(out=tA[:, :, c], in0=x1_[:, :, c], in1=c1_[:], op=ALU.mult),
                               ["Bn_re", "Bn_im", "cf_re", "cf_im"], ["tA"])
                            dv(lambda c=c, x2_=x2_, c2_=c2_: dve.tensor_tensor(out=tB[:, :, c], in0=x2_[:, :, c], in1=c2_[:], op=ALU.mult),
                               ["Bn_re", "Bn_im", "cf_re", "cf_im"], ["tB"])
                        dv(lambda dst=dst, op_=op_: dve.tensor_tensor(out=dst[:], in0=tA[:], in1=tB[:], op=op_), ["tA", "tB"], ["Bb"])
                    src_t = sb2("src_t", [128, 4, 2, 16])
                    for Bb, Bblk in ((Bb_re, Bblk_re), (Bb_im, Bblk_im)):
                        for ch in range(NUC):
                            for g2 in range(2):
                                dv(lambda Bb=Bb, ch=ch, g2=g2: dve.tensor_scalar(out=src_t[:, :, g2, :], in0=Bb[:, 4 * ch:4 * ch + 4, :],
                                                                                 scalar1=mask2[:, g2:g2 + 1], scalar2=None, op0=ALU.mult),
                                   ["Bb", "mask2"], ["src_t"])
                            P.op("pe", lambda: pe.transpose(out=pb[7][:, 0:128], in_=src_t[:].rearrange("p a b c -> p (a b c)"), identity=ident[:]),
                                 reads=["src_t", "ident"], writes=[("ps", 7)])
                            for q4 in range(4):
                                dv(lambda Bblk=Bblk, ch=ch, q4=q4: dve.tensor_scalar(out=Bblk[ch][:, q4 * 128:(q4 + 1) * 128], in0=pb[7][:, 0:128],
                                                                                     scalar1=mask3[:, q4:q4 + 1], scalar2=None, op0=ALU.mult),
                                   [("ps", 7), "mask3"], [("Bblk", ch)])
                    Cd = sb2("Cd", [128, 2, 64])
                    for name, outs in (("ssm_c_re", ((Cre, mask2), (nCre, nmask2))), ("ssm_c_im", ((nCim, nmask2),))):
                        cv = SS[name].rearrange("(c p n) -> c p n", p=128, n=64)
                        for ch in range(NUC):
                            cdma(Cd[:, 0, :], cv[ch], "Cd")
                            cdma(Cd[:, 1, :], cv[ch], "Cd")
                            P.op("pe", lambda: pe.transpose(out=pb[7][:, 0:128], in_=Cd[:].rearrange("p a b -> p (a b)"), identity=ident[:]),
                                 reads=["Cd", "ident"], writes=[("ps", 7)])
                            trv = pb[7][:, 0:128].rearrange("p (a b c) -> p a b c", a=4, b=2)
                            for dst, mk in outs:
                                for g2 in range(2):
                                    dv(lambda dst=dst, mk=mk, ch=ch, g2=g2, trv=trv: dve.tensor_scalar(
                                        out=dst[:, 4 * ch:4 * ch + 4, g2, :], in0=trv[:, :, g2, :], scalar1=mk[:, g2:g2 + 1],
                                        scalar2=None, op0=ALU.mult), [("ps", 7), "mask2", "nmask2"], ["Cw"])
                    P.barrier()

                uTc = [sbl(f"uTc{i}", [128, NUC, 512], BF16) for i in range(2)]
                yst = sbl("yst", [128, NUC, 512])
                dm = [[sbl(f"dm{i}_{j}", [128, 512], BF16) for j in range(4)] for i in range(2)]
                rm = [[sbl(f"rm{i}_{j}", [128, 512], BF16) for j in range(4)] for i in range(2)]
                tn = [sbl(f"tn{i}", [128, 4]) for i in range(4)]
                gt = [sbl(f"gt{i}", [128, 512]) for i in range(3)]
                gout = [sbl(f"gout{i}", [128, 512], BF16) for i in range(2)]
                yst2 = sbl("yst2", [128, NUC, 512])
                ysts = [yst, yst2]
                XR, XI, YB = 4, 5, 6
                NCH = SEQ // 128

                def s0_BU(g):
                    ub, ch, ts, bre, bim = g["ub"], g["ch"], g["ts"], g["bre"], g["bim"]
                    P.op("pe", lambda: pe.matmul(pb[bre][:, :], lhsT=uTc[ub][:, ch, ts], rhs=Bblk_re[ch][:], start=True, stop=True),
                         reads=[("uTc", ub), ("Bblk", ch)], writes=[("ps", bre)])
                    P.op("pe", lambda: pe.matmul(pb[bim][:, :], lhsT=uTc[ub][:, ch, ts], rhs=Bblk_im[ch][:], start=True, stop=True),
                         reads=[("uTc", ub), ("Bblk", ch)], writes=[("ps", bim)])

                def s1_demod(g):
                    i2, ch, bre, bim = g["i2"], g["ch"], g["bre"], g["bim"]
                    tsl = slice(ch * 512, (ch + 1) * 512)
                    A_, B_, C_, D_ = dm[i2]
                    for dst, tab, bsrc, k_ in ((A_, Tm_re, bre, 0), (B_, Tm_im, bim, 1), (C_, Tm_re, bim, 2), (D_, Tm_im, bre, 3)):
                        P.op("dve", lambda dst=dst, tab=tab, bsrc=bsrc: dve.tensor_tensor(out=dst[:], in0=pb[bsrc][:, :], in1=tab[:, tsl], op=ALU.mult),
                             reads=[("ps", bsrc), "Tm"], writes=[("dm", i2, k_)])

                def s2_cumsum(g):
                    i2, ch, c = g["i2"], g["ch"], g["c"]
                    A_, B_, C_, D_ = dm[i2]
                    for q4 in range(4):
                        q = 4 * ch + q4
                        cs = slice(q4 * 128, (q4 + 1) * 128)
                        for (xb, m1, k1, m2, k2, rhs2, injT, inm) in ((XR, A_, 0, B_, 1, ntri, injT_re, "injT_re"), (XI, C_, 2, D_, 3, tri, injT_im, "injT_im")):
                            P.op("pe", lambda xb=xb, m1=m1, cs=cs: pe.matmul(pb[xb][:, cs], lhsT=m1[:, cs], rhs=tri[:], start=True, stop=False),
                                 reads=[("dm", i2, k1), "tri"], writes=[("ps", xb)])
                            P.op("pe", lambda xb=xb, m2=m2, cs=cs, rhs2=rhs2: pe.matmul(pb[xb][:, cs], lhsT=m2[:, cs], rhs=rhs2[:], start=False, stop=(c == 0)),
                                 reads=[("dm", i2, k2), "tri", "ntri"], writes=[("ps", xb)])
                            if c > 0:
                                P.op("pe", lambda xb=xb, cs=cs, injT=injT, q=q: pe.matmul(pb[xb][:, cs], lhsT=injT[0:NP2, :], rhs=sel[0:NP2, q, :], start=False, stop=True),
                                     reads=[inm, "sel"], writes=[("ps", xb)])

                def s3_remod(g):
                    i2, ch, c = g["i2"], g["ch"], g["c"]
                    last = c >= NCH - 1
                    qs = slice(4 * ch, 4 * ch + 4)
                    if not last:
                        xr = pb[XR][:, 127:512:128]
                        xi = pb[XI][:, 127:512:128]
                        P.op("dve", lambda: dve.tensor_tensor(out=tn[0][:], in0=xr, in1=t128_re[:, qs], op=ALU.mult), reads=[("ps", XR), "t128"], writes=["tn0"])
                        P.op("dve", lambda: dve.tensor_tensor(out=tn[1][:], in0=xi, in1=t128_im[:, qs], op=ALU.mult), reads=[("ps", XI), "t128"], writes=["tn1"])
                        P.op("dve", lambda: dve.tensor_tensor(out=tn[2][:], in0=xi, in1=t128_re[:, qs], op=ALU.mult), reads=[("ps", XI), "t128"], writes=["tn2"])
                        P.op("dve", lambda: dve.tensor_tensor(out=tn[3][:], in0=xr, in1=t128_im[:, qs], op=ALU.mult), reads=[("ps", XR), "t128"], writes=["tn3"])
                    E1, E2, E3, E4 = rm[i2]
                    tps = slice(4 * ch * 128, (4 * ch + 4) * 128)
                    for dst, tab, xsrc, k_ in ((E1, Tp_re, XR, 0), (E2, Tp_im, XI, 1), (E3, Tp_re, XI, 2), (E4, Tp_im, XR, 3)):
                        P.op("dve", lambda dst=dst, tab=tab, xsrc=xsrc: dve.tensor_tensor(out=dst[:], in0=pb[xsrc][:, :], in1=tab[:, tps], op=ALU.mult),
                             reads=[("ps", xsrc), "Tp"], writes=[("rm", i2, k_)])
                    if not last:
                        P.op("dve", lambda: dve.tensor_tensor(out=inj_re[:, qs], in0=tn[0][:], in1=tn[1][:], op=ALU.subtract), reads=["tn0", "tn1"], writes=[("inj_re", ch)])
                        P.op("dve", lambda: dve.tensor_tensor(out=inj_im[:, qs], in0=tn[2][:], in1=tn[3][:], op=ALU.add), reads=["tn2", "tn3"], writes=[("inj_im", ch)])
                        if ch == NUC - 1:
                            for src_, dstT, inm, rn in ((inj_re, injT_re, "injT_re", "inj_re"), (inj_im, injT_im, "injT_im", "inj_im")):
                                P.op("pe", lambda src_=src_: pe.transpose(out=pb[7][0:NP2, 0:128], in_=src_[:, :], identity=ident[:]),
                                     reads=[(rn, k_) for k_ in range(NUC)] + ["ident"], writes=[("ps", 7)])
                                P.op("act", lambda dstT=dstT: act.copy(out=dstT[0:NP2, :], in_=pb[7][0:NP2, 0:128]), reads=[("ps", 7)], writes=[inm])

                def s4_y(g):
                    i2, ch, yk = g["i2"], g["ch"], g["yk"]
                    E1, E2, E3, E4 = rm[i2]
                    for q4 in range(4):
                        q = 4 * ch + q4
                        cs = slice(q4 * 128, (q4 + 1) * 128)
                        yo = pb[YB + yk][32 * q4:32 * q4 + 32, 0:128]
                        for wi, (wt, et, k_) in enumerate(((Cre, E1, 0), (nCre, E2, 1), (nCim, E3, 2), (nCim, E4, 3))):
                            P.op("pe", lambda yo=yo, wt=wt, et=et, q=q, cs=cs, wi=wi, q4=q4: pe.matmul(
                                yo, lhsT=wt[:, q, :, :].rearrange("p a b -> p (a b)"), rhs=et[:, cs], start=(wi == 0), stop=(wi == 3),
                                tile_position=(0, 32 * q4)),
                                 reads=["Cw", ("rm", i2, k_)], writes=[("ps", YB + yk)])

                def s5_evac(g):
                    ub, ch, ts, yk, yb = g["ub"], g["ch"], g["ts"], g["yk"], g["yb"]
                    P.op("dve", lambda: dve.scalar_tensor_tensor(
                        out=ysts[yb][:, ch, ts], in0=uTc[ub][:, ch, ts], scalar=d_col[:, ch:ch + 1], in1=pb[YB + yk][:, 0:128],
                        op0=ALU.mult, op1=ALU.add),
                         reads=[("uTc", ub), "d_col", ("ps", YB + yk)], writes=[("yst", yb, ch)])
                    if g["sc_last"]:
                        gelu_out(g["sc"], yb)

                ngo = [0]

                def gelu_out(sc, yb):
                    for ch in range(NUC):
                        yv = ysts[yb][:, ch, :]
                        go = ngo[0] % 2
                        ngo[0] += 1
                        P.op("act", lambda yv=yv: act.activation(out=gt[0][:], in_=yv, func=AF.Square), reads=[("yst", yb, ch)], writes=["gt0"])
                        P.op("pool", lambda: pool.tensor_scalar(out=gt[1][:], in0=gt[0][:], scalar1=0.044715, scalar2=1.0, op0=ALU.mult, op1=ALU.add),
                             reads=["gt0"], writes=["gt1"])
                        P.op("pool", lambda yv=yv: pool.tensor_tensor(out=gt[1][:], in0=gt[1][:], in1=yv, op=ALU.mult), reads=["gt1", ("yst", yb, ch)], writes=["gt1"])
                        P.op("act", lambda: act.activation(out=gt[2][:], in_=gt[1][:], func=AF.Sigmoid, scale=1.5957691216057308), reads=["gt1"], writes=["gt2"])
                        P.op("pool", lambda yv=yv, go=go: pool.tensor_tensor(out=gout[go][:], in0=gt[2][:], in1=yv, op=ALU.mult),
                             reads=["gt2", ("yst", yb, ch)], writes=[("gout", go)])
                        if npair == 1:
                            dst = mix_h.ap()[HPC * 128 + ch * 128:HPC * 128 + (ch + 1) * 128, sc * 512:(sc + 1) * 512]
                        else:
                            j_, tl_ = divmod(sc * 512, ntok)
                            dst = bv(mixA[j_][1])[ch * 128:(ch + 1) * 128, tl_:tl_ + 512]
                        P.op("sp", lambda go=go, dst=dst: sp.dma_start(out=dst, in_=gout[go][:]), reads=[("gout", go)], writes=["mix"], dma="scr")

                groups = []
                for sc in range(SEQ // 512):
                    for c4 in range(4):
                        for ch in range(NUC):
                            n_ = len(groups)
                            groups.append(dict(sc=sc, ub=sc % 2, yb=sc % 2, c=sc * 4 + c4, ch=ch, ts=slice(c4 * 128, (c4 + 1) * 128),
                                               i2=n_ % 2, bre=(0, 2)[n_ % 2], bim=(1, 3)[n_ % 2], yk=n_ % 2,
                                               sc_first=(c4 == 0 and ch == 0), sc_last=(c4 == 3 and ch == NUC - 1)))
                stages_fn = (s0_BU, s1_demod, s2_cumsum, s3_remod, s4_y, s5_evac)
                for t in range(len(groups) + 5):
                    for k in range(5, -1, -1):
                        gi = t - k
                        if 0 <= gi < len(groups):
                            g = groups[gi]
                            if k == 0 and g["sc_first"]:
                                P.op("sp", lambda ub=g["ub"], sc=g["sc"]: sp.dma_start(
                                    out=uTc[ub][:, :, :],
                                    in_=mine["u"].ap()[:, sc * 512:(sc + 1) * 512].rearrange("(c p) t -> p c t", p=128)),
                                     reads=["projM"], writes=[("uTc", g["ub"])], dma="ld")
                            stages_fn[k](g)
                P.barrier()

        if "A" in stages:
            row_local_phase("A")
        if npair > 1:
            for kd in ("q", "k", "u", "v"):
                for j in range(2):
                    allgather(kd, ownS[kd][j], gatS[kd][j], "projA", "projG")
            for kd in ("q", "k", "u", "v"):
                nr = SEQ if kd == "v" else 1024
                for j in range(2):
                    P.op("sp", lambda kd=kd, j=j, nr=nr: sp.dma_start(out=stg[kd].ap()[j * nr:(j + 1) * nr, :], in_=bv(gatS[kd][j])),
                         reads=["projG"], writes=["projS"], dma="ld")
            for kd in ("q", "k", "u"):
                for rb in range(2):
                    P.op("sp", lambda kd=kd, rb=rb: sp.dma_start(
                        out=mine[kd].ap()[:, rb * ntok:(rb + 1) * ntok], in_=rows_dyn(stg[kd].ap(), rb * 512, 512, 1024)),
                         reads=["projS"], writes=["projM"], dma="ld")
            for hf in range(2):
                P.op("sp", lambda hf=hf: sp.dma_start(
                    out=mine["v"].ap()[hf * 2048:(hf + 1) * 2048, :], in_=rows_dyn(stg["v"].ap(), hf * 2048, 2048, SEQ)),
                     reads=["projS"], writes=["projM"], dma="ld")
            P.barrier()
        if "B" in stages:
            attention_phase()
            if ssm_on:
                ssm_phase()
        if npair > 1:
            for j in range(2):
                for h in range(2):
                    allgather("mix", mixA[j][h], mixGt[j][h], "mix", "mixG")
            for j in range(2):
                for h in range(2):
                    P.op("sp", lambda j=j, h=h: sp.dma_start(
                        out=stg_mix.ap()[j * 2048 + h * 1024:j * 2048 + (h + 1) * 1024, :], in_=bv(mixGt[j][h])),
                         reads=["mixG"], writes=["mixS"], dma="ld")
            for h in range(2):
                for rb in range(2):
                    P.op("sp", lambda h=h, rb=rb: sp.dma_start(
                        out=mixM_h.ap()[rb * 1024 + h * 512:rb * 1024 + (h + 1) * 512, :],
                        in_=rows_dyn(stg_mix.ap(), h * 1024 + rb * 512, 512, 2048)),
                         reads=["mixS"], writes=["mixM"], dma="ld")
            P.barrier()
        if "C" in stages:
            row_local_phase("C")
        P.barrier()
    return nc


_CONSTS = None


def _t5_bucket(dist):
    dist = np.asarray(dist)
    max_exact = 16
    d_f = np.maximum(dist, max_exact).astype(np.float32)
    val = (np.log(d_f / np.float32(max_exact)) / np.float32(np.log(2048 / max_exact)) * np.float32(32 - max_exact))
    large = max_exact + (np.rint(val) if BUCKET_ROUND else val).astype(np.int32)
    large = np.minimum(large, 31)
    return np.where(dist < max_exact, dist, large)


def _consts():
    global _CONSTS
    if _CONSTS is None:
        oh = np.zeros((32, 3 * 129), np.float32)
        for pi, d in enumerate(DILS):
            b = _t5_bucket(np.arange(129) * d)
            oh[b, pi * 129 + np.arange(129)] = 1.0
        _CONSTS = {
            "ident": np.eye(128, dtype=np.float32),
            "ones_bf": np.ones((128, 128), dtype=ml_dtypes.bfloat16),
            "jrev_bf": np.eye(128, dtype=np.float32)[::-1].copy().astype(ml_dtypes.bfloat16),
            "onehot": oh,
            "iota_c": np.arange(128, dtype=np.float32).reshape(128, 1),
            "iota_r": np.tile(np.arange(128, dtype=np.float32), (128, 1)),
            "tri_bf": np.triu(np.ones((128, 128), np.float32)).astype(ml_dtypes.bfloat16),
            "ntri_bf": (-np.triu(np.ones((128, 128), np.float32))).astype(ml_dtypes.bfloat16),
            "mask2": (np.arange(128)[:, None] // 64 == np.arange(2)[None, :]).astype(np.float32),
            "nmask2": -(np.arange(128)[:, None] // 64 == np.arange(2)[None, :]).astype(np.float32),
            "mask3": (np.arange(128)[:, None] // 32 == np.arange(4)[None, :]).astype(np.float32),
            "sel": np.broadcast_to(np.eye(32, dtype=np.float32)[:, :, None], (32, 32, 128)).copy(),
        }
    return _CONSTS


PARAMS = ["ffn1_norm", "ffn1_w_gate", "ffn1_w_up", "ffn1_w_down", "ffn2_norm", "ffn2_w_gate", "ffn2_w_up",
          "ffn2_w_down", "mix_norm", "w_in", "q_norm", "k_norm", "glu_w", "glu_b", "w_out"]


def make_in_maps(inputs, ncores, npair=1):
    x = np.ascontiguousarray(inputs["x"], dtype=np.float32)
    base = dict(_consts())
    for n in PARAMS:
        base[n] = np.ascontiguousarray(inputs[n][0], dtype=np.float32)
    ntok = SEQ // npair
    gpc = N_GROUPS // npair
    in_maps = []
    for c in range(ncores):
        b, r = divmod(c, npair)
        m = dict(base)
        m["x"] = x[b % BATCH, r * ntok:(r + 1) * ntok]
        gs = slice(r * gpc, (r + 1) * gpc)
        for n in ("ssm_lambda_re", "ssm_lambda_im", "ssm_log_dt", "ssm_b_re", "ssm_b_im", "ssm_c_re", "ssm_c_im"):
            m[n] = np.ascontiguousarray(inputs[n][0][gs], dtype=np.float32).reshape(-1)
        m["ssm_d"] = np.ascontiguousarray(inputs["ssm_d"][0][r * gpc * 16:(r + 1) * gpc * 16], dtype=np.float32)
        hpc = N_HEADS // npair
        m["rel_bias"] = np.ascontiguousarray(np.asarray(inputs["rel_bias"], dtype=np.float32)[:, r * hpc:(r + 1) * hpc])
        in_maps.append(m)
    return in_maps


NPAIR = 2


def kernel(**inputs):
    npair = NPAIR
    ncores = BATCH * npair
    nc = build(dict(npair=npair))
    in_maps = make_in_maps(inputs, ncores, npair)
    res = run_bass_kernel_spmd(nc, in_maps, core_ids=list(range(ncores)))
    ntok = SEQ // npair
    out = np.empty((BATCH, SEQ, D), np.float32)
    for c in range(ncores):
        b, r = divmod(c, npair)
        out[b, r * ntok:(r + 1) * ntok] = np.asarray(res.results[c]["out"])
    return out
```

```python
from contextlib import ExitStack
import numpy as np
import ml_dtypes
import concourse.bass as bass
import concourse.mybir as mybir
from concourse.bass_utils import run_bass_kernel_spmd

F32 = mybir.dt.float32
BF16 = mybir.dt.bfloat16
ALU = mybir.AluOpType
AF = mybir.ActivationFunctionType

D = 2048
KD = D // 128
FF = 5632
FC = FF // 128
SEQ = 4096
BATCH = 4
EPS = 1e-6
TT = 1024
NTB = TT // 512
NPARTS = 11
FPP = FC // NPARTS
WSLOT = 128
WDW = 512
NST = 3


class Op:
    __slots__ = ("eng", "fn", "deps", "needed", "dma", "tok")

    def __init__(self, eng, fn, dma):
        self.eng = eng
        self.fn = fn
        self.deps = []
        self.needed = False
        self.dma = dma
        self.tok = None


class Prog:
    ENGS = ("pe", "act", "dve", "pool", "sp")
    LIMIT = 20000

    def __init__(self, nc, stack):
        self.nc = nc
        self.stack = stack
        self.eng = {"pe": nc.tensor, "act": nc.scalar, "dve": nc.vector,
                    "pool": nc.gpsimd, "sp": nc.sync}
        self.ops = []
        self.last_w = {}
        self.readers = {}
        self.cnt = {e: 0 for e in self.ENGS}
        self.esems = {e: [] for e in self.ENGS}
        self.streams = {}
        self.waited = {e: {} for e in self.ENGS}
        self.last_op = {}
        self.nsem = 0

    def _newsem(self, name):
        self.nsem += 1
        return self.stack.enter_context(self.nc.semaphore(f"{name}_{self.nsem}"))

    def stream(self, name, nsems, inc=16):
        self.streams[name] = dict(sems=[self._newsem(name) for _ in range(nsems)], n=0, inc=inc)

    def op(self, eng, fn, reads=(), writes=(), dma=None):
        o = Op(eng, fn, dma)
        deps = {}
        for r in reads:
            w = self.last_w.get(r)
            if w is not None:
                deps[id(w)] = w
        for r in writes:
            w = self.last_w.get(r)
            if w is not None:
                deps[id(w)] = w
            rd = self.readers.get(r)
            if rd:
                for x in rd.values():
                    deps[id(x)] = x
        for p in deps.values():
            if p is o:
                continue
            if p.dma is None and p.eng == "pe" and eng == "pe" and dma is None:
                continue
            p.needed = True
            o.deps.append(p)
        for r in writes:
            self.last_w[r] = o
            self.readers[r] = {}
        key = eng if dma is None else ("dma", dma, len(self.ops))
        for r in reads:
            self.readers.setdefault(r, {})[key if dma is None else id(o)] = o
        self.ops.append(o)
        self.last_op[eng] = o
        return o

    def _wait(self, eng, sem, val):
        k = id(sem)
        w = self.waited[eng]
        if w.get(k, 0) >= val:
            return
        w[k] = val
        self.eng[eng].wait_ge(sem, val)

    def flush(self):
        for o in self.ops:
            for p in o.deps:
                sem, val = p.tok
                self._wait(o.eng, sem, val)
            if o.dma is not None:
                st = self.streams[o.dma]
                n = st["n"]
                st["n"] = n + 1
                R = len(st["sems"])
                inc = st["inc"]
                sem = st["sems"][n % R]
                prev = inc * (n // R)
                if prev > 0:
                    self._wait(o.eng, sem, prev)
                ins = o.fn()
                if inc == 1:
                    ins.then_inc(sem)
                else:
                    ins.then_inc(sem, inc)
                o.tok = (sem, prev + inc)
            else:
                ins = o.fn()
                if o.needed:
                    c = self.cnt[o.eng]
                    ep, v = divmod(c, self.LIMIT)
                    sems = self.esems[o.eng]
                    if ep >= len(sems):
                        sems.append(self._newsem("e" + o.eng))
                    ins.then_inc(sems[ep], 1)
                    self.cnt[o.eng] = c + 1
                    o.tok = (sems[ep], v + 1)
        self.ops = []

    def barrier(self):
        lasts = []
        for e in self.ENGS:
            o = self.last_op.get(e)
            if o is not None and o.dma is None:
                o.needed = True
                lasts.append(o)
        self.flush()
        for e in self.ENGS:
            for o in lasts:
                if o.eng == e and e == "pe":
                    continue
                self._wait(e, *o.tok)
            for st in self.streams.values():
                R = len(st["sems"])
                for i, sem in enumerate(st["sems"]):
                    cnt = (st["n"] - i + R - 1) // R
                    if cnt > 0:
                        self._wait(e, sem, st["inc"] * cnt)
        self.last_w = {}
        self.readers = {}
        self.last_op = {}


NEG = -30000.0
BUCKET_ROUND = False
N_HEADS = 8
N_GROUPS = 64
DILS = (1, 4, 16)


def _dap(h, off, dims):
    return bass.AP(h, off, [list(d) for d in dims])


def build(cfg):
    npair = cfg.get("npair", 1)
    debug = cfg.get("debug", False)
    stages = cfg.get("stages", "ABC")
    ssm_on = cfg.get("ssm", True)
    ntok = SEQ // npair
    NT = ntok // TT
    HPC = N_HEADS // npair
    GPC = N_GROUPS // npair
    NP2 = GPC // 2
    NUC = GPC * 16 // 128
    MIXR = HPC * 128 + GPC * 16
    nc = bass.Bass("TRN2", target_bir_lowering=False)
    SCR = "ExternalOutput" if debug else "Internal"

    def dt(name, shape, dtype=F32, kind="ExternalInput"):
        return nc.dram_tensor(name, shape, dtype, kind=kind)

    x_d = dt("x", [ntok, D]).ap()
    out_d = dt("out", [ntok, D], kind="ExternalOutput").ap()
    ident_d = dt("ident", [128, 128]).ap()
    ones_d = dt("ones_bf", [128, 128], BF16).ap()
    jrev_d = dt("jrev_bf", [128, 128], BF16).ap()
    oh_d = dt("onehot", [32, 3 * 129]).ap()
    shapes = {"ffn1_norm": [D], "ffn1_w_gate": [D, FF], "ffn1_w_up": [D, FF], "ffn1_w_down": [FF, D],
              "ffn2_norm": [D], "ffn2_w_gate": [D, FF], "ffn2_w_up": [D, FF], "ffn2_w_down": [FF, D],
              "mix_norm": [D], "w_in": [D, 4096], "q_norm": [128], "k_norm": [128], "rel_bias": [32, N_HEADS // cfg.get("npair", 1)],
              "glu_w": [1024, 1024], "glu_b": [1024], "w_out": [D, D]}
    W = {n: dt(n, shapes[n]).ap() for n in shapes}
    sshapes = {"ssm_lambda_re": [GPC * 64], "ssm_lambda_im": [GPC * 64], "ssm_log_dt": [GPC], "ssm_b_re": [GPC * 1024],
               "ssm_b_im": [GPC * 1024], "ssm_c_re": [GPC * 1024], "ssm_c_im": [GPC * 1024], "ssm_d": [GPC * 16]}
    SS_h = {n: dt(n, sshapes[n]) for n in sshapes}
    SS = {n: h.ap() for n, h in SS_h.items()}
    iotac_d = dt("iota_c", [128, 1]).ap(); iotar_d = dt("iota_r", [128, 128]).ap()
    tri_d = dt("tri_bf", [128, 128], BF16).ap(); ntri_d = dt("ntri_bf", [128, 128], BF16).ap()
    mask2_d = dt("mask2", [128, 2]).ap(); nmask2_d = dt("nmask2", [128, 2]).ap(); mask3_d = dt("mask3", [128, 4]).ap()
    sel_d = dt("sel", [32, 32, 128]).ap()
    prm_h = dt("prm", [2 * GPC * 64], F32, kind=SCR)
    x1s_h = dt("x1s", [NT * 128 * KD * TT], F32, kind=SCR)
    own = {"q": dt("qA", [1024, ntok], BF16, kind=SCR), "k": dt("kA", [1024, ntok], BF16, kind=SCR),
           "u": dt("uA", [1024, ntok], BF16, kind=SCR), "v": dt("vA", [ntok, 1024], BF16, kind=SCR)} if npair == 1 else None
    mix_h = dt("mix", [MIXR, SEQ], BF16, kind=SCR) if npair == 1 else None
    zpad_h = dt("zpad", [8 * 3 * 384], F32, kind=SCR)
    bv = lambda h: h.ap().bitcast(BF16)
    if npair > 1:
        f32t = lambda name, rows, cols_bf: dt(name, [rows, cols_bf // 2], F32, kind="Internal")
        ownS = {kd: [f32t(f"{kd}A{j}", 512, ntok) for j in range(2)] for kd in ("q", "k", "u")}
        ownS["v"] = [f32t(f"vA{j}", ntok, 512) for j in range(2)]
        gatS = {kd: [f32t(f"{kd}G{j}", 1024, ntok) for j in range(2)] for kd in ("q", "k", "u")}
        gatS["v"] = [f32t(f"vG{j}", 2 * ntok, 512) for j in range(2)]
        stg = {kd: dt(f"{kd}S", [2 * 1024, ntok], BF16, kind="Internal") for kd in ("q", "k", "u")}
        stg["v"] = dt("vS", [2 * SEQ, 512], BF16, kind="Internal")
        mixA = [[f32t(f"mixA{j}{h}", 512, ntok) for h in range(2)] for j in range(2)]
        mixGt = [[f32t(f"mixG{j}{h}", 1024, ntok) for h in range(2)] for j in range(2)]
        stg_mix = dt("mixS", [2 * 2048, ntok], BF16, kind="Internal")
    RANK = (nc.partition_id() % 2) if npair > 1 else 0
    if npair == 1:
        mine = dict(own)
        mixM_h = mix_h
    else:
        mine = {"q": dt("qM", [HPC * 128, SEQ], BF16, kind="Internal"), "k": dt("kM", [HPC * 128, SEQ], BF16, kind="Internal"),
                "u": dt("uM", [GPC * 16, SEQ], BF16, kind="Internal"), "v": dt("vM", [SEQ, HPC * 128], BF16, kind="Internal")}
        mixM_h = dt("mixM", [2 * MIXR, ntok], BF16, kind="Internal")

    def rows_dyn(ap, static, size, mult):
        if npair == 1:
            return ap[static:static + size, :]
        return ap[static:static + mult + size, :][bass.ds(RANK * mult, size), :]

    def cols_dyn(ap, static, size, mult):
        if npair == 1:
            return ap[:, static:static + size]
        return ap[:, static:static + mult + size][:, bass.ds(RANK * mult, size)]

    with ExitStack() as stack:
        P = Prog(nc, stack)
        stack.enter_context(nc.allow_non_contiguous_dma(reason="tiny strided parameter loads"))
        mk_sb = lambda st, pfx='': (lambda name, shape, dtype=F32: st.enter_context(nc.sbuf_tensor(pfx + name, shape, dtype)))
        sb = mk_sb(stack)
        ps = lambda name, shape, dtype=F32: stack.enter_context(nc.psum_tensor(name, shape, dtype))
        sp, act, dve, pool, pe = nc.sync, nc.scalar, nc.vector, nc.gpsimd, nc.tensor

        ident = sb("ident_sb", [128, 128])
        ones = sb("ones_sb", [128, 128], BF16)
        jrev = sb("jrev_sb", [128, 128], BF16)
        g1 = sb("g1", [128, KD])
        g2 = sb("g2", [128, KD])
        g3 = sb("g3", [128, KD])
        gqk = sb("gqk", [128, 2])
        glub = sb("glub", [128, 8])
        pb = [ps(f"pb{i}", [128, 512]) for i in range(8)]
        PT, PG, PU, PD = (0, 1), (2, 3), (4, 5), (6, 7)

        P.stream("xin", 2)
        P.stream("const", 1)
        P.stream("wst", NST)
        P.stream("out", 2)
        P.stream("scr", 4)
        P.stream("ld", 4)
        P.stream("cc", 1, inc=1)
        PAIRS = [[0, 1], [2, 3], [4, 5], [6, 7]]

        def allgather(name, src_h, dst_h, rres, wres):
            P.op("pool", lambda: pool.collective_compute("AllGather", ALU.bypass, replica_groups=PAIRS,
                                                         ins=[src_h.ap().opt()], outs=[dst_h.ap().opt()]),
                 reads=[rres], writes=[wres], dma="cc")

        if npair == 1:
            _op = P.op

            def _op_alias(eng, fn, reads=(), writes=(), dma=None):
                al = lambda rs: [{"projG": "projA", "projM": "projA", "mixG": "mix", "mixM": "mix"}.get(x, x) if isinstance(x, str) else x for x in rs]
                return _op(eng, fn, reads=al(reads), writes=al(writes), dma=dma)
            P.op = _op_alias

        def cdma(dst, src, res):
            P.op("sp", lambda: sp.dma_start(out=dst, in_=src), writes=[res], dma="const")

        cdma(ident[:], ident_d, "ident")
        cdma(ones[:], ones_d, "ones")
        cdma(jrev[:], jrev_d, "jrev")
        cdma(g1[:], W["ffn1_norm"].rearrange("(k p) -> p k", p=128), "g1")
        cdma(g2[:], W["mix_norm"].rearrange("(k p) -> p k", p=128), "g2")
        cdma(g3[:], W["ffn2_norm"].rearrange("(k p) -> p k", p=128), "g3")
        cdma(gqk[:, 0:1], W["q_norm"].rearrange("(p o) -> p o", o=1), "gqk")
        cdma(gqk[:, 1:2], W["k_norm"].rearrange("(p o) -> p o", o=1), "gqk")
        cdma(glub[:], W["glu_b"].rearrange("(k p) -> p k", p=128), "glub")
        P.op("dve", lambda: dve.tensor_scalar(out=gqk[:, 0:1], in0=gqk[:, 0:1], scalar1=float(128 ** -0.5),
                                              scalar2=None, op0=ALU.mult), reads=["gqk"], writes=["gqk"])

        cnt = {"st": 0, "cp": 0, "xs": 0, "sq": 0, "wg": 0, "wd": 0, "pg": 0, "pd": 0, "pt": 0, "sg": 0,
               "qst": 0, "vst": 0}

        def evac_copy(out_ap, in_ap, reads, writes):
            cnt["cp"] += 1
            if cnt["cp"] % 2:
                P.op("act", lambda: act.copy(out=out_ap, in_=in_ap), reads=reads, writes=writes)
            else:
                P.op("dve", lambda: dve.tensor_copy(out=out_ap, in_=in_ap), reads=reads, writes=writes)

        def row_local_phase(phase):
            with ExitStack() as st:
                sbl = mk_sb(st, phase + "_")
                xT = sbl("xT", [128, KD, TT])
                hnT = sbl("hnT", [128, KD, TT], BF16)
                HT = sbl("HT", [128, FPP, TT], BF16)
                xs = [sbl(f"xs{i}", [128, D]) for i in range(2)]
                wg = [sbl(f"wg{i}", [128, KD, WSLOT], BF16) for i in range(2)]
                wu = [sbl(f"wu{i}", [128, KD, WSLOT], BF16) for i in range(2)]
                wd = [sbl(f"wd{i}", [128, FPP, WDW], BF16) for i in range(2)]
                stage = [sbl(f"stage{i}", [128, KD * WSLOT]) for i in range(NST)]
                sq = [sbl(f"sq{i}", [128, 512], BF16) for i in range(2)]
                sg = [sbl(f"sg{i}", [128, 512]) for i in range(2)]
                rstd = [sbl(f"rstd{i}", [128, 512]) for i in range(NTB)]
                qst = [sbl(f"qst{i}", [128, 512], BF16) for i in range(2)]
                vst = [sbl(f"vst{i}", [128, 4, 128], BF16) for i in range(2)]
                yTb = sbl("yTb", [128, 8, TT], BF16) if phase == "C" else None

                def load_xT(t0):
                    for s in range(TT // 128):
                        slot = cnt["xs"] % 2
                        cnt["xs"] += 1
                        src = x_d[t0 + s * 128:t0 + (s + 1) * 128, :]
                        P.op("sp", lambda slot=slot, src=src: sp.dma_start(out=xs[slot][:], in_=src),
                             writes=[("xs", slot)], dma="xin")
                        for k4 in range(KD // 4):
                            b = PT[cnt["pt"] % 2]
                            cnt["pt"] += 1
                            for j in range(4):
                                k = k4 * 4 + j
                                P.op("pe", lambda b=b, j=j, k=k, slot=slot: pe.transpose(
                                    out=pb[b][:, j * 128:(j + 1) * 128], in_=xs[slot][:, k * 128:(k + 1) * 128],
                                    identity=ident[:]), reads=[("xs", slot), "ident"], writes=[("ps", b)])
                            evac_copy(xT[:, k4 * 4:(k4 + 1) * 4, s * 128:(s + 1) * 128],
                                      pb[b][:, :].rearrange("p (j t) -> p j t", j=4),
                                      [("ps", b)], [("xT", k4 * 4 + j, s // 4) for j in range(4)])

                def store_xT(t0):
                    for s in range(TT // 128):
                        slot = cnt["xs"] % 2
                        cnt["xs"] += 1
                        for k4 in range(KD // 4):
                            b = PT[cnt["pt"] % 2]
                            cnt["pt"] += 1
                            for j in range(4):
                                k = k4 * 4 + j
                                P.op("pe", lambda b=b, j=j, k=k, s=s: pe.transpose(
                                    out=pb[b][:, j * 128:(j + 1) * 128], in_=xT[:, k, s * 128:(s + 1) * 128],
                                    identity=ident[:]), reads=[("xT", k, s // 4), "ident"], writes=[("ps", b)])
                            evac_copy(xs[slot][:, k4 * 512:(k4 + 1) * 512], pb[b][:, :],
                                      [("ps", b)], [("xs", slot)])
                        dst = out_d[t0 + s * 128:t0 + (s + 1) * 128, :]
                        P.op("sp", lambda slot=slot, dst=dst: sp.dma_start(out=dst, in_=xs[slot][:]),
                             reads=[("xs", slot)], dma="out")

                def rsqrt_inplace(t, res):
                    P.op("act", lambda: act.sqrt(out=t, in_=t), reads=[res], writes=[res])
                    P.op("dve", lambda: dve.reciprocal(out=t, in_=t), reads=[res], writes=[res])

                def rmsnorm(g, gname):
                    for tb in range(NTB):
                        b = PD[tb % 2]
                        tsl = slice(tb * 512, (tb + 1) * 512)
                        for k in range(KD):
                            q = cnt["sq"] % 2
                            cnt["sq"] += 1
                            P.op("act", lambda q=q, k=k, tsl=tsl: act.activation(out=sq[q][:], in_=xT[:, k, tsl], func=AF.Square),
                                 reads=[("xT", k, tb)], writes=[("sq", q)])
                            P.op("pe", lambda q=q, k=k, b=b: pe.matmul(pb[b][:, :], lhsT=ones[:], rhs=sq[q][:],
                                                                        start=(k == 0), stop=(k == KD - 1)),
                                 reads=[("sq", q), "ones"], writes=[("ps", b)])
                        P.op("dve", lambda b=b, tb=tb: dve.tensor_scalar(out=rstd[tb][:], in0=pb[b][:, :], scalar1=1.0 / D,
                                                                          scalar2=EPS, op0=ALU.mult, op1=ALU.add),
                             reads=[("ps", b)], writes=[("rstd", tb)])
                        rsqrt_inplace(rstd[tb][:], ("rstd", tb))
                        for k in range(KD):
                            P.op("dve", lambda k=k, tb=tb, tsl=tsl: dve.scalar_tensor_tensor(
                                out=hnT[:, k, tsl], in0=xT[:, k, tsl], scalar=g[:, k:k + 1], in1=rstd[tb][:],
                                op0=ALU.mult, op1=ALU.mult),
                                 reads=[("xT", k, tb), ("rstd", tb), gname], writes=[("hn", k, tb)])

                def wload(dst_ap, src_ap, wres):
                    st_ = cnt["st"] % NST
                    cnt["st"] += 1
                    shp = dst_ap.shape
                    sview = stage[st_][:, 0:shp[1] * shp[2]].rearrange("p (a b) -> p a b", a=shp[1])
                    P.op("sp", lambda: sp.dma_start(out=sview, in_=src_ap), writes=[("stage", st_)], dma="wst")
                    if cnt["st"] % 3 == 0:
                        P.op("act", lambda: act.copy(out=dst_ap, in_=sview), reads=[("stage", st_)], writes=[wres])
                    else:
                        P.op("pool", lambda: pool.tensor_copy(out=dst_ap, in_=sview), reads=[("stage", st_)], writes=[wres])

                def ffn(wgate, wup, wdown):
                    wg_v = wgate.rearrange("(k p) f -> p k f", p=128)
                    wu_v = wup.rearrange("(k p) f -> p k f", p=128)
                    wd_v = wdown.rearrange("(c p) d -> p c d", p=128)
                    for part in range(NPARTS):
                        for j in range(FPP):
                            f0 = (part * FPP + j) * 128
                            slot = cnt["wg"] % 2
                            cnt["wg"] += 1
                            wload(wg[slot][:, :, :], wg_v[:, :, f0:f0 + 128], ("wg", slot))
                            wload(wu[slot][:, :, :], wu_v[:, :, f0:f0 + 128], ("wu", slot))
                            for tb in range(NTB):
                                tsl = slice(tb * 512, (tb + 1) * 512)
                                i = cnt["pg"] % 2
                                cnt["pg"] += 1
                                bg, bu = PG[i], PU[i]
                                for k in range(KD):
                                    P.op("pe", lambda k=k, bg=bg, slot=slot, tsl=tsl: pe.matmul(
                                        pb[bg][:, :], lhsT=wg[slot][:, k, :], rhs=hnT[:, k, tsl],
                                        start=(k == 0), stop=(k == KD - 1)),
                                         reads=[("wg", slot), ("hn", k, tb)], writes=[("ps", bg)])
                                for k in range(KD):
                                    P.op("pe", lambda k=k, bu=bu, slot=slot, tsl=tsl: pe.matmul(
                                        pb[bu][:, :], lhsT=wu[slot][:, k, :], rhs=hnT[:, k, tsl],
                                        start=(k == 0), stop=(k == KD - 1)),
                                         reads=[("wu", slot), ("hn", k, tb)], writes=[("ps", bu)])
                                q = cnt["sg"] % 2
                                cnt["sg"] += 1
                                P.op("act", lambda q=q, bg=bg: act.activation(out=sg[q][:], in_=pb[bg][:, :], func=AF.Silu),
                                     reads=[("ps", bg)], writes=[("sg", q)])
                                P.op("dve", lambda q=q, bu=bu, j=j, tsl=tsl: dve.tensor_tensor(
                                    out=HT[:, j, tsl], in0=sg[q][:], in1=pb[bu][:, :], op=ALU.mult),
                                     reads=[("sg", q), ("ps", bu)], writes=[("H", j, tb)])
                        for dg in range(D // WDW):
                            slot = cnt["wd"] % 2
                            cnt["wd"] += 1
                            wload(wd[slot][:, :, :], wd_v[:, part * FPP:(part + 1) * FPP, dg * WDW:(dg + 1) * WDW], ("wd", slot))
                            for di in range(WDW // 128):
                                dc = dg * (WDW // 128) + di
                                for tb in range(NTB):
                                    tsl = slice(tb * 512, (tb + 1) * 512)
                                    b = PD[cnt["pd"] % 2]
                                    cnt["pd"] += 1
                                    for j in range(FPP):
                                        P.op("pe", lambda j=j, b=b, slot=slot, di=di, tsl=tsl: pe.matmul(
                                            pb[b][:, :], lhsT=wd[slot][:, j, di * 128:(di + 1) * 128], rhs=HT[:, j, tsl],
                                            start=(j == 0), stop=(j == FPP - 1)),
                                             reads=[("wd", slot), ("H", j, tb)], writes=[("ps", b)])
                                    P.op("dve", lambda b=b, dc=dc, tsl=tsl: dve.scalar_tensor_tensor(
                                        out=xT[:, dc, tsl], in0=pb[b][:, :], scalar=0.5, in1=xT[:, dc, tsl],
                                        op0=ALU.mult, op1=ALU.add),
                                         reads=[("ps", b), ("xT", dc, tb)], writes=[("xT", dc, tb)])

                all_xT = [("xT", k, tb) for k in range(KD) for tb in range(NTB)]

                def proj(t0):
                    rmsnorm(g2, "g2")
                    win_v = W["w_in"].rearrange("(k p) f -> p k f", p=128)
                    for oc in range(32):
                        slot = cnt["wg"] % 2
                        cnt["wg"] += 1
                        wload(wg[slot][:, :, :], win_v[:, :, oc * 128:(oc + 1) * 128], ("wg", slot))
                        if 16 <= oc < 24:
                            for s4 in range(TT // 512):
                                b = PD[cnt["pd"] % 2]
                                cnt["pd"] += 1
                                for si in range(4):
                                    s = s4 * 4 + si
                                    for k in range(KD):
                                        P.op("pe", lambda k=k, b=b, si=si, s=s, slot=slot: pe.matmul(
                                            pb[b][:, si * 128:(si + 1) * 128], lhsT=hnT[:, k, s * 128:(s + 1) * 128],
                                            rhs=wg[slot][:, k, :], start=(k == 0), stop=(k == KD - 1)),
                                             reads=[("wg", slot), ("hn", k, s // 4)], writes=[("ps", b)])
                                vq = cnt["vst"] % 2
                                cnt["vst"] += 1
                                evac_copy(vst[vq][:, :, :], pb[b][:, :].rearrange("p (j t) -> p j t", j=4),
                                          [("ps", b)], [("vst", vq)])
                                if npair == 1:
                                    dst = _dap(own["v"], (t0 + s4 * 512) * 1024 + (oc - 16) * 128,
                                               [[1024, 128], [128 * 1024, 4], [1, 128]])
                                else:
                                    hj, hl_ = divmod(oc - 16, HPC)
                                    dst = bv(ownS["v"][hj])[t0 + s4 * 512:t0 + (s4 + 1) * 512, hl_ * 128:(hl_ + 1) * 128] \
                                        .rearrange("(s p) e -> p s e", p=128)
                                P.op("sp", lambda vq=vq, dst=dst: sp.dma_start(out=dst, in_=vst[vq][:, :, :]),
                                     reads=[("vst", vq)], writes=["projA"], dma="scr")
                            continue
                        for tb in range(NTB):
                            tsl = slice(tb * 512, (tb + 1) * 512)
                            i = cnt["pg"] % 2
                            cnt["pg"] += 1
                            bg, bn = PG[i], PU[i]
                            for k in range(KD):
                                P.op("pe", lambda k=k, bg=bg, slot=slot, tsl=tsl: pe.matmul(
                                    pb[bg][:, :], lhsT=wg[slot][:, k, :], rhs=hnT[:, k, tsl],
                                    start=(k == 0), stop=(k == KD - 1)),
                                     reads=[("wg", slot), ("hn", k, tb)], writes=[("ps", bg)])
                            qq = cnt["qst"] % 2
                            cnt["qst"] += 1
                            if oc < 16:
                                q = cnt["sq"] % 2
                                cnt["sq"] += 1
                                r = cnt["sg"] % 2
                                cnt["sg"] += 1
                                P.op("act", lambda q=q, bg=bg: act.activation(out=sq[q][:], in_=pb[bg][:, :], func=AF.Square),
                                     reads=[("ps", bg)], writes=[("sq", q)])
                                P.op("pe", lambda q=q, bn=bn: pe.matmul(pb[bn][:, :], lhsT=ones[:], rhs=sq[q][:], start=True, stop=True),
                                     reads=[("sq", q), "ones"], writes=[("ps", bn)])
                                P.op("dve", lambda r=r, bn=bn: dve.tensor_scalar(out=sg[r][:], in0=pb[bn][:, :], scalar1=1.0 / 128,
                                                                                  scalar2=EPS, op0=ALU.mult, op1=ALU.add),
                                     reads=[("ps", bn)], writes=[("sg", r)])
                                rsqrt_inplace(sg[r][:], ("sg", r))
                                col = 0 if oc < 8 else 1
                                P.op("dve", lambda r=r, bg=bg, qq=qq, col=col: dve.scalar_tensor_tensor(
                                    out=qst[qq][:], in0=pb[bg][:, :], scalar=gqk[:, col:col + 1], in1=sg[r][:],
                                    op0=ALU.mult, op1=ALU.mult),
                                     reads=[("ps", bg), ("sg", r), "gqk"], writes=[("qst", qq)])
                                row0 = (oc % 8) * 128
                                dh = (own["q"] if oc < 8 else own["k"]) if npair == 1 else None
                            else:
                                evac_copy(qst[qq][:], pb[bg][:, :], [("ps", bg)], [("qst", qq)])
                                row0 = (oc - 24) * 128
                                dh = own["u"] if npair == 1 else None
                            if npair == 1:
                                dst = dh.ap()[row0:row0 + 128, t0 + tb * 512:t0 + (tb + 1) * 512]
                            else:
                                kd_ = "q" if oc < 8 else ("k" if oc < 16 else "u")
                                hj, r0_ = divmod(row0, 512)
                                dst = bv(ownS[kd_][hj])[r0_:r0_ + 128, t0 + tb * 512:t0 + (tb + 1) * 512]
                            P.op("sp", lambda qq=qq, dst=dst: sp.dma_start(out=dst, in_=qst[qq][:]),
                                 reads=[("qst", qq)], writes=["projA"], dma="scr")

                def mix_out(t0, tg0):
                    for h in range(N_HEADS):
                        r, hl = divmod(h, HPC)
                        row0 = r * MIXR + hl * 128
                        P.op("sp", lambda h=h, row0=row0: sp.dma_start(
                            out=hnT[:, h, :], in_=mixM_h.ap()[row0:row0 + 128, tg0:tg0 + TT]),
                             reads=["mixM"], writes=[("hn", h, tb) for tb in range(NTB)], dma="ld")
                    for c in range(8):
                        r, cl = divmod(c, NUC)
                        row0 = r * MIXR + HPC * 128 + cl * 128
                        P.op("sp", lambda c=c, row0=row0: sp.dma_start(
                            out=yTb[:, c, :], in_=mixM_h.ap()[row0:row0 + 128, tg0:tg0 + TT]),
                             reads=["mixM"], writes=[("yT", c)], dma="ld")
                    gw_v = W["glu_w"].rearrange("(k p) f -> p k f", p=128)
                    for c2 in range(8):
                        slot = cnt["wg"] % 2
                        cnt["wg"] += 1
                        wload(wg[slot][:, 0:8, :], gw_v[:, :, c2 * 128:(c2 + 1) * 128], ("wg", slot))
                        for tb in range(NTB):
                            tsl = slice(tb * 512, (tb + 1) * 512)
                            bg = PG[cnt["pg"] % 2]
                            cnt["pg"] += 1
                            for c in range(8):
                                P.op("pe", lambda c=c, bg=bg, slot=slot, tsl=tsl: pe.matmul(
                                    pb[bg][:, :], lhsT=wg[slot][:, c, :], rhs=yTb[:, c, tsl], start=(c == 0), stop=(c == 7)),
                                     reads=[("wg", slot), ("yT", c)], writes=[("ps", bg)])
                            q = cnt["sg"] % 2
                            cnt["sg"] += 1
                            P.op("act", lambda q=q, bg=bg, c2=c2: act.activation(out=sg[q][:], in_=pb[bg][:, :], func=AF.Sigmoid,
                                                                                  bias=glub[:, c2:c2 + 1]),
                                 reads=[("ps", bg), "glub"], writes=[("sg", q)])
                            P.op("dve", lambda q=q, c2=c2, tsl=tsl: dve.tensor_tensor(
                                out=hnT[:, 8 + c2, tsl], in0=sg[q][:], in1=yTb[:, c2, tsl], op=ALU.mult),
                                 reads=[("sg", q), ("yT", c2)], writes=[("hn", 8 + c2, tb)])
                    wo_v = W["w_out"].rearrange("(k p) f -> p k f", p=128)
                    for dc in range(KD):
                        slot = cnt["wg"] % 2
                        cnt["wg"] += 1
                        wload(wg[slot][:, :, :], wo_v[:, :, dc * 128:(dc + 1) * 128], ("wg", slot))
                        for tb in range(NTB):
                            tsl = slice(tb * 512, (tb + 1) * 512)
                            b = PD[cnt["pd"] % 2]
                            cnt["pd"] += 1
                            for k in range(KD):
                                P.op("pe", lambda k=k, b=b, slot=slot, tsl=tsl: pe.matmul(
                                    pb[b][:, :], lhsT=wg[slot][:, k, :], rhs=hnT[:, k, tsl], start=(k == 0), stop=(k == KD - 1)),
                                     reads=[("wg", slot), ("hn", k, tb)], writes=[("ps", b)])
                            P.op("dve", lambda b=b, dc=dc, tsl=tsl: dve.scalar_tensor_tensor(
                                out=xT[:, dc, tsl], in0=pb[b][:, :], scalar=1.0, in1=xT[:, dc, tsl],
                                op0=ALU.mult, op1=ALU.add),
                                 reads=[("ps", b), ("xT", dc, tb)], writes=[("xT", dc, tb)])

                for tt in range(NT):
                    t0 = tt * TT
                    x1v = _dap(x1s_h, tt * 128 * KD * TT, [[KD * TT, 128], [TT, KD], [1, TT]])
                    if phase == "A":
                        load_xT(t0)
                        rmsnorm(g1, "g1")
                        ffn(W["ffn1_w_gate"], W["ffn1_w_up"], W["ffn1_w_down"])
                        P.op("sp", lambda x1v=x1v: sp.dma_start(out=x1v, in_=xT[:, :, :]), reads=all_xT, writes=["x1s"], dma="scr")
                        proj(t0)
                    else:
                        P.op("sp", lambda x1v=x1v: sp.dma_start(out=xT[:, :, :], in_=x1v), reads=["x1s"], writes=all_xT, dma="ld")
                        mix_out(t0, t0)
                        rmsnorm(g3, "g3")
                        ffn(W["ffn2_w_gate"], W["ffn2_w_up"], W["ffn2_w_down"])
                        store_xT(t0)
                P.barrier()

        def attention_phase():
            with ExitStack() as st:
                sbl = mk_sb(st, "B_")
                qTh = sbl("qTh", [128, SEQ], BF16)
                kTh = sbl("kTh", [128, SEQ], BF16)
                vb = [sbl(f"vb{i}", [128, 32, 128], BF16) for i in range(3)]
                acc = sbl("acc", [128, 2, SEQ])
                rden = sbl("rden", [128, SEQ])
                outst = sbl("outst", [128, SEQ], BF16)
                Bt = [[sbl(f"Bt{pi}_{hl}", [128, 256], BF16) for hl in range(HPC)] for pi in range(3)]
                Hf = [sbl(f"Hf{i}", [128, 256]) for i in range(2)]
                pT = [sbl(f"pT{i}", [128, 256], BF16) for i in range(2)]
                rb = sbl("rb", [32, 8])
                oh = sbl("oh", [32, 3 * 129])
                zt = sbl("zt", [8, 3, 384])
                cdma(rb[:, 0:HPC], W["rel_bias"], "rb")
                cdma(oh[:], oh_d, "oh")
                P.op("pe", lambda: pe.matmul(pb[0][0:HPC, 0:387], lhsT=rb[:, 0:HPC], rhs=oh[:, :], start=True, stop=True),
                     reads=["rb", "oh"], writes=[("ps", 0)])
                P.op("pool", lambda: pool.memset(zt[:], NEG), writes=["zt"])
                P.op("dve", lambda: dve.tensor_copy(out=zt[0:HPC, :, 127:256], in_=pb[0][0:HPC, 0:387].rearrange("p (a b) -> p a b", a=3)),
                     reads=[("ps", 0)], writes=["zt"])
                P.op("sp", lambda: sp.dma_start(out=_dap(zpad_h, 0, [[3 * 384, 8], [384, 3], [1, 384]]), in_=zt[:]),
                     reads=["zt"], writes=["zpad"], dma="scr")
                hbase = 0
                for hl in range(HPC):
                    for pi in range(3):
                        i = (hl * 3 + pi) % 2
                        src = _dap(zpad_h, ((hbase + hl) * 3 + pi) * 384, [[1, 128], [1, 256]])
                        P.op("sp", lambda i=i, src=src: sp.dma_start(out=Hf[i][:], in_=src), reads=["zpad"],
                             writes=[("Hf", i)], dma="ld")
                        P.op("act", lambda i=i, hl=hl, pi=pi: act.copy(out=Bt[pi][hl][:], in_=Hf[i][:]),
                             reads=[("Hf", i)], writes=[("Bt", pi, hl)])
                nblk = 0
                for hl in range(HPC):
                    hg = hbase + hl
                    P.op("sp", lambda hl=hl: sp.dma_start(out=qTh[:, :], in_=mine["q"].ap()[hl * 128:(hl + 1) * 128, :]),
                         reads=["projM"], writes=["qTh"], dma="ld")
                    P.op("sp", lambda hl=hl: sp.dma_start(out=kTh[:, :], in_=mine["k"].ap()[hl * 128:(hl + 1) * 128, :]),
                         reads=["projM"], writes=["kTh"], dma="ld")
                    for pi, d in enumerate(DILS):
                        nm = 32 // d
                        for res in range(d):
                            for m0 in range(0, nm, 8):
                                mm = min(8, nm - m0)
                                t_0 = m0 * 128 * d + res
                                bi = res * nm + m0
                                P.op("sp", lambda pi=pi, bi=bi, mm=mm, t_0=t_0, d=d, hl=hl: sp.dma_start(
                                    out=vb[pi][:, bi:bi + mm, :],
                                    in_=mine["v"].ap()[t_0:t_0 + (mm * 128 - 1) * d + 1:d, hl * 128:(hl + 1) * 128]
                                    .rearrange("(m j) e -> j m e", j=128)),
                                     reads=["projM"], writes=[("vb", pi)], dma="ld")
                    blocks = []
                    for pi, d in enumerate(DILS):
                        nm = 32 // d
                        for res in range(d):
                            for n in range(nm):
                                blocks.append((pi, d, nm, res, n, nblk))
                                nblk += 1

                    def stA(blk, hl=hl):
                        pi, d, nm, res, n, ib = blk
                        off = n * 128 * d + res
                        qa = qTh[:, off:off + 127 * d + 1:d]
                        ka = kTh[:, off:off + 127 * d + 1:d]
                        bS = PT[ib % 2]
                        ip = ib % 2
                        Wd_ = 256 if n > 0 else 128
                        P.op("pe", lambda: pe.matmul(pb[bS][:, 0:Wd_], lhsT=jrev[:], rhs=Bt[pi][hl][:, 0:Wd_], start=True, stop=False),
                             reads=["jrev", ("Bt", pi, hl)], writes=[("ps", bS)])
                        P.op("pe", lambda: pe.matmul(pb[bS][:, 0:128], lhsT=ka, rhs=qa, start=False, stop=(n == 0)),
                             reads=["qTh", "kTh"], writes=[("ps", bS)])
                        if n > 0:
                            offp = off - 128 * d
                            kp = kTh[:, offp:offp + 127 * d + 1:d]
                            P.op("pe", lambda: pe.matmul(pb[bS][:, 128:256], lhsT=kp, rhs=qa, start=False, stop=True),
                                 reads=["qTh", "kTh"], writes=[("ps", bS)])
                        P.op("act", lambda: act.activation(out=pT[ip][:, 0:Wd_], in_=pb[bS][:, 0:Wd_], func=AF.Exp),
                             reads=[("ps", bS)], writes=[("pT", ip)])

                    def stB(blk):
                        pi, d, nm, res, n, ib = blk
                        off = n * 128 * d + res
                        bO = PG[ib % 2]
                        ip = ib % 2
                        bi = res * nm + n
                        P.op("pe", lambda: pe.matmul(pb[bO][:, 0:128], lhsT=vb[pi][:, bi, :], rhs=pT[ip][:, 0:128], start=True, stop=(n == 0)),
                             reads=[("vb", pi), ("pT", ip)], writes=[("ps", bO)])
                        if n > 0:
                            P.op("pe", lambda: pe.matmul(pb[bO][:, 0:128], lhsT=vb[pi][:, bi - 1, :], rhs=pT[ip][:, 128:256], start=False, stop=True),
                                 reads=[("vb", pi), ("pT", ip)], writes=[("ps", bO)])
                        P.op("pe", lambda: pe.matmul(pb[bO][:, 128:256], lhsT=ones[:], rhs=pT[ip][:, 0:128], start=True, stop=(n == 0)),
                             reads=["ones", ("pT", ip)], writes=[("ps", bO)])
                        if n > 0:
                            P.op("pe", lambda: pe.matmul(pb[bO][:, 128:256], lhsT=ones[:], rhs=pT[ip][:, 128:256], start=False, stop=True),
                                 reads=["ones", ("pT", ip)], writes=[("ps", bO)])
                        av = acc[:, :, off:off + 127 * d + 1:d]
                        pv = pb[bO][:, 0:256].rearrange("p (a b) -> p a b", a=2)
                        if pi == 0:
                            P.op("dve", lambda: dve.tensor_copy(out=av, in_=pv), reads=[("ps", bO)], writes=[("acc", 0, n)])
                        else:
                            if pi == 1:
                                prev = [("acc", 0, 4 * n + k_) for k_ in range(4)]
                            else:
                                prev = [("acc", 1, r_, 4 * n + k_) for r_ in range(4) for k_ in range(4)]
                            P.op("dve", lambda: dve.tensor_tensor(out=av, in0=pv, in1=av, op=ALU.add),
                                 reads=[("ps", bO)] + prev, writes=[("acc", pi, res, n)])

                    for i_ in range(len(blocks) + 1):
                        if i_ < len(blocks):
                            stA(blocks[i_])
                        if i_ >= 1:
                            stB(blocks[i_ - 1])
                    accall = [("acc", 2, r_, n_) for r_ in range(16) for n_ in range(2)]
                    P.op("dve", lambda: dve.reciprocal(out=rden[:], in_=acc[:, 1, :]), reads=accall, writes=["rden"])
                    P.op("pool", lambda: pool.tensor_tensor(out=outst[:], in0=acc[:, 0, :], in1=rden[:], op=ALU.mult),
                         reads=accall + ["rden"], writes=["outst"] + [("acc", 0, n_) for n_ in range(32)]
                         + [("acc", 1, r_, n_) for r_ in range(4) for n_ in range(8)] + accall)
                    if npair == 1:
                        P.op("sp", lambda hl=hl: sp.dma_start(out=mix_h.ap()[hl * 128:(hl + 1) * 128, :], in_=outst[:]),
                             reads=["outst"], writes=["mix"], dma="scr")
                    else:
                        for j in range(2):
                            P.op("sp", lambda hl=hl, j=j: sp.dma_start(out=bv(mixA[j][0])[hl * 128:(hl + 1) * 128, :],
                                                                       in_=outst[:, j * ntok:(j + 1) * ntok]),
                                 reads=["outst"], writes=["mix"], dma="scr")
                P.barrier()


        def ssm_phase():
            S = GPC * 64
            TWO_PI = float(2 * np.pi)
            with ExitStack() as st:
                sbl = mk_sb(st, "S_")
                Tm_re = sbl("Tm_re", [128, S]); Tm_im = sbl("Tm_im", [128, S])
                Tp_re = sbl("Tp_re", [128, NP2 * 128]); Tp_im = sbl("Tp_im", [128, NP2 * 128])
                t128_re = sbl("t128_re", [128, NP2]); t128_im = sbl("t128_im", [128, NP2])
                Bblk_re = [sbl(f"Bblk_re{i}", [128, 512], BF16) for i in range(NUC)]
                Bblk_im = [sbl(f"Bblk_im{i}", [128, 512], BF16) for i in range(NUC)]
                Cre = sbl("Cre", [128, NP2, 4, 2, 16], BF16); nCre = sbl("nCre", [128, NP2, 4, 2, 16], BF16)
                nCim = sbl("nCim", [128, NP2, 4, 2, 16], BF16)
                for t_ in (Cre, nCre, nCim):
                    P.op("pool", lambda t_=t_: pool.memset(t_[:], 0.0), writes=["Cw"])
                d_col = sbl("d_col", [128, NUC])
                iota_c = sbl("iota_c", [128, 1]); iota_r = sbl("iota_r", [128, 128])
                tri = sbl("tri", [128, 128], BF16); ntri = sbl("ntri", [128, 128], BF16)
                mask2 = sbl("mask2", [128, 2]); nmask2 = sbl("nmask2", [128, 2]); mask3 = sbl("mask3", [128, 4])
                inj_re = [sbl(f"inj_re{i}", [128, NP2]) for i in range(2)]
                inj_im = [sbl(f"inj_im{i}", [128, NP2]) for i in range(2)]
                for i_ in range(2):
                    P.op("pool", lambda i_=i_: pool.memset(inj_re[i_][:], 0.0), writes=[("inj", i_, k_) for k_ in range(NUC)])
                    P.op("pool", lambda i_=i_: pool.memset(inj_im[i_][:], 0.0), writes=[("inj", i_, k_) for k_ in range(NUC)])
                for t_, d_, r_ in ((iota_c, iotac_d, "iota_c"), (iota_r, iotar_d, "iota_r"), (tri, tri_d, "tri"),
                                   (ntri, ntri_d, "ntri"), (mask2, mask2_d, "mask2"), (nmask2, nmask2_d, "nmask2"),
                                   (mask3, mask3_d, "mask3")):
                    cdma(t_[:], d_, r_)
                cdma(d_col[:], SS["ssm_d"].rearrange("(c p) -> p c", p=128), "d_col")

                with ExitStack() as st2:
                    sb2 = mk_sb(st2, "S2_")
                    BLK = 1024
                    tmp = {n: sb2("tg_" + n, [128, BLK]) for n in ("t", "fr", "m", "cosv", "sinv", "mag")}
                    tint = sb2("tg_int", [128, BLK], mybir.dt.int32)

                    def dv(fn, reads, writes):
                        P.op("dve", fn, reads=reads, writes=writes)

                    def trig(out_re, out_im, ang, marg, n, np_, sign, rres, wres):
                        for c0 in range(0, n, BLK):
                            w = min(BLK, n - c0)
                            cs = slice(c0, c0 + w)
                            T = {k: v[0:np_, 0:w] for k, v in tmp.items()}
                            ti = tint[0:np_, 0:w]
                            dv(lambda T=T, cs=cs: dve.tensor_scalar(out=T["t"], in0=ang[:, cs], scalar1=1.0 / TWO_PI, scalar2=None, op0=ALU.mult),
                               rres, ["tg_t"])
                            for name, shift in (("cosv", 0.25), ("sinv", 0.0)):
                                dv(lambda T=T, shift=shift: dve.tensor_scalar(out=T["fr"], in0=T["t"], scalar1=shift, scalar2=None, op0=ALU.add),
                                   ["tg_t"], ["tg_fr"])
                                dv(lambda T=T, ti=ti: dve.tensor_copy(out=ti, in_=T["fr"]), ["tg_fr"], ["tg_i"])
                                dv(lambda T=T, ti=ti: dve.tensor_copy(out=T["m"], in_=ti), ["tg_i"], ["tg_m"])
                                dv(lambda T=T: dve.tensor_tensor(out=T["fr"], in0=T["fr"], in1=T["m"], op=ALU.subtract), ["tg_fr", "tg_m"], ["tg_fr"])
                                dv(lambda T=T: dve.tensor_scalar(out=T["m"], in0=T["fr"], scalar1=0.5, scalar2=None, op0=ALU.is_gt), ["tg_fr"], ["tg_m"])
                                dv(lambda T=T: dve.tensor_tensor(out=T["fr"], in0=T["fr"], in1=T["m"], op=ALU.subtract), ["tg_fr", "tg_m"], ["tg_fr"])
                                dv(lambda T=T: dve.tensor_scalar(out=T["m"], in0=T["fr"], scalar1=-0.5, scalar2=None, op0=ALU.is_lt), ["tg_fr"], ["tg_m"])
                                dv(lambda T=T: dve.tensor_tensor(out=T["fr"], in0=T["fr"], in1=T["m"], op=ALU.add), ["tg_fr", "tg_m"], ["tg_fr"])
                                P.op("act", lambda T=T, name=name: act.activation(out=T[name], in_=T["fr"], func=AF.Sin, scale=TWO_PI),
                                     reads=["tg_fr"], writes=["tg_" + name])
                            P.op("act", lambda T=T, cs=cs: act.activation(out=T["mag"], in_=marg[:, cs], func=AF.Exp, scale=float(sign)),
                                 reads=rres, writes=["tg_mag"])
                            dv(lambda T=T, cs=cs: dve.tensor_tensor(out=out_re[:, cs], in0=T["mag"], in1=T["cosv"], op=ALU.mult),
                               ["tg_mag", "tg_cosv"], wres)
                            dv(lambda T=T, cs=cs: dve.scalar_tensor_tensor(out=out_im[:, cs], in0=T["mag"], scalar=float(sign), in1=T["sinv"],
                                                                           op0=ALU.mult, op1=ALU.mult),
                               ["tg_mag", "tg_sinv"], wres)

                    col = lambda n: sb2(n, [128, NP2])
                    lre, lim, ldt, alpha, theta = col("lre"), col("lim"), col("ldt"), col("alpha"), col("theta")
                    a_re, a_im, cf_re, cf_im, w1, w2 = col("a_re"), col("a_im"), col("cf_re"), col("cf_im"), col("w1"), col("w2")
                    al128, th128 = col("al128"), col("th128")
                    cdma(lre[:], SS["ssm_lambda_re"].rearrange("(q p) -> p q", p=128), "lre")
                    cdma(lim[:], SS["ssm_lambda_im"].rearrange("(q p) -> p q", p=128), "lim")
                    ldt_h = SS_h["ssm_log_dt"]
                    cdma(ldt[0:64, :], _dap(ldt_h, 0, [[0, 64], [2, NP2]]), "ldt")
                    cdma(ldt[64:128, :], _dap(ldt_h, 1, [[0, 64], [2, NP2]]), "ldt")
                    P.op("act", lambda: act.activation(out=ldt[:], in_=ldt[:], func=AF.Exp), reads=["ldt"], writes=["ldt"])
                    dv(lambda: dve.tensor_tensor(out=alpha[:], in0=lre[:], in1=ldt[:], op=ALU.mult), ["lre", "ldt"], ["alpha"])
                    dv(lambda: dve.tensor_tensor(out=theta[:], in0=lim[:], in1=ldt[:], op=ALU.mult), ["lim", "ldt"], ["theta"])
                    trig(a_re, a_im, theta, alpha, NP2, 128, 1.0, ["theta", "alpha"], ["a"])
                    dv(lambda: dve.tensor_scalar(out=a_re[:], in0=a_re[:], scalar1=-1.0, scalar2=None, op0=ALU.add), ["a"], ["a"])
                    dv(lambda: dve.tensor_tensor(out=w1[:], in0=lre[:], in1=lre[:], op=ALU.mult), ["lre"], ["w1"])
                    dv(lambda: dve.tensor_tensor(out=w2[:], in0=lim[:], in1=lim[:], op=ALU.mult), ["lim"], ["w2"])
                    dv(lambda: dve.tensor_tensor(out=w1[:], in0=w1[:], in1=w2[:], op=ALU.add), ["w1", "w2"], ["w1"])
                    dv(lambda: dve.reciprocal(out=w1[:], in_=w1[:]), ["w1"], ["w1"])
                    dv(lambda: dve.tensor_tensor(out=cf_re[:], in0=a_re[:], in1=lre[:], op=ALU.mult), ["a", "lre"], ["cf_re"])
                    dv(lambda: dve.tensor_tensor(out=w2[:], in0=a_im[:], in1=lim[:], op=ALU.mult), ["a", "lim"], ["w2"])
                    dv(lambda: dve.tensor_tensor(out=cf_re[:], in0=cf_re[:], in1=w2[:], op=ALU.add), ["cf_re", "w2"], ["cf_re"])
                    dv(lambda: dve.tensor_tensor(out=cf_re[:], in0=cf_re[:], in1=w1[:], op=ALU.mult), ["cf_re", "w1"], ["cf_re"])
                    dv(lambda: dve.tensor_tensor(out=cf_im[:], in0=a_im[:], in1=lre[:], op=ALU.mult), ["a", "lre"], ["cf_im"])
                    dv(lambda: dve.tensor_tensor(out=w2[:], in0=a_re[:], in1=lim[:], op=ALU.mult), ["a", "lim"], ["w2"])
                    dv(lambda: dve.tensor_tensor(out=cf_im[:], in0=cf_im[:], in1=w2[:], op=ALU.subtract), ["cf_im", "w2"], ["cf_im"])
                    dv(lambda: dve.tensor_tensor(out=cf_im[:], in0=cf_im[:], in1=w1[:], op=ALU.mult), ["cf_im", "w1"], ["cf_im"])
                    dv(lambda: dve.tensor_scalar(out=al128[:], in0=alpha[:], scalar1=128.0, scalar2=None, op0=ALU.mult), ["alpha"], ["al128"])
                    dv(lambda: dve.tensor_scalar(out=th128[:], in0=theta[:], scalar1=128.0, scalar2=None, op0=ALU.mult), ["theta"], ["th128"])
                    trig(t128_re, t128_im, th128, al128, NP2, 128, 1.0, ["th128", "al128"], ["t128"])
                    angp = sb2("angp", [128, S]); margp = sb2("margp", [128, S])
                    for q in range(NP2):
                        dv(lambda q=q: dve.tensor_scalar(out=angp[:, q * 128:(q + 1) * 128], in0=iota_r[:], scalar1=theta[:, q:q + 1],
                                                         scalar2=None, op0=ALU.mult), ["iota_r", "theta"], ["angm"])
                        P.op("pool", lambda q=q: pool.tensor_scalar(out=margp[:, q * 128:(q + 1) * 128], in0=iota_r[:], scalar1=alpha[:, q:q + 1],
                                                                    scalar2=None, op0=ALU.mult), reads=["iota_r", "alpha"], writes=["margm"])
                    trig(Tp_re, Tp_im, angp, margp, NP2 * 128, 128, 1.0, ["angm", "margm"], ["Tp"])
                    P.op("sp", lambda: sp.dma_start(out=_dap(prm_h, 0, [[1, 128], [128, NP2]]), in_=theta[:]), reads=["theta"], writes=["prm"], dma="scr")
                    P.op("sp", lambda: sp.dma_start(out=_dap(prm_h, S, [[1, 128], [128, NP2]]), in_=alpha[:]), reads=["alpha"], writes=["prm"], dma="scr")
                    angm, margm = angp, margp
                    P.op("sp", lambda: sp.dma_start(out=angm[:], in_=_dap(prm_h, 0, [[0, 128], [1, S]])), reads=["prm"], writes=["angm"], dma="ld")
                    P.op("sp", lambda: sp.dma_start(out=margm[:], in_=_dap(prm_h, S, [[0, 128], [1, S]])), reads=["prm"], writes=["margm"], dma="ld")
                    dv(lambda: dve.tensor_scalar(out=angm[:], in0=angm[:], scalar1=iota_c[:, 0:1], scalar2=None, op0=ALU.mult), ["angm", "iota_c"], ["angm"])
                    P.op("pool", lambda: pool.tensor_scalar(out=margm[:], in0=margm[:], scalar1=iota_c[:, 0:1], scalar2=None, op0=ALU.mult),
                         reads=["margm", "iota_c"], writes=["margm"])
                    trig(Tm_re, Tm_im, angm, margm, S, 128, -1.0, ["angm", "margm"], ["Tm"])
                    Bn_re = sb2("Bn_re", [128, NP2, 16]); Bn_im = sb2("Bn_im", [128, NP2, 16])
                    tA = sb2("tA", [128, NP2, 16]); tB = sb2("tB", [128, NP2, 16])
                    Bb_re = sb2("Bb_re", [128, NP2, 16]); Bb_im = sb2("Bb_im", [128, NP2, 16])
                    cdma(Bn_re[:], SS["ssm_b_re"].rearrange("(q p c) -> p q c", p=128, c=16), "Bn_re")
                    cdma(Bn_im[:], SS["ssm_b_im"].rearrange("(q p c) -> p q c", p=128, c=16), "Bn_im")
                    for (dst, x1_, c1_, x2_, c2_, op_) in ((Bb_re, Bn_re, cf_re, Bn_im, cf_im, ALU.subtract),
                                                           (Bb_im, Bn_im, cf_re, Bn_re, cf_im, ALU.add)):
                        for c in range(16):
                            dv(lambda c=c, x1_=x1_, c1_=c1_: dve.tensor_tensor(out=tA[:, :, c], in0=x1_[:, :, c], in1=c1_[:], op=ALU.mult),
                               ["Bn_re", "Bn_im", "cf_re", "cf_im"], ["tA"])
                            dv(lambda c=c, x2_=x2_, c2_=c2_: dve.tensor_tensor(out=tB[:, :, c], in0=x2_[:, :, c], in1=c2_[:], op=ALU.mult),
                               ["Bn_re", "Bn_im", "cf_re", "cf_im"], ["tB"])
                        dv(lambda dst=dst, op_=op_: dve.tensor_tensor(out=dst[:], in0=tA[:], in1=tB[:], op=op_), ["tA", "tB"], ["Bb"])
                    src_t = sb2("src_t", [128, 4, 2, 16])
                    for Bb, Bblk in ((Bb_re, Bblk_re), (Bb_im, Bblk_im)):
                        for ch in range(NUC):
                            for g2 in range(2):
                                dv(lambda Bb=Bb, ch=ch, g2=g2: dve.tensor_scalar(out=src_t[:, :, g2, :], in0=Bb[:, 4 * ch:4 * ch + 4, :],
                                                                                 scalar1=mask2[:, g2:g2 + 1], scalar2=None, op0=ALU.mult),
                                   ["Bb", "mask2"], ["src_t"])
                            P.op("pe", lambda: pe.transpose(out=pb[7][:, 0:128], in_=src_t[:].rearrange("p a b c -> p (a b c)"), identity=ident[:]),
                                 reads=["src_t", "ident"], writes=[("ps", 7)])
                            for q4 in range(4):
                                dv(lambda Bblk=Bblk, ch=ch, q4=q4: dve.tensor_scalar(out=Bblk[ch][:, q4 * 128:(q4 + 1) * 128], in0=pb[7][:, 0:128],
                                                                                     scalar1=mask3[:, q4:q4 + 1], scalar2=None, op0=ALU.mult),
                                   [("ps", 7), "mask3"], [("Bblk", ch)])
                    Cd = sb2("Cd", [128, 2, 64])
                    for name, outs in (("ssm_c_re", ((Cre, mask2), (nCre, nmask2))), ("ssm_c_im", ((nCim, nmask2),))):
                        cv = SS[name].rearrange("(c p n) -> c p n", p=128, n=64)
                        for ch in range(NUC):
                            cdma(Cd[:, 0, :], cv[ch], "Cd")
                            cdma(Cd[:, 1, :], cv[ch], "Cd")
                            P.op("pe", lambda: pe.transpose(out=pb[7][:, 0:128], in_=Cd[:].rearrange("p a b -> p (a b)"), identity=ident[:]),
                                 reads=["Cd", "ident"], writes=[("ps", 7)])
                            trv = pb[7][:, 0:128].rearrange("p (a b c) -> p a b c", a=4, b=2)
                            for dst, mk in outs:
                                for g2 in range(2):
                                    for q4 in range(4):
                                        dv(lambda dst=dst, mk=mk, ch=ch, g2=g2, trv=trv, q4=q4: dve.tensor_scalar(
                                            out=dst[:, 4 * ch + q4, q4, g2, :], in0=trv[:, q4, g2, :], scalar1=mk[:, g2:g2 + 1],
                                            scalar2=None, op0=ALU.mult), [("ps", 7), "mask2", "nmask2"], ["Cw"])
                    P.barrier()

                uTc = [sbl(f"uTc{i}", [128, NUC, 512], BF16) for i in range(2)]
                yst = sbl("yst", [128, NUC, 512])
                dm = [[sbl(f"dm{i}_{j}", [128, 512], BF16) for j in range(4)] for i in range(2)]
                rm = [[sbl(f"rm{i}_{j}", [128, 512], BF16) for j in range(4)] for i in range(2)]
                tn = [sbl(f"tn{i}", [128, 4]) for i in range(4)]
                gt = [sbl(f"gt{i}", [128, 512]) for i in range(3)]
                gout = [sbl(f"gout{i}", [128, 512], BF16) for i in range(2)]
                yst2 = sbl("yst2", [128, NUC, 512])
                ysts = [yst, yst2]
                XR, XI, YB = 4, 5, 6
                NCH = SEQ // 128

                def s0_BU(g):
                    ub, ch, ts, bre, bim = g["ub"], g["ch"], g["ts"], g["bre"], g["bim"]
                    P.op("pe", lambda: pe.matmul(pb[bre][:, :], lhsT=uTc[ub][:, ch, ts], rhs=Bblk_re[ch][:], start=True, stop=True),
                         reads=[("uTc", ub), ("Bblk", ch)], writes=[("ps", bre)])
                    P.op("pe", lambda: pe.matmul(pb[bim][:, :], lhsT=uTc[ub][:, ch, ts], rhs=Bblk_im[ch][:], start=True, stop=True),
                         reads=[("uTc", ub), ("Bblk", ch)], writes=[("ps", bim)])

                def s1_demod(g):
                    i2, ch, bre, bim = g["i2"], g["ch"], g["bre"], g["bim"]
                    tsl = slice(ch * 512, (ch + 1) * 512)
                    A_, B_, C_, D_ = dm[i2]
                    for dst, tab, bsrc, k_ in ((A_, Tm_re, bre, 0), (B_, Tm_im, bim, 1), (C_, Tm_re, bim, 2), (D_, Tm_im, bre, 3)):
                        P.op("dve", lambda dst=dst, tab=tab, bsrc=bsrc: dve.tensor_tensor(out=dst[:], in0=pb[bsrc][:, :], in1=tab[:, tsl], op=ALU.mult),
                             reads=[("ps", bsrc), "Tm"], writes=[("dm", i2, k_)])

                def s2_cumsum(g):
                    i2, ch, c = g["i2"], g["ch"], g["c"]
                    A_, B_, C_, D_ = dm[i2]
                    for q4 in range(4):
                        cs = slice(q4 * 128, (q4 + 1) * 128)
                        for (xb, m1, k1, m2, k2, rhs2) in ((XR, A_, 0, B_, 1, ntri), (XI, C_, 2, D_, 3, tri)):
                            P.op("pe", lambda xb=xb, m1=m1, cs=cs: pe.matmul(pb[xb][:, cs], lhsT=m1[:, cs], rhs=tri[:], start=True, stop=False),
                                 reads=[("dm", i2, k1), "tri"], writes=[("ps", xb)])
                            P.op("pe", lambda xb=xb, m2=m2, cs=cs, rhs2=rhs2: pe.matmul(pb[xb][:, cs], lhsT=m2[:, cs], rhs=rhs2[:], start=False, stop=True),
                                 reads=[("dm", i2, k2), "tri", "ntri"], writes=[("ps", xb)])

                tn5 = [sbl(f"tn5_{i}", [128, 4]) for i in range(2)]

                def s3_remod(g):
                    i2, ch, c = g["i2"], g["ch"], g["c"]
                    last = c >= NCH - 1
                    cur, nxt = c % 2, (c + 1) % 2
                    qs = slice(4 * ch, 4 * ch + 4)
                    if not last:
                        xr = pb[XR][:, 127:512:128]
                        xi = pb[XI][:, 127:512:128]
                        P.op("dve", lambda: dve.tensor_tensor(out=tn5[0][:], in0=xr, in1=inj_re[cur][:, qs], op=ALU.add), reads=[("ps", XR), ("inj", cur, ch)], writes=["tn5a"])
                        P.op("dve", lambda: dve.tensor_tensor(out=tn5[1][:], in0=xi, in1=inj_im[cur][:, qs], op=ALU.add), reads=[("ps", XI), ("inj", cur, ch)], writes=["tn5b"])
                    E1, E2, E3, E4 = rm[i2]
                    for q4 in range(4):
                        q = 4 * ch + q4
                        cs = slice(q4 * 128, (q4 + 1) * 128)
                        tq = slice(q * 128, (q + 1) * 128)
                        for dst, tab, xsrc, inj_, k_ in ((E1, Tp_re, XR, inj_re, 0), (E2, Tp_im, XI, inj_im, 1), (E3, Tp_re, XI, inj_im, 2), (E4, Tp_im, XR, inj_re, 3)):
                            P.op("dve", lambda dst=dst, tab=tab, xsrc=xsrc, inj_=inj_, cs=cs, tq=tq, q=q: dve.scalar_tensor_tensor(
                                out=dst[:, cs], in0=pb[xsrc][:, cs], scalar=inj_[cur][:, q:q + 1], in1=tab[:, tq], op0=ALU.add, op1=ALU.mult),
                                 reads=[("ps", xsrc), "Tp", ("inj", cur, ch)], writes=[("rm", i2, k_)])
                    if not last:
                        P.op("dve", lambda: dve.tensor_tensor(out=tn[0][:], in0=tn5[0][:], in1=t128_re[:, qs], op=ALU.mult), reads=["tn5a", "t128"], writes=["tn0"])
                        P.op("dve", lambda: dve.tensor_tensor(out=tn[1][:], in0=tn5[1][:], in1=t128_im[:, qs], op=ALU.mult), reads=["tn5b", "t128"], writes=["tn1"])
                        P.op("dve", lambda: dve.tensor_tensor(out=tn[2][:], in0=tn5[1][:], in1=t128_re[:, qs], op=ALU.mult), reads=["tn5b", "t128"], writes=["tn2"])
                        P.op("dve", lambda: dve.tensor_tensor(out=tn[3][:], in0=tn5[0][:], in1=t128_im[:, qs], op=ALU.mult), reads=["tn5a", "t128"], writes=["tn3"])
                        P.op("dve", lambda: dve.tensor_tensor(out=inj_re[nxt][:, qs], in0=tn[0][:], in1=tn[1][:], op=ALU.subtract), reads=["tn0", "tn1"], writes=[("inj", nxt, ch)])
                        P.op("dve", lambda: dve.tensor_tensor(out=inj_im[nxt][:, qs], in0=tn[2][:], in1=tn[3][:], op=ALU.add), reads=["tn2", "tn3"], writes=[("inj", nxt, ch)])

                def s4_y(g):
                    i2, ch, yk = g["i2"], g["ch"], g["yk"]
                    E1, E2, E3, E4 = rm[i2]
                    n_ = 0
                    for q4 in range(4):
                        q = 4 * ch + q4
                        cs = slice(q4 * 128, (q4 + 1) * 128)
                        for (wt, et, k_) in ((Cre, E1, 0), (nCre, E2, 1), (nCim, E3, 2), (nCim, E4, 3)):
                            P.op("pe", lambda wt=wt, et=et, q=q, cs=cs, n_=n_: pe.matmul(
                                pb[YB + yk][:, 0:128], lhsT=wt[:, q, :, :, :].rearrange("p a b c -> p (a b c)"), rhs=et[:, cs],
                                start=(n_ == 0), stop=(n_ == 15)),
                                 reads=["Cw", ("rm", i2, k_)], writes=[("ps", YB + yk)])
                            n_ += 1

                def s5_evac(g):
                    ub, ch, ts, yk, yb = g["ub"], g["ch"], g["ts"], g["yk"], g["yb"]
                    P.op("dve", lambda: dve.scalar_tensor_tensor(
                        out=ysts[yb][:, ch, ts], in0=uTc[ub][:, ch, ts], scalar=d_col[:, ch:ch + 1], in1=pb[YB + yk][:, 0:128],
                        op0=ALU.mult, op1=ALU.add),
                         reads=[("uTc", ub), "d_col", ("ps", YB + yk)], writes=[("yst", yb, ch)])
                    if g["sc_last"]:
                        gelu_out(g["sc"], yb)

                ngo = [0]

                def gelu_out(sc, yb):
                    for ch in range(NUC):
                        yv = ysts[yb][:, ch, :]
                        go = ngo[0] % 2
                        ngo[0] += 1
                        P.op("act", lambda yv=yv: act.activation(out=gt[0][:], in_=yv, func=AF.Square), reads=[("yst", yb, ch)], writes=["gt0"])
                        P.op("pool", lambda: pool.tensor_scalar(out=gt[1][:], in0=gt[0][:], scalar1=0.044715, scalar2=1.0, op0=ALU.mult, op1=ALU.add),
                             reads=["gt0"], writes=["gt1"])
                        P.op("pool", lambda yv=yv: pool.tensor_tensor(out=gt[1][:], in0=gt[1][:], in1=yv, op=ALU.mult), reads=["gt1", ("yst", yb, ch)], writes=["gt1"])
                        P.op("act", lambda: act.activation(out=gt[2][:], in_=gt[1][:], func=AF.Sigmoid, scale=1.5957691216057308), reads=["gt1"], writes=["gt2"])
                        P.op("pool", lambda yv=yv, go=go: pool.tensor_tensor(out=gout[go][:], in0=gt[2][:], in1=yv, op=ALU.mult),
                             reads=["gt2", ("yst", yb, ch)], writes=[("gout", go)])
                        if npair == 1:
                            dst = mix_h.ap()[HPC * 128 + ch * 128:HPC * 128 + (ch + 1) * 128, sc * 512:(sc + 1) * 512]
                        else:
                            j_, tl_ = divmod(sc * 512, ntok)
                            dst = bv(mixA[j_][1])[ch * 128:(ch + 1) * 128, tl_:tl_ + 512]
                        P.op("sp", lambda go=go, dst=dst: sp.dma_start(out=dst, in_=gout[go][:]), reads=[("gout", go)], writes=["mix"], dma="scr")

                groups = []
                for sc in range(SEQ // 512):
                    for c4 in range(4):
                        for ch in range(NUC):
                            n_ = len(groups)
                            groups.append(dict(sc=sc, ub=sc % 2, yb=sc % 2, c=sc * 4 + c4, ch=ch, ts=slice(c4 * 128, (c4 + 1) * 128),
                                               i2=n_ % 2, bre=(0, 2)[n_ % 2], bim=(1, 3)[n_ % 2], yk=n_ % 2,
                                               sc_first=(c4 == 0 and ch == 0), sc_last=(c4 == 3 and ch == NUC - 1)))
                stages_fn = (s0_BU, s1_demod, s2_cumsum, s3_remod, s4_y, s5_evac)
                for t in range(len(groups) + 5):
                    for k in range(5, -1, -1):
                        gi = t - k
                        if 0 <= gi < len(groups):
                            g = groups[gi]
                            if k == 0 and g["sc_first"]:
                                P.op("sp", lambda ub=g["ub"], sc=g["sc"]: sp.dma_start(
                                    out=uTc[ub][:, :, :],
                                    in_=mine["u"].ap()[:, sc * 512:(sc + 1) * 512].rearrange("(c p) t -> p c t", p=128)),
                                     reads=["projM"], writes=[("uTc", g["ub"])], dma="ld")
                            stages_fn[k](g)
                P.barrier()

        if "A" in stages:
            row_local_phase("A")
        if npair > 1:
            for kd in ("q", "k", "u", "v"):
                for j in range(2):
                    allgather(kd, ownS[kd][j], gatS[kd][j], "projA", "projG")
            for kd in ("q", "k", "u", "v"):
                nr = SEQ if kd == "v" else 1024
                for j in range(2):
                    P.op("sp", lambda kd=kd, j=j, nr=nr: sp.dma_start(out=stg[kd].ap()[j * nr:(j + 1) * nr, :], in_=bv(gatS[kd][j])),
                         reads=["projG"], writes=["projS"], dma="ld")
            for kd in ("q", "k", "u"):
                for rb in range(2):
                    P.op("sp", lambda kd=kd, rb=rb: sp.dma_start(
                        out=mine[kd].ap()[:, rb * ntok:(rb + 1) * ntok], in_=rows_dyn(stg[kd].ap(), rb * 512, 512, 1024)),
                         reads=["projS"], writes=["projM"], dma="ld")
            for hf in range(2):
                P.op("sp", lambda hf=hf: sp.dma_start(
                    out=mine["v"].ap()[hf * 2048:(hf + 1) * 2048, :], in_=rows_dyn(stg["v"].ap(), hf * 2048, 2048, SEQ)),
                     reads=["projS"], writes=["projM"], dma="ld")
            P.barrier()
        if "B" in stages:
            attention_phase()
            if ssm_on:
                ssm_phase()
        if npair > 1:
            for j in range(2):
                for h in range(2):
                    allgather("mix", mixA[j][h], mixGt[j][h], "mix", "mixG")
            for j in range(2):
                for h in range(2):
                    P.op("sp", lambda j=j, h=h: sp.dma_start(
                        out=stg_mix.ap()[j * 2048 + h * 1024:j * 2048 + (h + 1) * 1024, :], in_=bv(mixGt[j][h])),
                         reads=["mixG"], writes=["mixS"], dma="ld")
            for h in range(2):
                for rb in range(2):
                    P.op("sp", lambda h=h, rb=rb: sp.dma_start(
                        out=mixM_h.ap()[rb * 1024 + h * 512:rb * 1024 + (h + 1) * 512, :],
                        in_=rows_dyn(stg_mix.ap(), h * 1024 + rb * 512, 512, 2048)),
                         reads=["mixS"], writes=["mixM"], dma="ld")
            P.barrier()
        if "C" in stages:
            row_local_phase("C")
        P.barrier()
    return nc


_CONSTS = None


def _t5_bucket(dist):
    dist = np.asarray(dist)
    max_exact = 16
    d_f = np.maximum(dist, max_exact).astype(np.float32)
    val = (np.log(d_f / np.float32(max_exact)) / np.float32(np.log(2048 / max_exact)) * np.float32(32 - max_exact))
    large = max_exact + (np.rint(val) if BUCKET_ROUND else val).astype(np.int32)
    large = np.minimum(large, 31)
    return np.where(dist < max_exact, dist, large)


def _consts():
    global _CONSTS
    if _CONSTS is None:
        oh = np.zeros((32, 3 * 129), np.float32)
        for pi, d in enumerate(DILS):
            b = _t5_bucket(np.arange(129) * d)
            oh[b, pi * 129 + np.arange(129)] = 1.0
        _CONSTS = {
            "ident": np.eye(128, dtype=np.float32),
            "ones_bf": np.ones((128, 128), dtype=ml_dtypes.bfloat16),
            "jrev_bf": np.eye(128, dtype=np.float32)[::-1].copy().astype(ml_dtypes.bfloat16),
            "onehot": oh,
            "iota_c": np.arange(128, dtype=np.float32).reshape(128, 1),
            "iota_r": np.tile(np.arange(128, dtype=np.float32), (128, 1)),
            "tri_bf": np.triu(np.ones((128, 128), np.float32)).astype(ml_dtypes.bfloat16),
            "ntri_bf": (-np.triu(np.ones((128, 128), np.float32))).astype(ml_dtypes.bfloat16),
            "mask2": (np.arange(128)[:, None] // 64 == np.arange(2)[None, :]).astype(np.float32),
            "nmask2": -(np.arange(128)[:, None] // 64 == np.arange(2)[None, :]).astype(np.float32),
            "mask3": (np.arange(128)[:, None] // 32 == np.arange(4)[None, :]).astype(np.float32),
            "sel": np.broadcast_to(np.eye(32, dtype=np.float32)[:, :, None], (32, 32, 128)).copy(),
        }
    return _CONSTS


PARAMS = ["ffn1_norm", "ffn1_w_gate", "ffn1_w_up", "ffn1_w_down", "ffn2_norm", "ffn2_w_gate", "ffn2_w_up",
          "ffn2_w_down", "mix_norm", "w_in", "q_norm", "k_norm", "glu_w", "glu_b", "w_out"]


def make_in_maps(inputs, ncores, npair=1):
    x = np.ascontiguousarray(inputs["x"], dtype=np.float32)
    base = dict(_consts())
    for n in PARAMS:
        base[n] = np.ascontiguousarray(inputs[n][0], dtype=np.float32)
    ntok = SEQ // npair
    gpc = N_GROUPS // npair
    in_maps = []
    for c in range(ncores):
        b, r = divmod(c, npair)
        m = dict(base)
        m["x"] = x[b % BATCH, r * ntok:(r + 1) * ntok]
        gs = slice(r * gpc, (r + 1) * gpc)
        for n in ("ssm_lambda_re", "ssm_lambda_im", "ssm_log_dt", "ssm_b_re", "ssm_b_im", "ssm_c_re", "ssm_c_im"):
            m[n] = np.ascontiguousarray(inputs[n][0][gs], dtype=np.float32).reshape(-1)
        m["ssm_d"] = np.ascontiguousarray(inputs["ssm_d"][0][r * gpc * 16:(r + 1) * gpc * 16], dtype=np.float32)
        hpc = N_HEADS // npair
        m["rel_bias"] = np.ascontiguousarray(np.asarray(inputs["rel_bias"], dtype=np.float32)[:, r * hpc:(r + 1) * hpc])
        in_maps.append(m)
    return in_maps


NPAIR = 2


def kernel(**inputs):
    npair = NPAIR
    ncores = BATCH * npair
    nc = build(dict(npair=npair))
    in_maps = make_in_maps(inputs, ncores, npair)
    res = run_bass_kernel_spmd(nc, in_maps, core_ids=list(range(ncores)))
    ntok = SEQ // npair
    out = np.empty((BATCH, SEQ, D), np.float32)
    for c in range(ncores):
        b, r = divmod(c, npair)
        out[b, r * ntok:(r + 1) * ntok] = np.asarray(res.results[c]["out"])
    return out
```

```python
from contextlib import ExitStack
import numpy as np
import ml_dtypes
import concourse.bass as bass
import concourse.mybir as mybir
from concourse.bass_utils import run_bass_kernel_spmd

F32 = mybir.dt.float32
BF16 = mybir.dt.bfloat16
ALU = mybir.AluOpType
AF = mybir.ActivationFunctionType

D = 2048
KD = D // 128
FF = 5632
FC = FF // 128
SEQ = 4096
BATCH = 4
EPS = 1e-6
TT = 1024
NTB = TT // 512
NPARTS = 11
FPP = FC // NPARTS
WSLOT = 128
WDW = 512
NST = 3


class Op:
    __slots__ = ("eng", "fn", "deps", "needed", "dma", "tok")

    def __init__(self, eng, fn, dma):
        self.eng = eng
        self.fn = fn
        self.deps = []
        self.needed = False
        self.dma = dma
        self.tok = None


class Prog:
    ENGS = ("pe", "act", "dve", "pool", "sp")
    LIMIT = 20000

    def __init__(self, nc, stack):
        self.nc = nc
        self.stack = stack
        self.eng = {"pe": nc.tensor, "act": nc.scalar, "dve": nc.vector,
                    "pool": nc.gpsimd, "sp": nc.sync}
        self.ops = []
        self.last_w = {}
        self.readers = {}
        self.cnt = {e: 0 for e in self.ENGS}
        self.esems = {e: [] for e in self.ENGS}
        self.streams = {}
        self.waited = {e: {} for e in self.ENGS}
        self.last_op = {}
        self.nsem = 0

    def _newsem(self, name):
        self.nsem += 1
        return self.stack.enter_context(self.nc.semaphore(f"{name}_{self.nsem}"))

    def stream(self, name, nsems, inc=16):
        self.streams[name] = dict(sems=[self._newsem(name) for _ in range(nsems)], n=0, inc=inc)

    def op(self, eng, fn, reads=(), writes=(), dma=None):
        o = Op(eng, fn, dma)
        deps = {}
        for r in reads:
            w = self.last_w.get(r)
            if w is not None:
                deps[id(w)] = w
        for r in writes:
            w = self.last_w.get(r)
            if w is not None:
                deps[id(w)] = w
            rd = self.readers.get(r)
            if rd:
                for x in rd.values():
                    deps[id(x)] = x
        for p in deps.values():
            if p is o:
                continue
            if p.dma is None and p.eng == "pe" and eng == "pe" and dma is None:
                continue
            p.needed = True
            o.deps.append(p)
        for r in writes:
            self.last_w[r] = o
            self.readers[r] = {}
        key = eng if dma is None else ("dma", dma, len(self.ops))
        for r in reads:
            self.readers.setdefault(r, {})[key if dma is None else id(o)] = o
        self.ops.append(o)
        self.last_op[eng] = o
        return o

    def _wait(self, eng, sem, val):
        k = id(sem)
        w = self.waited[eng]
        if w.get(k, 0) >= val:
            return
        w[k] = val
        self.eng[eng].wait_ge(sem, val)

    def flush(self):
        for o in self.ops:
            for p in o.deps:
                sem, val = p.tok
                self._wait(o.eng, sem, val)
            if o.dma is not None:
                st = self.streams[o.dma]
                n = st["n"]
                st["n"] = n + 1
                R = len(st["sems"])
                inc = st["inc"]
                sem = st["sems"][n % R]
                prev = inc * (n // R)
                if prev > 0:
                    self._wait(o.eng, sem, prev)
                ins = o.fn()
                if inc == 1:
                    ins.then_inc(sem)
                else:
                    ins.then_inc(sem, inc)
                o.tok = (sem, prev + inc)
            else:
                ins = o.fn()
                if o.needed:
                    c = self.cnt[o.eng]
                    ep, v = divmod(c, self.LIMIT)
                    sems = self.esems[o.eng]
                    if ep >= len(sems):
                        sems.append(self._newsem("e" + o.eng))
                    ins.then_inc(sems[ep], 1)
                    self.cnt[o.eng] = c + 1
                    o.tok = (sems[ep], v + 1)
        self.ops = []

    def barrier(self):
        lasts = []
        for e in self.ENGS:
            o = self.last_op.get(e)
            if o is not None and o.dma is None:
                o.needed = True
                lasts.append(o)
        self.flush()
        for e in self.ENGS:
            for o in lasts:
                if o.eng == e and e == "pe":
                    continue
                self._wait(e, *o.tok)
            for st in self.streams.values():
                R = len(st["sems"])
                for i, sem in enumerate(st["sems"]):
                    cnt = (st["n"] - i + R - 1) // R
                    if cnt > 0:
                        self._wait(e, sem, st["inc"] * cnt)
        self.last_w = {}
        self.readers = {}
        self.last_op = {}


NEG = -30000.0
BUCKET_ROUND = False
N_HEADS = 8
N_GROUPS = 64
DILS = (1, 4, 16)


def _dap(h, off, dims):
    return bass.AP(h, off, [list(d) for d in dims])


def build(cfg):
    npair = cfg.get("npair", 1)
    debug = cfg.get("debug", False)
    stages = cfg.get("stages", "ABC")
    ssm_on = cfg.get("ssm", True)
    ntok = SEQ // npair
    NT = ntok // TT
    HPC = N_HEADS // npair
    GPC = N_GROUPS // npair
    NP2 = GPC // 2
    NUC = GPC * 16 // 128
    MIXR = HPC * 128 + GPC * 16
    nc = bass.Bass("TRN2", target_bir_lowering=False)
    SCR = "ExternalOutput" if debug else "Internal"

    def dt(name, shape, dtype=F32, kind="ExternalInput"):
        return nc.dram_tensor(name, shape, dtype, kind=kind)

    x_d = dt("x", [ntok, D]).ap()
    out_d = dt("out", [ntok, D], kind="ExternalOutput").ap()
    ident_d = dt("ident", [128, 128]).ap()
    ones_d = dt("ones_bf", [128, 128], BF16).ap()
    jrev_d = dt("jrev_bf", [128, 128], BF16).ap()
    oh_d = dt("onehot", [32, 3 * 129]).ap()
    shapes = {"ffn1_norm": [D], "ffn1_w_gate": [D, FF], "ffn1_w_up": [D, FF], "ffn1_w_down": [FF, D],
              "ffn2_norm": [D], "ffn2_w_gate": [D, FF], "ffn2_w_up": [D, FF], "ffn2_w_down": [FF, D],
              "mix_norm": [D], "w_in": [D, 4096], "q_norm": [128], "k_norm": [128], "rel_bias": [32, N_HEADS // cfg.get("npair", 1)],
              "glu_w": [1024, 1024], "glu_b": [1024], "w_out": [D, D]}
    W = {n: dt(n, shapes[n]).ap() for n in shapes}
    sshapes = {"ssm_lambda_re": [GPC * 64], "ssm_lambda_im": [GPC * 64], "ssm_log_dt": [GPC], "ssm_b_re": [GPC * 1024],
               "ssm_b_im": [GPC * 1024], "ssm_c_re": [GPC * 1024], "ssm_c_im": [GPC * 1024], "ssm_d": [GPC * 16]}
    SS_h = {n: dt(n, sshapes[n]) for n in sshapes}
    SS = {n: h.ap() for n, h in SS_h.items()}
    iotac_d = dt("iota_c", [128, 1]).ap(); iotar_d = dt("iota_r", [128, 128]).ap()
    tri_d = dt("tri_bf", [128, 128], BF16).ap(); ntri_d = dt("ntri_bf", [128, 128], BF16).ap()
    mask2_d = dt("mask2", [128, 2]).ap(); nmask2_d = dt("nmask2", [128, 2]).ap(); mask3_d = dt("mask3", [128, 4]).ap()
    sel_d = dt("sel", [32, 32, 128]).ap()
    prm_h = dt("prm", [2 * GPC * 64], F32, kind=SCR)
    x1s_h = dt("x1s", [NT * 128 * KD * TT], F32, kind=SCR)
    own = {"q": dt("qA", [1024, ntok], BF16, kind=SCR), "k": dt("kA", [1024, ntok], BF16, kind=SCR),
           "u": dt("uA", [1024, ntok], BF16, kind=SCR), "v": dt("vA", [ntok, 1024], BF16, kind=SCR)} if npair == 1 else None
    mix_h = dt("mix", [MIXR, SEQ], BF16, kind=SCR) if npair == 1 else None
    zpad_h = dt("zpad", [8 * 3 * 384], F32, kind=SCR)
    bv = lambda h: h.ap().bitcast(BF16)
    if npair > 1:
        f32t = lambda name, rows, cols_bf: dt(name, [rows, cols_bf // 2], F32, kind="Internal")
        ownS = {kd: [f32t(f"{kd}A{j}", 512, ntok) for j in range(2)] for kd in ("q", "k", "u")}
        ownS["v"] = [f32t(f"vA{j}", ntok, 512) for j in range(2)]
        gatS = {kd: [f32t(f"{kd}G{j}", 1024, ntok) for j in range(2)] for kd in ("q", "k", "u")}
        gatS["v"] = [f32t(f"vG{j}", 2 * ntok, 512) for j in range(2)]
        stg = {kd: dt(f"{kd}S", [2 * 1024, ntok], BF16, kind="Internal") for kd in ("q", "k", "u")}
        stg["v"] = dt("vS", [2 * SEQ, 512], BF16, kind="Internal")
        mixA = [[f32t(f"mixA{j}{h}", 512, ntok) for h in range(2)] for j in range(2)]
        mixGt = [[f32t(f"mixG{j}{h}", 1024, ntok) for h in range(2)] for j in range(2)]
        stg_mix = dt("mixS", [2 * 2048, ntok], BF16, kind="Internal")
    RANK = (nc.partition_id() % 2) if npair > 1 else 0
    if npair == 1:
        mine = dict(own)
        mixM_h = mix_h
    else:
        mine = {"q": dt("qM", [HPC * 128, SEQ], BF16, kind="Internal"), "k": dt("kM", [HPC * 128, SEQ], BF16, kind="Internal"),
                "u": dt("uM", [GPC * 16, SEQ], BF16, kind="Internal"), "v": dt("vM", [SEQ, HPC * 128], BF16, kind="Internal")}
        mixM_h = dt("mixM", [2 * MIXR, ntok], BF16, kind="Internal")

    def rows_dyn(ap, static, size, mult):
        if npair == 1:
            return ap[static:static + size, :]
        return ap[static:static + mult + size, :][bass.ds(RANK * mult, size), :]

    def cols_dyn(ap, static, size, mult):
        if npair == 1:
            return ap[:, static:static + size]
        return ap[:, static:static + mult + size][:, bass.ds(RANK * mult, size)]

    with ExitStack() as stack:
        P = Prog(nc, stack)
        stack.enter_context(nc.allow_non_contiguous_dma(reason="tiny strided parameter loads"))
        mk_sb = lambda st, pfx='': (lambda name, shape, dtype=F32: st.enter_context(nc.sbuf_tensor(pfx + name, shape, dtype)))
        sb = mk_sb(stack)
        ps = lambda name, shape, dtype=F32: stack.enter_context(nc.psum_tensor(name, shape, dtype))
        sp, act, dve, pool, pe = nc.sync, nc.scalar, nc.vector, nc.gpsimd, nc.tensor

        ident = sb("ident_sb", [128, 128])
        ones = sb("ones_sb", [128, 128], BF16)
        jrev = sb("jrev_sb", [128, 128], BF16)
        g1 = sb("g1", [128, KD])
        g2 = sb("g2", [128, KD])
        g3 = sb("g3", [128, KD])
        gqk = sb("gqk", [128, 2])
        glub = sb("glub", [128, 8])
        pb = [ps(f"pb{i}", [128, 512]) for i in range(8)]
        PT, PG, PU, PD = (0, 1), (2, 3), (4, 5), (6, 7)

        P.stream("xin", 2)
        P.stream("const", 1)
        P.stream("wst", NST)
        P.stream("out", 2)
        P.stream("scr", 4)
        P.stream("ld", 4)
        P.stream("cc", 1, inc=1)
        PAIRS = [[0, 1], [2, 3], [4, 5], [6, 7]]

        def allgather(name, src_h, dst_h, rres, wres):
            P.op("pool", lambda: pool.collective_compute("AllGather", ALU.bypass, replica_groups=PAIRS,
                                                         ins=[src_h.ap().opt()], outs=[dst_h.ap().opt()]),
                 reads=[rres], writes=[wres], dma="cc")

        if npair == 1:
            _op = P.op

            def _op_alias(eng, fn, reads=(), writes=(), dma=None):
                al = lambda rs: [{"projG": "projA", "projM": "projA", "mixG": "mix", "mixM": "mix"}.get(x, x) if isinstance(x, str) else x for x in rs]
                return _op(eng, fn, reads=al(reads), writes=al(writes), dma=dma)
            P.op = _op_alias

        def cdma(dst, src, res):
            P.op("sp", lambda: sp.dma_start(out=dst, in_=src), writes=[res], dma="const")

        cdma(ident[:], ident_d, "ident")
        cdma(ones[:], ones_d, "ones")
        cdma(jrev[:], jrev_d, "jrev")
        cdma(g1[:], W["ffn1_norm"].rearrange("(k p) -> p k", p=128), "g1")
        cdma(g2[:], W["mix_norm"].rearrange("(k p) -> p k", p=128), "g2")
        cdma(g3[:], W["ffn2_norm"].rearrange("(k p) -> p k", p=128), "g3")
        cdma(gqk[:, 0:1], W["q_norm"].rearrange("(p o) -> p o", o=1), "gqk")
        cdma(gqk[:, 1:2], W["k_norm"].rearrange("(p o) -> p o", o=1), "gqk")
        cdma(glub[:], W["glu_b"].rearrange("(k p) -> p k", p=128), "glub")
        P.op("dve", lambda: dve.tensor_scalar(out=gqk[:, 0:1], in0=gqk[:, 0:1], scalar1=float(128 ** -0.5),
                                              scalar2=None, op0=ALU.mult), reads=["gqk"], writes=["gqk"])

        cnt = {"st": 0, "cp": 0, "xs": 0, "sq": 0, "wg": 0, "wd": 0, "pg": 0, "pd": 0, "pt": 0, "sg": 0,
               "qst": 0, "vst": 0}

        def evac_copy(out_ap, in_ap, reads, writes):
            cnt["cp"] += 1
            if cnt["cp"] % 2:
                P.op("act", lambda: act.copy(out=out_ap, in_=in_ap), reads=reads, writes=writes)
            else:
                P.op("dve", lambda: dve.tensor_copy(out=out_ap, in_=in_ap), reads=reads, writes=writes)

        def row_local_phase(phase):
            with ExitStack() as st:
                sbl = mk_sb(st, phase + "_")
                xT = sbl("xT", [128, KD, TT])
                hnT = sbl("hnT", [128, KD, TT], BF16)
                HT = sbl("HT", [128, FPP, TT], BF16)
                xs = [sbl(f"xs{i}", [128, D]) for i in range(2)]
                wg = [sbl(f"wg{i}", [128, KD, WSLOT], BF16) for i in range(2)]
                wu = [sbl(f"wu{i}", [128, KD, WSLOT], BF16) for i in range(2)]
                wd = [sbl(f"wd{i}", [128, FPP, WDW], BF16) for i in range(2)]
                stage = [sbl(f"stage{i}", [128, KD * WSLOT]) for i in range(NST)]
                sq = [sbl(f"sq{i}", [128, 512], BF16) for i in range(2)]
                sg = [sbl(f"sg{i}", [128, 512]) for i in range(2)]
                rstd = [sbl(f"rstd{i}", [128, 512]) for i in range(NTB)]
                qst = [sbl(f"qst{i}", [128, 512], BF16) for i in range(2)]
                vst = [sbl(f"vst{i}", [128, 4, 128], BF16) for i in range(2)]
                yTb = sbl("yTb", [128, 8, TT], BF16) if phase == "C" else None

                def load_xT(t0):
                    for s in range(TT // 128):
                        slot = cnt["xs"] % 2
                        cnt["xs"] += 1
                        src = x_d[t0 + s * 128:t0 + (s + 1) * 128, :]
                        P.op("sp", lambda slot=slot, src=src: sp.dma_start(out=xs[slot][:], in_=src),
                             writes=[("xs", slot)], dma="xin")
                        for k4 in range(KD // 4):
                            b = PT[cnt["pt"] % 2]
                            cnt["pt"] += 1
                            for j in range(4):
                                k = k4 * 4 + j
                                P.op("pe", lambda b=b, j=j, k=k, slot=slot: pe.transpose(
                                    out=pb[b][:, j * 128:(j + 1) * 128], in_=xs[slot][:, k * 128:(k + 1) * 128],
                                    identity=ident[:]), reads=[("xs", slot), "ident"], writes=[("ps", b)])
                            evac_copy(xT[:, k4 * 4:(k4 + 1) * 4, s * 128:(s + 1) * 128],
                                      pb[b][:, :].rearrange("p (j t) -> p j t", j=4),
                                      [("ps", b)], [("xT", k4 * 4 + j, s // 4) for j in range(4)])

                def store_xT(t0):
                    for s in range(TT // 128):
                        slot = cnt["xs"] % 2
                        cnt["xs"] += 1
                        for k4 in range(KD // 4):
                            b = PT[cnt["pt"] % 2]
                            cnt["pt"] += 1
                            for j in range(4):
                                k = k4 * 4 + j
                                P.op("pe", lambda b=b, j=j, k=k, s=s: pe.transpose(
                                    out=pb[b][:, j * 128:(j + 1) * 128], in_=xT[:, k, s * 128:(s + 1) * 128],
                                    identity=ident[:]), reads=[("xT", k, s // 4), "ident"], writes=[("ps", b)])
                            evac_copy(xs[slot][:, k4 * 512:(k4 + 1) * 512], pb[b][:, :],
                                      [("ps", b)], [("xs", slot)])
                        dst = out_d[t0 + s * 128:t0 + (s + 1) * 128, :]
                        P.op("sp", lambda slot=slot, dst=dst: sp.dma_start(out=dst, in_=xs[slot][:]),
                             reads=[("xs", slot)], dma="out")

                def rsqrt_inplace(t, res):
                    P.op("act", lambda: act.sqrt(out=t, in_=t), reads=[res], writes=[res])
                    P.op("dve", lambda: dve.reciprocal(out=t, in_=t), reads=[res], writes=[res])

                def rmsnorm(g, gname):
                    for tb in range(NTB):
                        b = PD[tb % 2]
                        tsl = slice(tb * 512, (tb + 1) * 512)
                        for k in range(KD):
                            q = cnt["sq"] % 2
                            cnt["sq"] += 1
                            P.op("act", lambda q=q, k=k, tsl=tsl: act.activation(out=sq[q][:], in_=xT[:, k, tsl], func=AF.Square),
                                 reads=[("xT", k, tb)], writes=[("sq", q)])
                            P.op("pe", lambda q=q, k=k, b=b: pe.matmul(pb[b][:, :], lhsT=ones[:], rhs=sq[q][:],
                                                                        start=(k == 0), stop=(k == KD - 1)),
                                 reads=[("sq", q), "ones"], writes=[("ps", b)])
                        P.op("dve", lambda b=b, tb=tb: dve.tensor_scalar(out=rstd[tb][:], in0=pb[b][:, :], scalar1=1.0 / D,
                                                                          scalar2=EPS, op0=ALU.mult, op1=ALU.add),
                             reads=[("ps", b)], writes=[("rstd", tb)])
                        rsqrt_inplace(rstd[tb][:], ("rstd", tb))
                        for k in range(KD):
                            P.op("dve", lambda k=k, tb=tb, tsl=tsl: dve.scalar_tensor_tensor(
                                out=hnT[:, k, tsl], in0=xT[:, k, tsl], scalar=g[:, k:k + 1], in1=rstd[tb][:],
                                op0=ALU.mult, op1=ALU.mult),
                                 reads=[("xT", k, tb), ("rstd", tb), gname], writes=[("hn", k, tb)])

                def wload(dst_ap, src_ap, wres):
                    st_ = cnt["st"] % NST
                    cnt["st"] += 1
                    shp = dst_ap.shape
                    sview = stage[st_][:, 0:shp[1] * shp[2]].rearrange("p (a b) -> p a b", a=shp[1])
                    P.op("sp", lambda: sp.dma_start(out=sview, in_=src_ap), writes=[("stage", st_)], dma="wst")
                    if cnt["st"] % 3 == 0:
                        P.op("act", lambda: act.copy(out=dst_ap, in_=sview), reads=[("stage", st_)], writes=[wres])
                    else:
                        P.op("pool", lambda: pool.tensor_copy(out=dst_ap, in_=sview), reads=[("stage", st_)], writes=[wres])

                def ffn(wgate, wup, wdown):
                    wg_v = wgate.rearrange("(k p) f -> p k f", p=128)
                    wu_v = wup.rearrange("(k p) f -> p k f", p=128)
                    wd_v = wdown.rearrange("(c p) d -> p c d", p=128)
                    for part in range(NPARTS):
                        for j in range(FPP):
                            f0 = (part * FPP + j) * 128
                            slot = cnt["wg"] % 2
                            cnt["wg"] += 1
                            wload(wg[slot][:, :, :], wg_v[:, :, f0:f0 + 128], ("wg", slot))
                            wload(wu[slot][:, :, :], wu_v[:, :, f0:f0 + 128], ("wu", slot))
                            for tb in range(NTB):
                                tsl = slice(tb * 512, (tb + 1) * 512)
                                i = cnt["pg"] % 2
                                cnt["pg"] += 1
                                bg, bu = PG[i], PU[i]
                                for k in range(KD):
                                    P.op("pe", lambda k=k, bg=bg, slot=slot, tsl=tsl: pe.matmul(
                                        pb[bg][:, :], lhsT=wg[slot][:, k, :], rhs=hnT[:, k, tsl],
                                        start=(k == 0), stop=(k == KD - 1)),
                                         reads=[("wg", slot), ("hn", k, tb)], writes=[("ps", bg)])
                                for k in range(KD):
                                    P.op("pe", lambda k=k, bu=bu, slot=slot, tsl=tsl: pe.matmul(
                                        pb[bu][:, :], lhsT=wu[slot][:, k, :], rhs=hnT[:, k, tsl],
                                        start=(k == 0), stop=(k == KD - 1)),
                                         reads=[("wu", slot), ("hn", k, tb)], writes=[("ps", bu)])
                                q = cnt["sg"] % 2
                                cnt["sg"] += 1
                                P.op("act", lambda q=q, bg=bg: act.activation(out=sg[q][:], in_=pb[bg][:, :], func=AF.Silu),
                                     reads=[("ps", bg)], writes=[("sg", q)])
                                P.op("dve", lambda q=q, bu=bu, j=j, tsl=tsl: dve.tensor_tensor(
                                    out=HT[:, j, tsl], in0=sg[q][:], in1=pb[bu][:, :], op=ALU.mult),
                                     reads=[("sg", q), ("ps", bu)], writes=[("H", j, tb)])
                        for dg in range(D // WDW):
                            slot = cnt["wd"] % 2
                            cnt["wd"] += 1
                            wload(wd[slot][:, :, :], wd_v[:, part * FPP:(part + 1) * FPP, dg * WDW:(dg + 1) * WDW], ("wd", slot))
                            for di in range(WDW // 128):
                                dc = dg * (WDW // 128) + di
                                for tb in range(NTB):
                                    tsl = slice(tb * 512, (tb + 1) * 512)
                                    b = PD[cnt["pd"] % 2]
                                    cnt["pd"] += 1
                                    for j in range(FPP):
                                        P.op("pe", lambda j=j, b=b, slot=slot, di=di, tsl=tsl: pe.matmul(
                                            pb[b][:, :], lhsT=wd[slot][:, j, di * 128:(di + 1) * 128], rhs=HT[:, j, tsl],
                                            start=(j == 0), stop=(j == FPP - 1)),
                                             reads=[("wd", slot), ("H", j, tb)], writes=[("ps", b)])
                                    P.op("dve", lambda b=b, dc=dc, tsl=tsl: dve.scalar_tensor_tensor(
                                        out=xT[:, dc, tsl], in0=pb[b][:, :], scalar=0.5, in1=xT[:, dc, tsl],
                                        op0=ALU.mult, op1=ALU.add),
                                         reads=[("ps", b), ("xT", dc, tb)], writes=[("xT", dc, tb)])

                all_xT = [("xT", k, tb) for k in range(KD) for tb in range(NTB)]

                def proj(t0):
                    rmsnorm(g2, "g2")
                    win_v = W["w_in"].rearrange("(k p) f -> p k f", p=128)
                    ring = [(wg[0], ("wg", 0)), (wg[1], ("wg", 1)), (wu[0], ("wu", 0)), (wu[1], ("wu", 1))]

                    def issue(oc_):
                        wt_, wr_ = ring[oc_ % 4]
                        wload(wt_[:, :, :], win_v[:, :, oc_ * 128:(oc_ + 1) * 128], wr_)
                    issue(0)
                    issue(1)
                    for oc in range(32):
                        if oc + 2 < 32:
                            issue(oc + 2)
                        wsl, wres_ = ring[oc % 4]
                        if 16 <= oc < 24:
                            for s4 in range(TT // 512):
                                b = PD[cnt["pd"] % 2]
                                cnt["pd"] += 1
                                for si in range(4):
                                    s = s4 * 4 + si
                                    for k in range(KD):
                                        P.op("pe", lambda k=k, b=b, si=si, s=s, wsl=wsl: pe.matmul(
                                            pb[b][:, si * 128:(si + 1) * 128], lhsT=hnT[:, k, s * 128:(s + 1) * 128],
                                            rhs=wsl[:, k, :], start=(k == 0), stop=(k == KD - 1)),
                                             reads=[wres_, ("hn", k, s // 4)], writes=[("ps", b)])
                                vq = cnt["vst"] % 2
                                cnt["vst"] += 1
                                evac_copy(vst[vq][:, :, :], pb[b][:, :].rearrange("p (j t) -> p j t", j=4),
                                          [("ps", b)], [("vst", vq)])
                                if npair == 1:
                                    dst = _dap(own["v"], (t0 + s4 * 512) * 1024 + (oc - 16) * 128,
                                               [[1024, 128], [128 * 1024, 4], [1, 128]])
                                else:
                                    hj, hl_ = divmod(oc - 16, HPC)
                                    dst = bv(ownS["v"][hj])[t0 + s4 * 512:t0 + (s4 + 1) * 512, hl_ * 128:(hl_ + 1) * 128] \
                                        .rearrange("(s p) e -> p s e", p=128)
                                P.op("sp", lambda vq=vq, dst=dst: sp.dma_start(out=dst, in_=vst[vq][:, :, :]),
                                     reads=[("vst", vq)], writes=["projA"], dma="scr")
                            continue
                        for tb in range(NTB):
                            tsl = slice(tb * 512, (tb + 1) * 512)
                            i = cnt["pg"] % 2
                            cnt["pg"] += 1
                            bg, bn = PG[i], PU[i]
                            for k in range(KD):
                                P.op("pe", lambda k=k, bg=bg, wsl=wsl, tsl=tsl: pe.matmul(
                                    pb[bg][:, :], lhsT=wsl[:, k, :], rhs=hnT[:, k, tsl],
                                    start=(k == 0), stop=(k == KD - 1)),
                                     reads=[wres_, ("hn", k, tb)], writes=[("ps", bg)])
                            qq = cnt["qst"] % 2
                            cnt["qst"] += 1
                            if oc < 16:
                                q = cnt["sq"] % 2
                                cnt["sq"] += 1
                                r = cnt["sg"] % 2
                                cnt["sg"] += 1
                                P.op("act", lambda q=q, bg=bg: act.activation(out=sq[q][:], in_=pb[bg][:, :], func=AF.Square),
                                     reads=[("ps", bg)], writes=[("sq", q)])
                                P.op("pe", lambda q=q, bn=bn: pe.matmul(pb[bn][:, :], lhsT=ones[:], rhs=sq[q][:], start=True, stop=True),
                                     reads=[("sq", q), "ones"], writes=[("ps", bn)])
                                P.op("dve", lambda r=r, bn=bn: dve.tensor_scalar(out=sg[r][:], in0=pb[bn][:, :], scalar1=1.0 / 128,
                                                                                  scalar2=EPS, op0=ALU.mult, op1=ALU.add),
                                     reads=[("ps", bn)], writes=[("sg", r)])
                                rsqrt_inplace(sg[r][:], ("sg", r))
                                col = 0 if oc < 8 else 1
                                P.op("dve", lambda r=r, bg=bg, qq=qq, col=col: dve.scalar_tensor_tensor(
                                    out=qst[qq][:], in0=pb[bg][:, :], scalar=gqk[:, col:col + 1], in1=sg[r][:],
                                    op0=ALU.mult, op1=ALU.mult),
                                     reads=[("ps", bg), ("sg", r), "gqk"], writes=[("qst", qq)])
                                row0 = (oc % 8) * 128
                                dh = (own["q"] if oc < 8 else own["k"]) if npair == 1 else None
                            else:
                                evac_copy(qst[qq][:], pb[bg][:, :], [("ps", bg)], [("qst", qq)])
                                row0 = (oc - 24) * 128
                                dh = own["u"] if npair == 1 else None
                            if npair == 1:
                                dst = dh.ap()[row0:row0 + 128, t0 + tb * 512:t0 + (tb + 1) * 512]
                            else:
                                kd_ = "q" if oc < 8 else ("k" if oc < 16 else "u")
                                hj, r0_ = divmod(row0, 512)
                                dst = bv(ownS[kd_][hj])[r0_:r0_ + 128, t0 + tb * 512:t0 + (tb + 1) * 512]
                            P.op("sp", lambda qq=qq, dst=dst: sp.dma_start(out=dst, in_=qst[qq][:]),
                                 reads=[("qst", qq)], writes=["projA"], dma="scr")

                def mix_out(t0, tg0):
                    for h in range(N_HEADS):
                        r, hl = divmod(h, HPC)
                        row0 = r * MIXR + hl * 128
                        P.op("sp", lambda h=h, row0=row0: sp.dma_start(
                            out=hnT[:, h, :], in_=mixM_h.ap()[row0:row0 + 128, tg0:tg0 + TT]),
                             reads=["mixM"], writes=[("hn", h, tb) for tb in range(NTB)], dma="ld")
                    for c in range(8):
                        r, cl = divmod(c, NUC)
                        row0 = r * MIXR + HPC * 128 + cl * 128
                        P.op("sp", lambda c=c, row0=row0: sp.dma_start(
                            out=yTb[:, c, :], in_=mixM_h.ap()[row0:row0 + 128, tg0:tg0 + TT]),
                             reads=["mixM"], writes=[("yT", c)], dma="ld")
                    gw_v = W["glu_w"].rearrange("(k p) f -> p k f", p=128)
                    for c2 in range(8):
                        slot = cnt["wg"] % 2
                        cnt["wg"] += 1
                        wload(wg[slot][:, 0:8, :], gw_v[:, :, c2 * 128:(c2 + 1) * 128], ("wg", slot))
                        for tb in range(NTB):
                            tsl = slice(tb * 512, (tb + 1) * 512)
                            bg = PG[cnt["pg"] % 2]
                            cnt["pg"] += 1
                            for c in range(8):
                                P.op("pe", lambda c=c, bg=bg, slot=slot, tsl=tsl: pe.matmul(
                                    pb[bg][:, :], lhsT=wg[slot][:, c, :], rhs=yTb[:, c, tsl], start=(c == 0), stop=(c == 7)),
                                     reads=[("wg", slot), ("yT", c)], writes=[("ps", bg)])
                            q = cnt["sg"] % 2
                            cnt["sg"] += 1
                            P.op("act", lambda q=q, bg=bg, c2=c2: act.activation(out=sg[q][:], in_=pb[bg][:, :], func=AF.Sigmoid,
                                                                                  bias=glub[:, c2:c2 + 1]),
                                 reads=[("ps", bg), "glub"], writes=[("sg", q)])
                            P.op("dve", lambda q=q, c2=c2, tsl=tsl: dve.tensor_tensor(
                                out=hnT[:, 8 + c2, tsl], in0=sg[q][:], in1=yTb[:, c2, tsl], op=ALU.mult),
                                 reads=[("sg", q), ("yT", c2)], writes=[("hn", 8 + c2, tb)])
                    wo_v = W["w_out"].rearrange("(k p) f -> p k f", p=128)
                    for dc in range(KD):
                        slot = cnt["wg"] % 2
                        cnt["wg"] += 1
                        wload(wg[slot][:, :, :], wo_v[:, :, dc * 128:(dc + 1) * 128], ("wg", slot))
                        for tb in range(NTB):
                            tsl = slice(tb * 512, (tb + 1) * 512)
                            b = PD[cnt["pd"] % 2]
                            cnt["pd"] += 1
                            for k in range(KD):
                                P.op("pe", lambda k=k, b=b, slot=slot, tsl=tsl: pe.matmul(
                                    pb[b][:, :], lhsT=wg[slot][:, k, :], rhs=hnT[:, k, tsl], start=(k == 0), stop=(k == KD - 1)),
                                     reads=[("wg", slot), ("hn", k, tb)], writes=[("ps", b)])
                            P.op("dve", lambda b=b, dc=dc, tsl=tsl: dve.scalar_tensor_tensor(
                                out=xT[:, dc, tsl], in0=pb[b][:, :], scalar=1.0, in1=xT[:, dc, tsl],
                                op0=ALU.mult, op1=ALU.add),
                                 reads=[("ps", b), ("xT", dc, tb)], writes=[("xT", dc, tb)])

                for tt in range(NT):
                    t0 = tt * TT
                    x1v = _dap(x1s_h, tt * 128 * KD * TT, [[KD * TT, 128], [TT, KD], [1, TT]])
                    if phase == "A":
                        load_xT(t0)
                        rmsnorm(g1, "g1")
                        ffn(W["ffn1_w_gate"], W["ffn1_w_up"], W["ffn1_w_down"])
                        P.op("sp", lambda x1v=x1v: sp.dma_start(out=x1v, in_=xT[:, :, :]), reads=all_xT, writes=["x1s"], dma="scr")
                        proj(t0)
                    else:
                        P.op("sp", lambda x1v=x1v: sp.dma_start(out=xT[:, :, :], in_=x1v), reads=["x1s"], writes=all_xT, dma="ld")
                        mix_out(t0, t0)
                        rmsnorm(g3, "g3")
                        ffn(W["ffn2_w_gate"], W["ffn2_w_up"], W["ffn2_w_down"])
                        store_xT(t0)
                P.barrier()

        def attention_phase():
            with ExitStack() as st:
                sbl = mk_sb(st, "B_")
                qTh = sbl("qTh", [128, SEQ], BF16)
                kTh = sbl("kTh", [128, SEQ], BF16)
                vb = [sbl(f"vb{i}", [128, 32, 128], BF16) for i in range(3)]
                acc = sbl("acc", [128, 2, SEQ])
                rden = sbl("rden", [128, SEQ])
                outst = sbl("outst", [128, SEQ], BF16)
                Bt = [[sbl(f"Bt{pi}_{hl}", [128, 256], BF16) for hl in range(HPC)] for pi in range(3)]
                Hf = [sbl(f"Hf{i}", [128, 256]) for i in range(2)]
                pT = [sbl(f"pT{i}", [128, 256], BF16) for i in range(2)]
                rb = sbl("rb", [32, 8])
                oh = sbl("oh", [32, 3 * 129])
                zt = sbl("zt", [8, 3, 384])
                cdma(rb[:, 0:HPC], W["rel_bias"], "rb")
                cdma(oh[:], oh_d, "oh")
                P.op("pe", lambda: pe.matmul(pb[0][0:HPC, 0:387], lhsT=rb[:, 0:HPC], rhs=oh[:, :], start=True, stop=True),
                     reads=["rb", "oh"], writes=[("ps", 0)])
                P.op("pool", lambda: pool.memset(zt[:], NEG), writes=["zt"])
                P.op("dve", lambda: dve.tensor_copy(out=zt[0:HPC, :, 127:256], in_=pb[0][0:HPC, 0:387].rearrange("p (a b) -> p a b", a=3)),
                     reads=[("ps", 0)], writes=["zt"])
                P.op("sp", lambda: sp.dma_start(out=_dap(zpad_h, 0, [[3 * 384, 8], [384, 3], [1, 384]]), in_=zt[:]),
                     reads=["zt"], writes=["zpad"], dma="scr")
                hbase = 0
                for hl in range(HPC):
                    for pi in range(3):
                        i = (hl * 3 + pi) % 2
                        src = _dap(zpad_h, ((hbase + hl) * 3 + pi) * 384, [[1, 128], [1, 256]])
                        P.op("sp", lambda i=i, src=src: sp.dma_start(out=Hf[i][:], in_=src), reads=["zpad"],
                             writes=[("Hf", i)], dma="ld")
                        P.op("act", lambda i=i, hl=hl, pi=pi: act.copy(out=Bt[pi][hl][:], in_=Hf[i][:]),
                             reads=[("Hf", i)], writes=[("Bt", pi, hl)])
                nblk = 0
                for hl in range(HPC):
                    hg = hbase + hl
                    P.op("sp", lambda hl=hl: sp.dma_start(out=qTh[:, :], in_=mine["q"].ap()[hl * 128:(hl + 1) * 128, :]),
                         reads=["projM"], writes=["qTh"], dma="ld")
                    P.op("sp", lambda hl=hl: sp.dma_start(out=kTh[:, :], in_=mine["k"].ap()[hl * 128:(hl + 1) * 128, :]),
                         reads=["projM"], writes=["kTh"], dma="ld")
                    for pi, d in enumerate(DILS):
                        nm = 32 // d
                        for res in range(d):
                            for m0 in range(0, nm, 8):
                                mm = min(8, nm - m0)
                                t_0 = m0 * 128 * d + res
                                bi = res * nm + m0
                                P.op("sp", lambda pi=pi, bi=bi, mm=mm, t_0=t_0, d=d, hl=hl: sp.dma_start(
                                    out=vb[pi][:, bi:bi + mm, :],
                                    in_=mine["v"].ap()[t_0:t_0 + (mm * 128 - 1) * d + 1:d, hl * 128:(hl + 1) * 128]
                                    .rearrange("(m j) e -> j m e", j=128)),
                                     reads=["projM"], writes=[("vb", pi)], dma="ld")
                    blocks = []
                    for pi, d in enumerate(DILS):
                        nm = 32 // d
                        for res in range(d):
                            for n in range(nm):
                                blocks.append((pi, d, nm, res, n, nblk))
                                nblk += 1

                    def stA(blk, hl=hl):
                        pi, d, nm, res, n, ib = blk
                        off = n * 128 * d + res
                        qa = qTh[:, off:off + 127 * d + 1:d]
                        ka = kTh[:, off:off + 127 * d + 1:d]
                        bS = PT[ib % 2]
                        ip = ib % 2
                        Wd_ = 256 if n > 0 else 128
                        P.op("pe", lambda: pe.matmul(pb[bS][:, 0:Wd_], lhsT=jrev[:], rhs=Bt[pi][hl][:, 0:Wd_], start=True, stop=False),
                             reads=["jrev", ("Bt", pi, hl)], writes=[("ps", bS)])
                        P.op("pe", lambda: pe.matmul(pb[bS][:, 0:128], lhsT=ka, rhs=qa, start=False, stop=(n == 0)),
                             reads=["qTh", "kTh"], writes=[("ps", bS)])
                        if n > 0:
                            offp = off - 128 * d
                            kp = kTh[:, offp:offp + 127 * d + 1:d]
                            P.op("pe", lambda: pe.matmul(pb[bS][:, 128:256], lhsT=kp, rhs=qa, start=False, stop=True),
                                 reads=["qTh", "kTh"], writes=[("ps", bS)])
                        P.op("act", lambda: act.activation(out=pT[ip][:, 0:Wd_], in_=pb[bS][:, 0:Wd_], func=AF.Exp),
                             reads=[("ps", bS)], writes=[("pT", ip)])

                    def stB(blk):
                        pi, d, nm, res, n, ib = blk
                        off = n * 128 * d + res
                        bO = PG[ib % 2]
                        ip = ib % 2
                        bi = res * nm + n
                        P.op("pe", lambda: pe.matmul(pb[bO][:, 0:128], lhsT=vb[pi][:, bi, :], rhs=pT[ip][:, 0:128], start=True, stop=(n == 0)),
                             reads=[("vb", pi), ("pT", ip)], writes=[("ps", bO)])
                        if n > 0:
                            P.op("pe", lambda: pe.matmul(pb[bO][:, 0:128], lhsT=vb[pi][:, bi - 1, :], rhs=pT[ip][:, 128:256], start=False, stop=True),
                                 reads=[("vb", pi), ("pT", ip)], writes=[("ps", bO)])
                        P.op("pe", lambda: pe.matmul(pb[bO][:, 128:256], lhsT=ones[:], rhs=pT[ip][:, 0:128], start=True, stop=(n == 0)),
                             reads=["ones", ("pT", ip)], writes=[("ps", bO)])
                        if n > 0:
                            P.op("pe", lambda: pe.matmul(pb[bO][:, 128:256], lhsT=ones[:], rhs=pT[ip][:, 128:256], start=False, stop=True),
                                 reads=["ones", ("pT", ip)], writes=[("ps", bO)])
                        av = acc[:, :, off:off + 127 * d + 1:d]
                        pv = pb[bO][:, 0:256].rearrange("p (a b) -> p a b", a=2)
                        if pi == 0:
                            P.op("dve", lambda: dve.tensor_copy(out=av, in_=pv), reads=[("ps", bO)], writes=[("acc", 0, n)])
                        else:
                            if pi == 1:
                                prev = [("acc", 0, 4 * n + k_) for k_ in range(4)]
                            else:
                                prev = [("acc", 1, r_, 4 * n + k_) for r_ in range(4) for k_ in range(4)]
                            P.op("dve", lambda: dve.tensor_tensor(out=av, in0=pv, in1=av, op=ALU.add),
                                 reads=[("ps", bO)] + prev, writes=[("acc", pi, res, n)])

                    for i_ in range(len(blocks) + 1):
                        if i_ < len(blocks):
                            stA(blocks[i_])
                        if i_ >= 1:
                            stB(blocks[i_ - 1])
                    accall = [("acc", 2, r_, n_) for r_ in range(16) for n_ in range(2)]
                    P.op("dve", lambda: dve.reciprocal(out=rden[:], in_=acc[:, 1, :]), reads=accall, writes=["rden"])
                    P.op("pool", lambda: pool.tensor_tensor(out=outst[:], in0=acc[:, 0, :], in1=rden[:], op=ALU.mult),
                         reads=accall + ["rden"], writes=["outst"] + [("acc", 0, n_) for n_ in range(32)]
                         + [("acc", 1, r_, n_) for r_ in range(4) for n_ in range(8)] + accall)
                    if npair == 1:
                        P.op("sp", lambda hl=hl: sp.dma_start(out=mix_h.ap()[hl * 128:(hl + 1) * 128, :], in_=outst[:]),
                             reads=["outst"], writes=["mix"], dma="scr")
                    else:
                        for j in range(2):
                            P.op("sp", lambda hl=hl, j=j: sp.dma_start(out=bv(mixA[j][0])[hl * 128:(hl + 1) * 128, :],
                                                                       in_=outst[:, j * ntok:(j + 1) * ntok]),
                                 reads=["outst"], writes=["mix"], dma="scr")
                P.barrier()


        def ssm_phase():
            S = GPC * 64
            TWO_PI = float(2 * np.pi)
            with ExitStack() as st:
                sbl = mk_sb(st, "S_")
                Tm_re = sbl("Tm_re", [128, S]); Tm_im = sbl("Tm_im", [128, S])
                Tp_re = sbl("Tp_re", [128, NP2 * 128]); Tp_im = sbl("Tp_im", [128, NP2 * 128])
                t128_re = sbl("t128_re", [128, NP2]); t128_im = sbl("t128_im", [128, NP2])
                Bblk_re = [sbl(f"Bblk_re{i}", [128, 512], BF16) for i in range(NUC)]
                Bblk_im = [sbl(f"Bblk_im{i}", [128, 512], BF16) for i in range(NUC)]
                Cre = sbl("Cre", [128, NP2, 4, 2, 16], BF16); nCre = sbl("nCre", [128, NP2, 4, 2, 16], BF16)
                nCim = sbl("nCim", [128, NP2, 4, 2, 16], BF16)
                for t_ in (Cre, nCre, nCim):
                    P.op("pool", lambda t_=t_: pool.memset(t_[:], 0.0), writes=["Cw"])
                d_col = sbl("d_col", [128, NUC])
                iota_c = sbl("iota_c", [128, 1]); iota_r = sbl("iota_r", [128, 128])
                tri = sbl("tri", [128, 128], BF16); ntri = sbl("ntri", [128, 128], BF16)
                mask2 = sbl("mask2", [128, 2]); nmask2 = sbl("nmask2", [128, 2]); mask3 = sbl("mask3", [128, 4])
                inj_re = [sbl(f"inj_re{i}", [128, NP2]) for i in range(2)]
                inj_im = [sbl(f"inj_im{i}", [128, NP2]) for i in range(2)]
                for i_ in range(2):
                    P.op("pool", lambda i_=i_: pool.memset(inj_re[i_][:], 0.0), writes=[("inj", i_, k_) for k_ in range(NUC)])
                    P.op("pool", lambda i_=i_: pool.memset(inj_im[i_][:], 0.0), writes=[("inj", i_, k_) for k_ in range(NUC)])
                for t_, d_, r_ in ((iota_c, iotac_d, "iota_c"), (iota_r, iotar_d, "iota_r"), (tri, tri_d, "tri"),
                                   (ntri, ntri_d, "ntri"), (mask2, mask2_d, "mask2"), (nmask2, nmask2_d, "nmask2"),
                                   (mask3, mask3_d, "mask3")):
                    cdma(t_[:], d_, r_)
                cdma(d_col[:], SS["ssm_d"].rearrange("(c p) -> p c", p=128), "d_col")

                with ExitStack() as st2:
                    sb2 = mk_sb(st2, "S2_")
                    BLK = 1024
                    tmp = {n: sb2("tg_" + n, [128, BLK]) for n in ("t", "fr", "m", "cosv", "sinv", "mag")}
                    tint = sb2("tg_int", [128, BLK], mybir.dt.int32)

                    def dv(fn, reads, writes):
                        P.op("dve", fn, reads=reads, writes=writes)

                    def trig(out_re, out_im, ang, marg, n, np_, sign, rres, wres):
                        for c0 in range(0, n, BLK):
                            w = min(BLK, n - c0)
                            cs = slice(c0, c0 + w)
                            T = {k: v[0:np_, 0:w] for k, v in tmp.items()}
                            ti = tint[0:np_, 0:w]
                            dv(lambda T=T, cs=cs: dve.tensor_scalar(out=T["t"], in0=ang[:, cs], scalar1=1.0 / TWO_PI, scalar2=None, op0=ALU.mult),
                               rres, ["tg_t"])
                            for name, shift in (("cosv", 0.25), ("sinv", 0.0)):
                                dv(lambda T=T, shift=shift: dve.tensor_scalar(out=T["fr"], in0=T["t"], scalar1=shift, scalar2=None, op0=ALU.add),
                                   ["tg_t"], ["tg_fr"])
                                dv(lambda T=T, ti=ti: dve.tensor_copy(out=ti, in_=T["fr"]), ["tg_fr"], ["tg_i"])
                                dv(lambda T=T, ti=ti: dve.tensor_copy(out=T["m"], in_=ti), ["tg_i"], ["tg_m"])
                                dv(lambda T=T: dve.tensor_tensor(out=T["fr"], in0=T["fr"], in1=T["m"], op=ALU.subtract), ["tg_fr", "tg_m"], ["tg_fr"])
                                dv(lambda T=T: dve.tensor_scalar(out=T["m"], in0=T["fr"], scalar1=0.5, scalar2=None, op0=ALU.is_gt), ["tg_fr"], ["tg_m"])
                                dv(lambda T=T: dve.tensor_tensor(out=T["fr"], in0=T["fr"], in1=T["m"], op=ALU.subtract), ["tg_fr", "tg_m"], ["tg_fr"])
                                dv(lambda T=T: dve.tensor_scalar(out=T["m"], in0=T["fr"], scalar1=-0.5, scalar2=None, op0=ALU.is_lt), ["tg_fr"], ["tg_m"])
                                dv(lambda T=T: dve.tensor_tensor(out=T["fr"], in0=T["fr"], in1=T["m"], op=ALU.add), ["tg_fr", "tg_m"], ["tg_fr"])
                                P.op("act", lambda T=T, name=name: act.activation(out=T[name], in_=T["fr"], func=AF.Sin, scale=TWO_PI),
                                     reads=["tg_fr"], writes=["tg_" + name])
                            P.op("act", lambda T=T, cs=cs: act.activation(out=T["mag"], in_=marg[:, cs], func=AF.Exp, scale=float(sign)),
                                 reads=rres, writes=["tg_mag"])
                            dv(lambda T=T, cs=cs: dve.tensor_tensor(out=out_re[:, cs], in0=T["mag"], in1=T["cosv"], op=ALU.mult),
                               ["tg_mag", "tg_cosv"], wres)
                            dv(lambda T=T, cs=cs: dve.scalar_tensor_tensor(out=out_im[:, cs], in0=T["mag"], scalar=float(sign), in1=T["sinv"],
                                                                           op0=ALU.mult, op1=ALU.mult),
                               ["tg_mag", "tg_sinv"], wres)

                    col = lambda n: sb2(n, [128, NP2])
                    lre, lim, ldt, alpha, theta = col("lre"), col("lim"), col("ldt"), col("alpha"), col("theta")
                    a_re, a_im, cf_re, cf_im, w1, w2 = col("a_re"), col("a_im"), col("cf_re"), col("cf_im"), col("w1"), col("w2")
                    al128, th128 = col("al128"), col("th128")
                    cdma(lre[:], SS["ssm_lambda_re"].rearrange("(q p) -> p q", p=128), "lre")
                    cdma(lim[:], SS["ssm_lambda_im"].rearrange("(q p) -> p q", p=128), "lim")
                    ldt_h = SS_h["ssm_log_dt"]
                    cdma(ldt[0:64, :], _dap(ldt_h, 0, [[0, 64], [2, NP2]]), "ldt")
                    cdma(ldt[64:128, :], _dap(ldt_h, 1, [[0, 64], [2, NP2]]), "ldt")
                    P.op("act", lambda: act.activation(out=ldt[:], in_=ldt[:], func=AF.Exp), reads=["ldt"], writes=["ldt"])
                    dv(lambda: dve.tensor_tensor(out=alpha[:], in0=lre[:], in1=ldt[:], op=ALU.mult), ["lre", "ldt"], ["alpha"])
                    dv(lambda: dve.tensor_tensor(out=theta[:], in0=lim[:], in1=ldt[:], op=ALU.mult), ["lim", "ldt"], ["theta"])
                    trig(a_re, a_im, theta, alpha, NP2, 128, 1.0, ["theta", "alpha"], ["a"])
                    dv(lambda: dve.tensor_scalar(out=a_re[:], in0=a_re[:], scalar1=-1.0, scalar2=None, op0=ALU.add), ["a"], ["a"])
                    dv(lambda: dve.tensor_tensor(out=w1[:], in0=lre[:], in1=lre[:], op=ALU.mult), ["lre"], ["w1"])
                    dv(lambda: dve.tensor_tensor(out=w2[:], in0=lim[:], in1=lim[:], op=ALU.mult), ["lim"], ["w2"])
                    dv(lambda: dve.tensor_tensor(out=w1[:], in0=w1[:], in1=w2[:], op=ALU.add), ["w1", "w2"], ["w1"])
                    dv(lambda: dve.reciprocal(out=w1[:], in_=w1[:]), ["w1"], ["w1"])
                    dv(lambda: dve.tensor_tensor(out=cf_re[:], in0=a_re[:], in1=lre[:], op=ALU.mult), ["a", "lre"], ["cf_re"])
                    dv(lambda: dve.tensor_tensor(out=w2[:], in0=a_im[:], in1=lim[:], op=ALU.mult), ["a", "lim"], ["w2"])
                    dv(lambda: dve.tensor_tensor(out=cf_re[:], in0=cf_re[:], in1=w2[:], op=ALU.add), ["cf_re", "w2"], ["cf_re"])
                    dv(lambda: dve.tensor_tensor(out=cf_re[:], in0=cf_re[:], in1=w1[:], op=ALU.mult), ["cf_re", "w1"], ["cf_re"])
                    dv(lambda: dve.tensor_tensor(out=cf_im[:], in0=a_im[:], in1=lre[:], op=ALU.mult), ["a", "lre"], ["cf_im"])
                    dv(lambda: dve.tensor_tensor(out=w2[:], in0=a_re[:], in1=lim[:], op=ALU.mult), ["a", "lim"], ["w2"])
                    dv(lambda: dve.tensor_tensor(out=cf_im[:], in0=cf_im[:], in1=w2[:], op=ALU.subtract), ["cf_im", "w2"], ["cf_im"])
                    dv(lambda: dve.tensor_tensor(out=cf_im[:], in0=cf_im[:], in1=w1[:], op=ALU.mult), ["cf_im", "w1"], ["cf_im"])
                    dv(lambda: dve.tensor_scalar(out=al128[:], in0=alpha[:], scalar1=128.0, scalar2=None, op0=ALU.mult), ["alpha"], ["al128"])
                    dv(lambda: dve.tensor_scalar(out=th128[:], in0=theta[:], scalar1=128.0, scalar2=None, op0=ALU.mult), ["theta"], ["th128"])
                    trig(t128_re, t128_im, th128, al128, NP2, 128, 1.0, ["th128", "al128"], ["t128"])
                    angp = sb2("angp", [128, S]); margp = sb2("margp", [128, S])
                    for q in range(NP2):
                        dv(lambda q=q: dve.tensor_scalar(out=angp[:, q * 128:(q + 1) * 128], in0=iota_r[:], scalar1=theta[:, q:q + 1],
                                                         scalar2=None, op0=ALU.mult), ["iota_r", "theta"], ["angm"])
                        P.op("pool", lambda q=q: pool.tensor_scalar(out=margp[:, q * 128:(q + 1) * 128], in0=iota_r[:], scalar1=alpha[:, q:q + 1],
                                                                    scalar2=None, op0=ALU.mult), reads=["iota_r", "alpha"], writes=["margm"])
                    trig(Tp_re, Tp_im, angp, margp, NP2 * 128, 128, 1.0, ["angm", "margm"], ["Tp"])
                    P.op("sp", lambda: sp.dma_start(out=_dap(prm_h, 0, [[1, 128], [128, NP2]]), in_=theta[:]), reads=["theta"], writes=["prm"], dma="scr")
                    P.op("sp", lambda: sp.dma_start(out=_dap(prm_h, S, [[1, 128], [128, NP2]]), in_=alpha[:]), reads=["alpha"], writes=["prm"], dma="scr")
                    angm, margm = angp, margp
                    P.op("sp", lambda: sp.dma_start(out=angm[:], in_=_dap(prm_h, 0, [[0, 128], [1, S]])), reads=["prm"], writes=["angm"], dma="ld")
                    P.op("sp", lambda: sp.dma_start(out=margm[:], in_=_dap(prm_h, S, [[0, 128], [1, S]])), reads=["prm"], writes=["margm"], dma="ld")
                    dv(lambda: dve.tensor_scalar(out=angm[:], in0=angm[:], scalar1=iota_c[:, 0:1], scalar2=None, op0=ALU.mult), ["angm", "iota_c"], ["angm"])
                    P.op("pool", lambda: pool.tensor_scalar(out=margm[:], in0=margm[:], scalar1=iota_c[:, 0:1], scalar2=None, op0=ALU.mult),
                         reads=["margm", "iota_c"], writes=["margm"])
                    trig(Tm_re, Tm_im, angm, margm, S, 128, -1.0, ["angm", "margm"], ["Tm"])
                    Bn_re = sb2("Bn_re", [128, NP2, 16]); Bn_im = sb2("Bn_im", [128, NP2, 16])
                    tA = sb2("tA", [128, NP2, 16]); tB = sb2("tB", [128, NP2, 16])
                    Bb_re = sb2("Bb_re", [128, NP2, 16]); Bb_im = sb2("Bb_im", [128, NP2, 16])
                    cdma(Bn_re[:], SS["ssm_b_re"].rearrange("(q p c) -> p q c", p=128, c=16), "Bn_re")
                    cdma(Bn_im[:], SS["ssm_b_im"].rearrange("(q p c) -> p q c", p=128, c=16), "Bn_im")
                    for (dst, x1_, c1_, x2_, c2_, op_) in ((Bb_re, Bn_re, cf_re, Bn_im, cf_im, ALU.subtract),
                                                           (Bb_im, Bn_im, cf_re, Bn_re, cf_im, ALU.add)):
                        for c in range(16):
                            dv(lambda c=c, x1_=x1_, c1_=c1_: dve.tensor_tensor(out=tA[:, :, c], in0=x1_[:, :, c], in1=c1_[:], op=ALU.mult),
                               ["Bn_re", "Bn_im", "cf_re", "cf_im"], ["tA"])
                            dv(lambda c=c, x2_=x2_, c2_=c2_: dve.tensor_tensor(out=tB[:, :, c], in0=x2_[:, :, c], in1=c2_[:], op=ALU.mult),
                               ["Bn_re", "Bn_im", "cf_re", "cf_im"], ["tB"])
                        dv(lambda dst=dst, op_=op_: dve.tensor_tensor(out=dst[:], in0=tA[:], in1=tB[:], op=op_), ["tA", "tB"], ["Bb"])
                    src_t = sb2("src_t", [128, 4, 2, 16])
                    for Bb, Bblk in ((Bb_re, Bblk_re), (Bb_im, Bblk_im)):
                        for ch in range(NUC):
                            for g2 in range(2):
                                dv(lambda Bb=Bb, ch=ch, g2=g2: dve.tensor_scalar(out=src_t[:, :, g2, :], in0=Bb[:, 4 * ch:4 * ch + 4, :],
                                                                                 scalar1=mask2[:, g2:g2 + 1], scalar2=None, op0=ALU.mult),
                                   ["Bb", "mask2"], ["src_t"])
                            P.op("pe", lambda: pe.transpose(out=pb[7][:, 0:128], in_=src_t[:].rearrange("p a b c -> p (a b c)"), identity=ident[:]),
                                 reads=["src_t", "ident"], writes=[("ps", 7)])
                            for q4 in range(4):
                                dv(lambda Bblk=Bblk, ch=ch, q4=q4: dve.tensor_scalar(out=Bblk[ch][:, q4 * 128:(q4 + 1) * 128], in0=pb[7][:, 0:128],
                                                                                     scalar1=mask3[:, q4:q4 + 1], scalar2=None, op0=ALU.mult),
                                   [("ps", 7), "mask3"], [("Bblk", ch)])
                    Cd = sb2("Cd", [128, 2, 64])
                    for name, outs in (("ssm_c_re", ((Cre, mask2), (nCre, nmask2))), ("ssm_c_im", ((nCim, nmask2),))):
                        cv = SS[name].rearrange("(c p n) -> c p n", p=128, n=64)
                        for ch in range(NUC):
                            cdma(Cd[:, 0, :], cv[ch], "Cd")
                            cdma(Cd[:, 1, :], cv[ch], "Cd")
                            P.op("pe", lambda: pe.transpose(out=pb[7][:, 0:128], in_=Cd[:].rearrange("p a b -> p (a b)"), identity=ident[:]),
                                 reads=["Cd", "ident"], writes=[("ps", 7)])
                            trv = pb[7][:, 0:128].rearrange("p (a b c) -> p a b c", a=4, b=2)
                            for dst, mk in outs:
                                for g2 in range(2):
                                    for q4 in range(4):
                                        dv(lambda dst=dst, mk=mk, ch=ch, g2=g2, trv=trv, q4=q4: dve.tensor_scalar(
                                            out=dst[:, 4 * ch + q4, q4, g2, :], in0=trv[:, q4, g2, :], scalar1=mk[:, g2:g2 + 1],
                                            scalar2=None, op0=ALU.mult), [("ps", 7), "mask2", "nmask2"], ["Cw"])
                    P.barrier()

                uTc = [sbl(f"uTc{i}", [128, NUC, 512], BF16) for i in range(2)]
                yst = sbl("yst", [128, NUC, 512])
                dm = [[sbl(f"dm{i}_{j}", [128, 512], BF16) for j in range(4)] for i in range(2)]
                rm = [[sbl(f"rm{i}_{j}", [128, 512], BF16) for j in range(4)] for i in range(2)]
                tn = [sbl(f"tn{i}", [128, 4]) for i in range(4)]
                gt = [sbl(f"gt{i}", [128, 512]) for i in range(3)]
                gout = [sbl(f"gout{i}", [128, 512], BF16) for i in range(2)]
                yst2 = sbl("yst2", [128, NUC, 512])
                ysts = [yst, yst2]
                XR, XI, YB = 4, 5, 6
                NCH = SEQ // 128

                def s0_BU(g):
                    ub, ch, ts, bre, bim = g["ub"], g["ch"], g["ts"], g["bre"], g["bim"]
                    P.op("pe", lambda: pe.matmul(pb[bre][:, :], lhsT=uTc[ub][:, ch, ts], rhs=Bblk_re[ch][:], start=True, stop=True),
                         reads=[("uTc", ub), ("Bblk", ch)], writes=[("ps", bre)])
                    P.op("pe", lambda: pe.matmul(pb[bim][:, :], lhsT=uTc[ub][:, ch, ts], rhs=Bblk_im[ch][:], start=True, stop=True),
                         reads=[("uTc", ub), ("Bblk", ch)], writes=[("ps", bim)])

                def s1_demod(g):
                    i2, ch, bre, bim = g["i2"], g["ch"], g["bre"], g["bim"]
                    tsl = slice(ch * 512, (ch + 1) * 512)
                    A_, B_, C_, D_ = dm[i2]
                    for dst, tab, bsrc, k_ in ((A_, Tm_re, bre, 0), (B_, Tm_im, bim, 1), (C_, Tm_re, bim, 2), (D_, Tm_im, bre, 3)):
                        P.op("dve", lambda dst=dst, tab=tab, bsrc=bsrc: dve.tensor_tensor(out=dst[:], in0=pb[bsrc][:, :], in1=tab[:, tsl], op=ALU.mult),
                             reads=[("ps", bsrc), "Tm"], writes=[("dm", i2, k_)])

                def s2_cumsum(g):
                    i2, ch, c = g["i2"], g["ch"], g["c"]
                    A_, B_, C_, D_ = dm[i2]
                    for q4 in range(4):
                        cs = slice(q4 * 128, (q4 + 1) * 128)
                        for (xb, m1, k1, m2, k2, rhs2) in ((XR, A_, 0, B_, 1, ntri), (XI, C_, 2, D_, 3, tri)):
                            P.op("pe", lambda xb=xb, m1=m1, cs=cs: pe.matmul(pb[xb][:, cs], lhsT=m1[:, cs], rhs=tri[:], start=True, stop=False),
                                 reads=[("dm", i2, k1), "tri"], writes=[("ps", xb)])
                            P.op("pe", lambda xb=xb, m2=m2, cs=cs, rhs2=rhs2: pe.matmul(pb[xb][:, cs], lhsT=m2[:, cs], rhs=rhs2[:], start=False, stop=True),
                                 reads=[("dm", i2, k2), "tri", "ntri"], writes=[("ps", xb)])

                tn5 = [sbl(f"tn5_{i}", [128, 4]) for i in range(2)]

                def s3_remod(g):
                    i2, ch, c = g["i2"], g["ch"], g["c"]
                    last = c >= NCH - 1
                    cur, nxt = c % 2, (c + 1) % 2
                    qs = slice(4 * ch, 4 * ch + 4)
                    if not last:
                        xr = pb[XR][:, 127:512:128]
                        xi = pb[XI][:, 127:512:128]
                        P.op("dve", lambda: dve.tensor_tensor(out=tn5[0][:], in0=xr, in1=inj_re[cur][:, qs], op=ALU.add), reads=[("ps", XR), ("inj", cur, ch)], writes=["tn5a"])
                        P.op("dve", lambda: dve.tensor_tensor(out=tn5[1][:], in0=xi, in1=inj_im[cur][:, qs], op=ALU.add), reads=[("ps", XI), ("inj", cur, ch)], writes=["tn5b"])
                    E1, E2, E3, E4 = rm[i2]
                    for q4 in range(4):
                        q = 4 * ch + q4
                        cs = slice(q4 * 128, (q4 + 1) * 128)
                        tq = slice(q * 128, (q + 1) * 128)
                        for dst, tab, xsrc, inj_, k_ in ((E1, Tp_re, XR, inj_re, 0), (E2, Tp_im, XI, inj_im, 1), (E3, Tp_re, XI, inj_im, 2), (E4, Tp_im, XR, inj_re, 3)):
                            P.op("dve", lambda dst=dst, tab=tab, xsrc=xsrc, inj_=inj_, cs=cs, tq=tq, q=q: dve.scalar_tensor_tensor(
                                out=dst[:, cs], in0=pb[xsrc][:, cs], scalar=inj_[cur][:, q:q + 1], in1=tab[:, tq], op0=ALU.add, op1=ALU.mult),
                                 reads=[("ps", xsrc), "Tp", ("inj", cur, ch)], writes=[("rm", i2, k_)])
                    if not last:
                        P.op("dve", lambda: dve.tensor_tensor(out=tn[0][:], in0=tn5[0][:], in1=t128_re[:, qs], op=ALU.mult), reads=["tn5a", "t128"], writes=["tn0"])
                        P.op("dve", lambda: dve.tensor_tensor(out=tn[1][:], in0=tn5[1][:], in1=t128_im[:, qs], op=ALU.mult), reads=["tn5b", "t128"], writes=["tn1"])
                        P.op("dve", lambda: dve.tensor_tensor(out=tn[2][:], in0=tn5[1][:], in1=t128_re[:, qs], op=ALU.mult), reads=["tn5b", "t128"], writes=["tn2"])
                        P.op("dve", lambda: dve.tensor_tensor(out=tn[3][:], in0=tn5[0][:], in1=t128_im[:, qs], op=ALU.mult), reads=["tn5a", "t128"], writes=["tn3"])
                        P.op("dve", lambda: dve.tensor_tensor(out=inj_re[nxt][:, qs], in0=tn[0][:], in1=tn[1][:], op=ALU.subtract), reads=["tn0", "tn1"], writes=[("inj", nxt, ch)])
                        P.op("dve", lambda: dve.tensor_tensor(out=inj_im[nxt][:, qs], in0=tn[2][:], in1=tn[3][:], op=ALU.add), reads=["tn2", "tn3"], writes=[("inj", nxt, ch)])

                def s4_y(g):
                    i2, ch, yk = g["i2"], g["ch"], g["yk"]
                    E1, E2, E3, E4 = rm[i2]
                    n_ = 0
                    for q4 in range(4):
                        q = 4 * ch + q4
                        cs = slice(q4 * 128, (q4 + 1) * 128)
                        for (wt, et, k_) in ((Cre, E1, 0), (nCre, E2, 1), (nCim, E3, 2), (nCim, E4, 3)):
                            P.op("pe", lambda wt=wt, et=et, q=q, cs=cs, n_=n_: pe.matmul(
                                pb[YB + yk][:, 0:128], lhsT=wt[:, q, :, :, :].rearrange("p a b c -> p (a b c)"), rhs=et[:, cs],
                                start=(n_ == 0), stop=(n_ == 15)),
                                 reads=["Cw", ("rm", i2, k_)], writes=[("ps", YB + yk)])
                            n_ += 1

                def s5_evac(g):
                    ub, ch, ts, yk, yb = g["ub"], g["ch"], g["ts"], g["yk"], g["yb"]
                    P.op("dve", lambda: dve.scalar_tensor_tensor(
                        out=ysts[yb][:, ch, ts], in0=uTc[ub][:, ch, ts], scalar=d_col[:, ch:ch + 1], in1=pb[YB + yk][:, 0:128],
                        op0=ALU.mult, op1=ALU.add),
                         reads=[("uTc", ub), "d_col", ("ps", YB + yk)], writes=[("yst", yb, ch)])
                    if g["sc_last"]:
                        gelu_out(g["sc"], yb)

                ngo = [0]

                def gelu_out(sc, yb):
                    for ch in range(NUC):
                        yv = ysts[yb][:, ch, :]
                        go = ngo[0] % 2
                        ngo[0] += 1
                        P.op("act", lambda yv=yv: act.activation(out=gt[0][:], in_=yv, func=AF.Square), reads=[("yst", yb, ch)], writes=["gt0"])
                        P.op("pool", lambda: pool.tensor_scalar(out=gt[1][:], in0=gt[0][:], scalar1=0.044715, scalar2=1.0, op0=ALU.mult, op1=ALU.add),
                             reads=["gt0"], writes=["gt1"])
                        P.op("pool", lambda yv=yv: pool.tensor_tensor(out=gt[1][:], in0=gt[1][:], in1=yv, op=ALU.mult), reads=["gt1", ("yst", yb, ch)], writes=["gt1"])
                        P.op("act", lambda: act.activation(out=gt[2][:], in_=gt[1][:], func=AF.Sigmoid, scale=1.5957691216057308), reads=["gt1"], writes=["gt2"])
                        P.op("pool", lambda yv=yv, go=go: pool.tensor_tensor(out=gout[go][:], in0=gt[2][:], in1=yv, op=ALU.mult),
                             reads=["gt2", ("yst", yb, ch)], writes=[("gout", go)])
                        if npair == 1:
                            dst = mix_h.ap()[HPC * 128 + ch * 128:HPC * 128 + (ch + 1) * 128, sc * 512:(sc + 1) * 512]
                        else:
                            j_, tl_ = divmod(sc * 512, ntok)
                            dst = bv(mixA[j_][1])[ch * 128:(ch + 1) * 128, tl_:tl_ + 512]
                        P.op("sp", lambda go=go, dst=dst: sp.dma_start(out=dst, in_=gout[go][:]), reads=[("gout", go)], writes=["mix"], dma="scr")

                groups = []
                for sc in range(SEQ // 512):
                    for c4 in range(4):
                        for ch in range(NUC):
                            n_ = len(groups)
                            groups.append(dict(sc=sc, ub=sc % 2, yb=sc % 2, c=sc * 4 + c4, ch=ch, ts=slice(c4 * 128, (c4 + 1) * 128),
                                               i2=n_ % 2, bre=(0, 2)[n_ % 2], bim=(1, 3)[n_ % 2], yk=n_ % 2,
                                               sc_first=(c4 == 0 and ch == 0), sc_last=(c4 == 3 and ch == NUC - 1)))
                stages_fn = (s0_BU, s1_demod, s2_cumsum, s3_remod, s4_y, s5_evac)
                for t in range(len(groups) + 5):
                    for k in range(5, -1, -1):
                        gi = t - k
                        if 0 <= gi < len(groups):
                            g = groups[gi]
                            if k == 0 and g["sc_first"]:
                                P.op("sp", lambda ub=g["ub"], sc=g["sc"]: sp.dma_start(
                                    out=uTc[ub][:, :, :],
                                    in_=mine["u"].ap()[:, sc * 512:(sc + 1) * 512].rearrange("(c p) t -> p c t", p=128)),
                                     reads=["projM"], writes=[("uTc", g["ub"])], dma="ld")
                            stages_fn[k](g)
                P.barrier()

        if "A" in stages:
            row_local_phase("A")
        if npair > 1:
            for kd in ("q", "k", "u", "v"):
                for j in range(2):
                    allgather(kd, ownS[kd][j], gatS[kd][j], "projA", "projG")
            for kd in ("q", "k", "u", "v"):
                nr = SEQ if kd == "v" else 1024
                for j in range(2):
                    P.op("sp", lambda kd=kd, j=j, nr=nr: sp.dma_start(out=stg[kd].ap()[j * nr:(j + 1) * nr, :], in_=bv(gatS[kd][j])),
                         reads=["projG"], writes=["projS"], dma="ld")
            for kd in ("q", "k", "u"):
                for rb in range(2):
                    P.op("sp", lambda kd=kd, rb=rb: sp.dma_start(
                        out=mine[kd].ap()[:, rb * ntok:(rb + 1) * ntok], in_=rows_dyn(stg[kd].ap(), rb * 512, 512, 1024)),
                         reads=["projS"], writes=["projM"], dma="ld")
            for hf in range(2):
                P.op("sp", lambda hf=hf: sp.dma_start(
                    out=mine["v"].ap()[hf * 2048:(hf + 1) * 2048, :], in_=rows_dyn(stg["v"].ap(), hf * 2048, 2048, SEQ)),
                     reads=["projS"], writes=["projM"], dma="ld")
            P.barrier()
        if "B" in stages:
            attention_phase()
            if ssm_on:
                ssm_phase()
        if npair > 1:
            for j in range(2):
                for h in range(2):
                    allgather("mix", mixA[j][h], mixGt[j][h], "mix", "mixG")
            for j in range(2):
                for h in range(2):
                    P.op("sp", lambda j=j, h=h: sp.dma_start(
                        out=stg_mix.ap()[j * 2048 + h * 1024:j * 2048 + (h + 1) * 1024, :], in_=bv(mixGt[j][h])),
                         reads=["mixG"], writes=["mixS"], dma="ld")
            for h in range(2):
                for rb in range(2):
                    P.op("sp", lambda h=h, rb=rb: sp.dma_start(
                        out=mixM_h.ap()[rb * 1024 + h * 512:rb * 1024 + (h + 1) * 512, :],
                        in_=rows_dyn(stg_mix.ap(), h * 1024 + rb * 512, 512, 2048)),
                         reads=["mixS"], writes=["mixM"], dma="ld")
            P.barrier()
        if "C" in stages:
            row_local_phase("C")
        P.barrier()
    return nc


_CONSTS = None


def _t5_bucket(dist):
    dist = np.asarray(dist)
    max_exact = 16
    d_f = np.maximum(dist, max_exact).astype(np.float32)
    val = (np.log(d_f / np.float32(max_exact)) / np.float32(np.log(2048 / max_exact)) * np.float32(32 - max_exact))
    large = max_exact + (np.rint(val) if BUCKET_ROUND else val).astype(np.int32)
    large = np.minimum(large, 31)
    return np.where(dist < max_exact, dist, large)


def _consts():
    global _CONSTS
    if _CONSTS is None:
        oh = np.zeros((32, 3 * 129), np.float32)
        for pi, d in enumerate(DILS):
            b = _t5_bucket(np.arange(129) * d)
            oh[b, pi * 129 + np.arange(129)] = 1.0
        _CONSTS = {
            "ident": np.eye(128, dtype=np.float32),
            "ones_bf": np.ones((128, 128), dtype=ml_dtypes.bfloat16),
            "jrev_bf": np.eye(128, dtype=np.float32)[::-1].copy().astype(ml_dtypes.bfloat16),
            "onehot": oh,
            "iota_c": np.arange(128, dtype=np.float32).reshape(128, 1),
            "iota_r": np.tile(np.arange(128, dtype=np.float32), (128, 1)),
            "tri_bf": np.triu(np.ones((128, 128), np.float32)).astype(ml_dtypes.bfloat16),
            "ntri_bf": (-np.triu(np.ones((128, 128), np.float32))).astype(ml_dtypes.bfloat16),
            "mask2": (np.arange(128)[:, None] // 64 == np.arange(2)[None, :]).astype(np.float32),
            "nmask2": -(np.arange(128)[:, None] // 64 == np.arange(2)[None, :]).astype(np.float32),
            "mask3": (np.arange(128)[:, None] // 32 == np.arange(4)[None, :]).astype(np.float32),
            "sel": np.broadcast_to(np.eye(32, dtype=np.float32)[:, :, None], (32, 32, 128)).copy(),
        }
    return _CONSTS


PARAMS = ["ffn1_norm", "ffn1_w_gate", "ffn1_w_up", "ffn1_w_down", "ffn2_norm", "ffn2_w_gate", "ffn2_w_up",
          "ffn2_w_down", "mix_norm", "w_in", "q_norm", "k_norm", "glu_w", "glu_b", "w_out"]


def make_in_maps(inputs, ncores, npair=1):
    x = np.ascontiguousarray(inputs["x"], dtype=np.float32)
    base = dict(_consts())
    for n in PARAMS:
        base[n] = np.ascontiguousarray(inputs[n][0], dtype=np.float32)
    ntok = SEQ // npair
    gpc = N_GROUPS // npair
    in_maps = []
    for c in range(ncores):
        b, r = divmod(c, npair)
        m = dict(base)
        m["x"] = x[b % BATCH, r * ntok:(r + 1) * ntok]
        gs = slice(r * gpc, (r + 1) * gpc)
        for n in ("ssm_lambda_re", "ssm_lambda_im", "ssm_log_dt", "ssm_b_re", "ssm_b_im", "ssm_c_re", "ssm_c_im"):
            m[n] = np.ascontiguousarray(inputs[n][0][gs], dtype=np.float32).reshape(-1)
        m["ssm_d"] = np.ascontiguousarray(inputs["ssm_d"][0][r * gpc * 16:(r + 1) * gpc * 16], dtype=np.float32)
        hpc = N_HEADS // npair
        m["rel_bias"] = np.ascontiguousarray(np.asarray(inputs["rel_bias"], dtype=np.float32)[:, r * hpc:(r + 1) * hpc])
        in_maps.append(m)
    return in_maps


NPAIR = 2


def kernel(**inputs):
    npair = NPAIR
    ncores = BATCH * npair
    nc = build(dict(npair=npair))
    in_maps = make_in_maps(inputs, ncores, npair)
    res = run_bass_kernel_spmd(nc, in_maps, core_ids=list(range(ncores)))
    ntok = SEQ // npair
    out = np.empty((BATCH, SEQ, D), np.float32)
    for c in range(ncores):
        b, r = divmod(c, npair)
        out[b, r * ntok:(r + 1) * ntok] = np.asarray(res.results[c]["out"])
    return out
```
